# Optimizing a Trainium2 kernel written in Bass

```python
import jax, jax.numpy as jnp
from jax import lax
import numpy as np

D_MODEL = 1024
BATCH = 4
SEQ = 8192
DEPTH = 2

CHUNK = 64
N_BRANCH = 4
BRANCH = D_MODEL // N_BRANCH
D_MIX = N_BRANCH * BRANCH
EPS = 1e-6

CONV_A_WIDTH = 3
CONV_A_GROUPS = 4
GLA_HEADS = 4
GLA_DK = BRANCH // 2
GLA_DV = BRANCH
GLA_GATE_RANK = 16
GLA_TAU = 16.0
POOL_WINDOWS = (2, 4, 8, 16)
POOL_GROUP = BRANCH // len(POOL_WINDOWS)
SSD_HEAD_DIM = 64
SSD_HEADS = BRANCH // SSD_HEAD_DIM
SSD_GROUPS = 2
SSD_STATE = 128
SSD_CONV = 4
SSD_XBC = BRANCH + 2 * SSD_GROUPS * SSD_STATE

PROJ_SPLITS = (BRANCH, BRANCH, BRANCH, BRANCH,
               GLA_DK, GLA_DK, GLA_DV, GLA_GATE_RANK, GLA_DV,
               BRANCH, BRANCH,
               BRANCH, SSD_XBC, SSD_HEADS)
D_PROJ = sum(PROJ_SPLITS)

kernel_name = 'hybrid_parallel_conv_gla_pool_ssd'


def _rmsnorm(x, w):
    xf = x.astype(jnp.float32)
    y = xf * lax.rsqrt(jnp.mean(xf * xf, axis=-1, keepdims=True) + EPS)
    return (y * w.astype(jnp.float32)).astype(x.dtype)


def _causal_depthwise_conv(u, w):
    k = w.shape[0]
    return lax.conv_general_dilated(
        u, w[:, None, :].astype(u.dtype), window_strides=(1,), padding=[(k - 1, 0)],
        dimension_numbers=('NWC', 'WIO', 'NWC'), feature_group_count=u.shape[-1])


def _chunk_states(decay, update):
    def step(s, inp):
        a_c, u_c = inp
        s = a_c * s + u_c
        return s, s
    init = jnp.zeros_like(update[:, 0])
    _, states = lax.scan(step, init, (jnp.moveaxis(decay, 1, 0), jnp.moveaxis(update, 1, 0)))
    return jnp.moveaxis(states, 0, 1)


def _conv_mixer(h, bg, cg, z, conv_w):
    y = bg * _causal_depthwise_conv(cg * h, conv_w)
    return y * jax.nn.silu(z)


def _gla_mixer(q, k, v, g_lr, z, gate_w, gate_b, norm_w):
    b, l, _ = q.shape
    nc = l // CHUNK
    dk = GLA_DK // GLA_HEADS
    dv = GLA_DV // GLA_HEADS
    f32 = jnp.float32
    qc = q.astype(f32).reshape(b, nc, CHUNK, GLA_HEADS, dk) * (dk ** -0.5)
    kc = k.astype(f32).reshape(b, nc, CHUNK, GLA_HEADS, dk)
    vc = v.astype(f32).reshape(b, nc, CHUNK, GLA_HEADS, dv)
    log_a = jax.nn.log_sigmoid((g_lr @ gate_w + gate_b).astype(f32)) / GLA_TAU
    log_a = log_a.reshape(b, nc, CHUNK, GLA_HEADS, dk)
    cum = jnp.cumsum(log_a, axis=2)
    total = cum[:, :, -1]
    k_dec = kc * jnp.exp(total[:, :, None] - cum)
    upd = jnp.einsum('bnchk,bnchv->bnhkv', k_dec, vc)
    states = _chunk_states(jnp.exp(total)[..., None], upd)
    o = jnp.einsum('bnchk,bnhkv->bnchv', qc, states)
    o = _rmsnorm(o, norm_w).reshape(b, l, GLA_DV)
    return (o * jax.nn.silu(z.astype(f32))).astype(q.dtype)


def _pool_mixer(u, z, pool_w, pool_scale):
    b, l, _ = u.shape
    f32 = jnp.float32
    uf = u.astype(f32)
    cs = jnp.pad(jnp.cumsum(uf, axis=1), ((0, 0), (1, 0), (0, 0)))
    pos = jnp.arange(l)
    outs = []
    for gi, win in enumerate(POOL_WINDOWS):
        sl = slice(gi * POOL_GROUP, (gi + 1) * POOL_GROUP)
        cg = cs[:, :, sl]
        prev = jnp.pad(cg, ((0, 0), (win - 1, 0), (0, 0)))[:, :l]
        cnt = jnp.minimum(pos + 1, win).astype(f32)[None, :, None]
        outs.append((cg[:, 1:] - prev) / cnt - uf[:, :, sl])
    pooled = jnp.stack(outs, axis=2)
    mixed = jnp.einsum('blgc,gcd->blgd', pooled, pool_w.astype(f32)).reshape(b, l, BRANCH)
    return (pool_scale.astype(f32) * mixed * jax.nn.silu(z.astype(f32))).astype(u.dtype)


def _ssd_mixer(xbc, dt, z, conv_w, conv_b, dt_bias, a_log, d_skip, norm_w):
    b, l, _ = xbc.shape
    nc = l // CHUNK
    hpg = SSD_HEADS // SSD_GROUPS
    f32 = jnp.float32
    xbc = jax.nn.silu(_causal_depthwise_conv(xbc, conv_w) + conv_b).astype(f32)
    xs, bm, cm = jnp.split(xbc, [BRANCH, BRANCH + SSD_GROUPS * SSD_STATE], axis=-1)
    xs = xs.reshape(b, nc, CHUNK, SSD_GROUPS, hpg, SSD_HEAD_DIM)
    bm = bm.reshape(b, nc, CHUNK, SSD_GROUPS, SSD_STATE)
    cm = cm.reshape(b, nc, CHUNK, SSD_GROUPS, SSD_STATE)
    dt = jax.nn.softplus(dt.astype(f32) + dt_bias.astype(f32)).reshape(b, nc, CHUNK, SSD_GROUPS, hpg)
    a = -jnp.exp(a_log.astype(f32)).reshape(SSD_GROUPS, hpg)
    cum = jnp.cumsum(dt * a, axis=2)
    total = cum[:, :, -1]
    w = jnp.exp(total[:, :, None] - cum) * dt
    upd = jnp.einsum('bncgs,bncgh,bncghp->bnghps', bm, w, xs)
    states = _chunk_states(jnp.exp(total)[..., None, None], upd)
    y = jnp.einsum('bncgs,bnghps->bncghp', cm, states)
    y = y + d_skip.astype(f32).reshape(SSD_GROUPS, hpg)[:, :, None] * xs
    y = y.reshape(b, l, BRANCH)
    y = _rmsnorm(y * jax.nn.silu(z.astype(f32)), norm_w)
    return y.astype(z.dtype)


def setup_inputs(seed: int = 0) -> dict:
    key = jax.random.key(seed)
    ks = jax.random.split(key, 20)
    f32 = jnp.float32
    nrm = lambda k, shape, s: jax.random.normal(k, shape, f32) * s
    x = jax.random.normal(ks[0], (BATCH, SEQ, D_MODEL), f32)
    norm_w = 1.0 + nrm(ks[1], (DEPTH, D_MODEL), 0.02)
    w_in = nrm(ks[2], (DEPTH, D_MODEL, D_PROJ), D_MODEL ** -0.5)
    conv_a_w = nrm(ks[3], (DEPTH, CONV_A_WIDTH, BRANCH), CONV_A_WIDTH ** -0.5)
    gla_gate_w = nrm(ks[4], (DEPTH, GLA_GATE_RANK, GLA_DK), GLA_GATE_RANK ** -0.5)
    gla_gate_b = nrm(ks[5], (DEPTH, GLA_DK), 0.1)
    gla_norm_w = 1.0 + nrm(ks[6], (DEPTH, GLA_DV // GLA_HEADS), 0.02)
    pool_w = nrm(ks[7], (DEPTH, len(POOL_WINDOWS), POOL_GROUP, POOL_GROUP), POOL_GROUP ** -0.5)
    pool_scale = 1.0 + nrm(ks[8], (DEPTH, BRANCH), 0.1)
    ssd_conv_w = nrm(ks[9], (DEPTH, SSD_CONV, SSD_XBC), SSD_CONV ** -0.5)
    ssd_conv_b = nrm(ks[10], (DEPTH, SSD_XBC), 0.01)
    dt0 = jnp.exp(jax.random.uniform(ks[11], (DEPTH, SSD_HEADS), f32, np.log(1e-3), np.log(1e-1)))
    ssd_dt_bias = dt0 + jnp.log(-jnp.expm1(-dt0))
    ssd_a_log = jnp.log(jax.random.uniform(ks[12], (DEPTH, SSD_HEADS), f32, 1.0, 16.0))
    ssd_d = 1.0 + nrm(ks[13], (DEPTH, SSD_HEADS), 0.1)
    ssd_norm_w = 1.0 + nrm(ks[14], (DEPTH, BRANCH), 0.02)
    w_out = nrm(ks[15], (DEPTH, D_MIX, D_MODEL), D_MIX ** -0.5)
    final_norm_w = 1.0 + nrm(ks[16], (D_MODEL,), 0.02)
    return {'x': x, 'norm_w': norm_w, 'w_in': w_in, 'conv_a_w': conv_a_w,
            'gla_gate_w': gla_gate_w, 'gla_gate_b': gla_gate_b, 'gla_norm_w': gla_norm_w,
            'pool_w': pool_w, 'pool_scale': pool_scale,
            'ssd_conv_w': ssd_conv_w, 'ssd_conv_b': ssd_conv_b, 'ssd_dt_bias': ssd_dt_bias,
            'ssd_a_log': ssd_a_log, 'ssd_d': ssd_d, 'ssd_norm_w': ssd_norm_w,
            'w_out': w_out, 'final_norm_w': final_norm_w}


def reference(x, norm_w, w_in, conv_a_w, gla_gate_w, gla_gate_b, gla_norm_w, pool_w, pool_scale,
              ssd_conv_w, ssd_conv_b, ssd_dt_bias, ssd_a_log, ssd_d, ssd_norm_w, w_out, final_norm_w):
    split_idx = [int(i) for i in np.cumsum(PROJ_SPLITS)[:-1]]
    for layer in range(DEPTH):
        h = _rmsnorm(x, norm_w[layer])
        proj = h @ w_in[layer]
        (a_h, a_b, a_c, a_z, g_q, g_k, g_v, g_lr, g_z, p_u, p_z, s_z, s_xbc, s_dt) = jnp.split(proj, split_idx, axis=-1)
        y_a = _conv_mixer(a_h, a_b, a_c, a_z, conv_a_w[layer])
        y_b = _gla_mixer(g_q, g_k, g_v, g_lr, g_z, gla_gate_w[layer], gla_gate_b[layer], gla_norm_w[layer])
        y_c = _pool_mixer(p_u, p_z, pool_w[layer], pool_scale[layer])
        y_d = _ssd_mixer(s_xbc, s_dt, s_z, ssd_conv_w[layer], ssd_conv_b[layer], ssd_dt_bias[layer],
                         ssd_a_log[layer], ssd_d[layer], ssd_norm_w[layer])
        mix = jnp.concatenate([y_a, y_b, y_c, y_d], axis=-1)
        x = x + mix @ w_out[layer]
    return _rmsnorm(x, final_norm_w)
```

```python
import contextlib
import os
import numpy as np
import concourse.bass as bass
import concourse.mybir as mybir
from concourse.bass_utils import run_bass_kernel_spmd

F32 = mybir.dt.float32
BF16 = mybir.dt.bfloat16
AF = mybir.ActivationFunctionType
ALU = mybir.AluOpType

D = 1024
DP = 3348
T = 256
EPS = 1e-6
POOL_WINDOWS = (2, 4, 8, 16)
C_AH, C_AB, C_AC, C_AZ, C_Q, C_K, C_V, C_DT, C_GLR, C_GZ, C_PU, C_PZ, C_SZ, C_XBC = (
    0, 256, 512, 768, 1024, 1152, 1280, 1536, 1540, 1556, 1812, 2068, 2324, 2580)
NK = 1666
ENGS = ("pe", "dve", "act", "pool", "sp")


class Prog:
    LAT = 0.2

    def __init__(self, nc):
        self.nc = nc
        self.ops = []

    tag = ""

    XSPLIT = {"x1": ["x1a", "x1b"], "x2": ["x2a", "x2b"]}

    def _ex(self, keys):
        out = []
        for k in keys:
            out.extend(self.XSPLIT.get(k, [k]))
        return out

    def op(self, eng, fn, reads=(), writes=(), dma_sem=None, signal=True, dur=0.3, aset=0):
        reads = self._ex(reads)
        writes = self._ex(writes)
        self.ops.append(dict(eng=eng, fns=[fn] if fn is not None else [], reads=list(reads), writes=list(writes),
                             dma_sem=dma_sem, signal=signal, dur=dur, aset=aset, tag=self.tag,
                             nm=",".join(writes[:2]) if writes else ""))

    def final_wait(self, eng, keys):
        self.op(eng, None, reads=keys, dur=0.0)

    def _merge_groups(self):
        out = []
        cur = None
        for o in self.ops:
            if cur is not None:
                assert o["eng"] == "pe", "unsignalled PE group interrupted"
                cur["fns"] += o["fns"]
                cur["reads"] += o["reads"]
                cur["writes"] += o["writes"]
                cur["dur"] += o["dur"]
                if o["signal"]:
                    cur["signal"] = True
                    out.append(cur)
                    cur = None
                continue
            if o["eng"] == "pe" and not o["signal"]:
                cur = dict(o)
                cur["fns"] = list(o["fns"])
                cur["reads"] = list(o["reads"])
                cur["writes"] = list(o["writes"])
                continue
            out.append(o)
        assert cur is None
        self.ops = out

    def schedule(self):
        self._merge_groups()
        ops = self.ops
        N = len(ops)
        lastw = {}
        readers = {}
        deps = [None] * N
        sdeps = []
        last_sp = None
        for i, o in enumerate(ops):
            d = set()
            for k in o["reads"]:
                if k in lastw:
                    d.add(lastw[k])
            for k in o["writes"]:
                if k in lastw:
                    d.add(lastw[k])
                d.update(readers.get(k, ()))
            d.discard(i)
            deps[i] = sorted(d)
            sd = set(d)
            if o["eng"] == "sp":
                if last_sp is not None:
                    sd.add(last_sp)
                last_sp = i
            sdeps.append(sorted(sd))
            for k in o["writes"]:
                lastw[k] = i
                readers[k] = []
            for k in o["reads"]:
                readers.setdefault(k, []).append(i)
        children = [[] for _ in range(N)]
        indeg = [0] * N
        for i in range(N):
            indeg[i] = len(sdeps[i])
            for d in sdeps[i]:
                children[d].append(i)
        prio = [0.0] * N
        for i in range(N - 1, -1, -1):
            m = 0.0
            for c in children[i]:
                if prio[c] > m:
                    m = prio[c]
            prio[i] = m + ops[i]["dur"] + 0.05
        def run(pr_):
            indeg_ = list(indeg)
            limdep = [None] * N
            blockers = []
            eng_free = {e: 0.0 for e in ENGS}
            act_set = [0]
            finish = [0.0] * N
            ready = [i for i in range(N) if indeg_[i] == 0]
            order = []
            LAT = self.LAT
            SAME = {'dve': 0.13, 'act': 0.10, 'pool': 0.17} if os.environ.get('K_SAME', '1') == '1' else {}
            WIN = float(os.environ.get('K_WIN', '0.15'))
            STAT = {} if os.environ.get('K_STAT') else None
            TL = [] if os.environ.get('K_TL') else None
            self.tl = TL
            self.stat = STAT
            HM = int(os.environ.get('K_H', '1'))
            ALPHA = float(os.environ.get('K_ALPHA', '0.01'))
            BETA = float(os.environ.get('K_BETA', '0.01'))
            while ready:
                best = None
                best_key = None
                cand = []
                for i in ready:
                    o = ops[i]
                    e = o["eng"]
                    t = eng_free[e]
                    for d in deps[i]:
                        f = finish[d] + (LAT if ops[d]["eng"] != e or ops[d]["dma_sem"] else SAME.get(e, 0.0))
                        if f > t:
                            t = f
                    if e == "act" and o["aset"] and o["aset"] != act_set[0]:
                        t += 1.3
                    cand.append((t, i))
                tmin = min(c[0] for c in cand)
                for t, i in cand:
                    if t <= tmin + WIN:
                        if HM == 1:
                            key = (-pr_[i], t)
                        elif HM == 2:
                            key = (i, t)
                        elif HM == 3:
                            key = (t - ALPHA * pr_[i], i)
                        else:
                            key = (-(pr_[i] - BETA * i), t)
                        if best_key is None or key < best_key:
                            best_key = key
                            best = (t, i)
                t, i = best
                o = ops[i]
                e = o["eng"]
                lim_ = None
                lt_ = -1.0
                for d in deps[i]:
                    if finish[d] > lt_:
                        lt_, lim_ = finish[d], d
                limdep[i] = lim_
                if e == "pe" and lim_ is not None and t - eng_free[e] > 0.1 and lt_ + 0.3 > t:
                    blockers.append((t - eng_free[e], i, lim_))
                if STAT is not None and e in ("pe", "dve", "act"):
                    idle = t - eng_free[e]
                    if idle > 0.05:
                        lim = None
                        lt = -1
                        for d in deps[i]:
                            f = finish[d]
                            if f > lt:
                                lt, lim = f, d
                        kk = (e, ops[lim]["eng"] if lim is not None else "-", o.get("tag", ""), ops[lim].get("tag", "") if lim is not None else "")
                        STAT[kk] = STAT.get(kk, 0.0) + idle
                if e == "act" and o["aset"]:
                    act_set[0] = o["aset"]
                if o["dma_sem"]:
                    eng_free[e] = t + 0.05
                    finish[i] = t + o["dur"]
                else:
                    eng_free[e] = t + o["dur"]
                    finish[i] = t + o["dur"]
                order.append(i)
                if TL is not None:
                    TL.append((t, e, o["dur"], o.get("tag", ""), o.get("nm", ""), i))
                ready.remove(i)
                for c in children[i]:
                    indeg_[c] -= 1
                    if indeg_[c] == 0:
                        ready.append(c)

            return (max(finish) if N else 0.0), order, blockers, limdep
        ITER = int(os.environ.get('K_ITER', '30'))
        GAIN = float(os.environ.get('K_GAIN', '1.0'))
        boost = [0.0] * N
        best = None
        hist = []
        for it in range(ITER):
            pr_ = [prio[i] + boost[i] for i in range(N)]
            ms_, ord_, blk_, lim_ = run(pr_)
            hist.append(round(ms_, 1))
            if best is None or ms_ < best[0]:
                best = (ms_, ord_)
            for (idle, i, d) in blk_:
                k = 0
                while d is not None and k < 12:
                    boost[d] += idle * GAIN
                    d = lim_[d]
                    k += 1
        self.hist = hist
        order = best[1]
        finish = [best[0]]
        assert len(order) == N
        self.order = order
        self.deps = deps
        self.makespan = max(finish) if N else 0.0
        return self.makespan

    def emit(self, sems, block):
        ops = self.ops
        cnt = {}
        ticket = {}
        queues = {e: [] for e in ENGS}
        seen = {e: {} for e in ENGS}
        for i in self.order:
            o = ops[i]
            e = o["eng"]
            waits = {}
            for d in self.deps[i]:
                od = ops[d]
                if od["eng"] == "pe" and e == "pe" and not od["dma_sem"]:
                    continue
                if not od["fns"]:
                    continue
                sk, v = ticket[d]
                if seen[e].get(sk, 0) >= v:
                    continue
                waits[sk] = max(waits.get(sk, 0), v)
            for sk, v in waits.items():
                seen[e][sk] = v
            inc = None
            if o["fns"]:
                if o["dma_sem"]:
                    sk = o["dma_sem"]
                    cnt[sk] = cnt.get(sk, 0) + 16
                    inc = (sk, 16)
                else:
                    sk = e
                    cnt[sk] = cnt.get(sk, 0) + 1
                    inc = (sk, 1)
                ticket[i] = (sk, cnt[sk])
            queues[e].append((list(waits.items()), o["fns"], inc))
        engmap = {"pe": "tensor", "dve": "vector", "act": "scalar", "pool": "gpsimd", "sp": "sync"}

        def make(ename):
            items = queues[ename]

            def body(eng):
                for waits, fns, inc in items:
                    for sk, v in waits:
                        eng.wait_ge(sems[sk], v)
                    ins = None
                    for fn in fns:
                        ins = fn(eng)
                    if ins is not None and inc is not None:
                        ins.then_inc(sems[inc[0]], inc[1])
            return body

        for ename in ENGS:
            if queues[ename]:
                getattr(block, engmap[ename])(make(ename))


def build_nc(SEQ, NL):
    NT = SEQ // T
    nc = bass.Bass("TRN2", target_bir_lowering=False)
    dt_ = nc.dram_tensor
    x_d = dt_("x", [SEQ, D], F32, kind="ExternalInput").ap()
    win_d = dt_("w_in", [NL, D, DP], F32, kind="ExternalInput").ap()
    wout_d = dt_("w_out", [NL, D, D], F32, kind="ExternalInput").ap()
    cp_d = dt_("cpack", [NL, 128, 43], F32, kind="ExternalInput").ap()
    bp_d = dt_("bpack", [NL, 128, 1160], F32, kind="ExternalInput").ap()
    gw_d = dt_("gatew", [NL, 16, 128], F32, kind="ExternalInput").ap()
    pw_d = dt_("poolw", [NL, 128, 256], F32, kind="ExternalInput").ap()
    fnw_d = dt_("fnw", [128, D], F32, kind="ExternalInput").ap()
    kp_d = dt_("kpack", [128, NK], F32, kind="ExternalInput").ap()
    y_d = dt_("y", [SEQ, D], F32, kind="ExternalOutput").ap()
    scr_d = dt_("scr", [SEQ, D], F32, kind="Internal").ap() if NL > 1 else None

    with contextlib.ExitStack() as es:
        def sb(name, shape, dt):
            return es.enter_context(nc.sbuf_tensor(name, shape, dt))

        def ps(name, shape, dt):
            return es.enter_context(nc.psum_tensor(name, shape, dt))

        wbf = sb("wbf", [128, 8, DP], BF16)
        woutbf = sb("woutbf", [128, 8, D], BF16)
        kp = sb("kp", [128, NK], F32)
        cp = sb("cp", [128, 43], F32)
        bp = sb("bp", [128, 1160], F32)
        fnw = sb("fnw_sb", [128, D], F32)
        gwf = sb("gwf", [16, 128], F32)
        gwb = sb("gwb", [16, 128], BF16)
        pwf = sb("pwf", [128, 256], F32)
        pwb = sb("pwb", [128, 2, 128], BF16)
        identb = sb("identb", [128, 128], BF16)
        ones = sb("ones", [128, 128], F32)
        abc = sb("abc", [128, 4], F32)
        xt = sb("xt", [128, 3, 2, D], F32)
        mhalf = sb("mhalf", [128, 2], F32)
        cvc = sb("cvc", [128, 2, T], F32)
        dtx = sb("dtx", [128, 2, 2, 4], F32)
        tmpS = sb("tmpS", [128, 256], F32)
        tmpH = sb("tmpH", [128, 256], F32)
        sq2 = sb("sq2", [128, 2, 128], F32)
        rs2 = sb("rs2", [128, 128], F32)
        hbf = sb("hbf", [128, 2, D], BF16)
        hT = sb("hT", [128, 2, 8, T], BF16)
        st = sb("st", [128, 16], F32)
        ah = sb("ah", [128, 2, T], F32)
        ubuf = sb("ubuf", [128, 2, T + 2], F32)
        cva = sb("cva", [128, 2, T], F32)
        yab = sb("yab", [128, 2, T], F32)
        sza = sb("sza", [128, 2, 2, T], F32)
        szg = sb("szg", [128, 2, 2, T], F32)
        szp = sb("szp", [128, 2, 2, T], F32)
        szs = sb("szs", [128, 2, 2, T], F32)
        qbf = sb("qbf", [128, 2, T], BF16)
        glr = sb("glr", [16, 2, T], BF16)
        pbuf = sb("pbuf", [128, 2, T + 16], F32)
        bA = sb("bA", [128, 2, T + 16], F32)
        bB = sb("bB", [128, 2, T + 16], F32)
        pl = sb("pl", [128, 2, T], F32)
        pooled = sb("pooled", [128, 2, T], BF16)
        xbuf = sb("xbuf", [128, 6, T + 3], F32)
        xs = sb("xs", [128, 2, 2, T], F32)
        bfm = sb("bfm", [128, 2, 2, T], F32)
        cbf = sb("cbf", [128, 2, 2, T], BF16)
        ksb = sb("ksb", [128, 2, 2, 128], F32)
        vbf = sb("vbf", [128, 2, 2, 256], BF16)
        gx = sb("gx", [128, 128], F32)
        spl = sb("spl", [128, 2, 128], F32)
        Eg = sb("Eg", [128, 128], F32)
        kdec = sb("kdec", [128, 2, 2, 128], BF16)
        gdec = sb("gdec", [128, 2, 4], F32)
        dts = sb("dts", [128, 2, 32], F32)
        sdec = sb("sdec", [128, 2, 2, 8], F32)
        wx = sb("wx", [128, 2, 2, 256], BF16)
        btok = sb("btok", [128, 2, 2, 256], BF16)
        S = sb("S", [128, 256], F32)
        Sb = sb("Sb", [128, 256], BF16)
        H = sb("H", [128, 256], F32)
        Hb = sb("Hb", [128, 256], BF16)
        osb = sb("osb", [128, 2, T], F32)
        sq = sb("sq", [128, 2, T], F32)
        rs = sb("rs", [128, T], F32)
        t1 = sb("t1", [128, 2, 128], F32)
        ysb = sb("ysb", [128, 2, 128], F32)
        yz = sb("yz", [128, 2, T], F32)
        mixT = sb("mixT", [128, 8, T], BF16)

        PA = ps("PA", [128, 512], F32)
        PB = ps("PB", [128, 512], F32)
        PC = ps("PC", [128, 512], F32)
        KV = ps("KV", [128, 512], F32)
        SM = ps("SM", [128, 512], F32)
        TRf = SM
        UB = ps("UB", [128, 512], F32)
        TRb = ps("TRb", [128, 2, 2, 128], BF16)
        OY = ps("OY", [128, 4, 128], F32)
        OT = OY[:, 0:2]
        YT = OY[:, 2:4]

        sem_names = list(ENGS) + ["ldx0", "ldx1", "ldx2", "st0", "st1", "st2", "lw0", "lw1", "lw2", "lw3", "lc"] + ["lk%d" % i for i in range(8)]
        sems = {k: es.enter_context(nc.semaphore(k)) for k in sem_names}
        block = es.enter_context(nc.Block())
        P = Prog(nc)
        op = P.op

        def PSR(bank):
            return [getattr(bank, "lock", bank) + ".lk"]

        PECOL = float(os.environ.get("K_PECOL", "0.00046"))
        ASET = {AF.Silu: 18, AF.Exp: 6, AF.Ln: 6, AF.Sqrt: 3}

        def act(out, in_, func, reads, writes, **kw):
            n = out.free_size()
            dur = 0.2 + n * 0.0009 + (0.1 if "accum_out" in kw else 0.0)
            return op("act", lambda e: e.activation(out=out, in_=in_, func=func, **kw), reads=reads, writes=writes,
                      dur=dur, aset=ASET.get(func, 0))

        def edur(eng, n):
            if eng == "pool":
                return 0.1 + n * 0.0022
            return 0.07 + n * 0.0014

        def cp_any(eng, out, in_, reads, writes):
            if eng == "act":
                return act(out, in_, AF.Copy, reads, writes)
            return op(eng, lambda e: e.tensor_copy(out=out, in_=in_), reads=reads, writes=writes, dur=edur(eng, out.free_size()))

        def tt(eng, out, in0, in1, o, reads, writes):
            return op(eng, lambda e: e.tensor_tensor(out=out, in0=in0, in1=in1, op=o), reads=reads, writes=writes,
                      dur=edur(eng, out.free_size()))

        def stt(out, in0, scalar, in1, o0, o1, reads, writes):
            return op("dve", lambda e: e.scalar_tensor_tensor(out=out, in0=in0, scalar=scalar, in1=in1, op0=o0, op1=o1),
                      reads=reads, writes=writes, dur=edur("dve", out.free_size()))

        def ts(eng, out, in0, s1, s2, o0, o1, reads, writes):
            d_ = edur(eng, out.free_size())
            if o1 is None:
                return op(eng, lambda e: e.tensor_scalar(out=out, in0=in0, scalar1=s1, scalar2=None, op0=o0),
                          reads=reads, writes=writes, dur=d_)
            return op(eng, lambda e: e.tensor_scalar(out=out, in0=in0, scalar1=s1, scalar2=s2, op0=o0, op1=o1),
                      reads=reads, writes=writes, dur=d_)

        def mm(out, lhsT, rhs, start, stop, reads, writes):
            n = max(64, rhs.free_size())
            d_ = n * PECOL * (2.5 if lhsT.dtype == F32 else 1) + 0.01
            return op("pe", lambda e: e.matmul(out, lhsT=lhsT, rhs=rhs, start=start, stop=stop),
                      reads=reads, writes=writes, signal=stop, dur=d_)

        def tr(out, in_, ident, reads, writes, signal):
            return op("pe", lambda e: e.transpose(out=out, in_=in_, identity=ident), reads=reads, writes=writes, signal=signal,
                      dur=0.12)

        def wkeys(col0, ncols):
            return ["w%d_%d" % (kc, hf) for hf in range(col0 // 1024, (col0 + ncols - 1) // 1024 + 1) for kc in range(8)]

        def dma(out, in_, reads, writes, sem):
            nbytes = out.free_size() * out.partition_size() * 4
            return op("sp", lambda e: e.dma_start(out=out, in_=in_), reads=reads, writes=writes, dma_sem=sem,
                      dur=1.5 + nbytes / 1.2e5)

        dma(kp[:], kp_d[:, :], [], ["kp"], "lk0")
        dma(fnw[:], fnw_d[:, :], [], ["fnw"], "lk1")
        cp_any("dve", identb[:], kp[:, 0:128], ["kp"], ["identb"])
        op("pool", lambda e: e.memset(ones[:], 1.0), writes=["ones"])
        op("pool", lambda e: e.memset(mhalf[:], -0.5), writes=["mhalf"])
        identf = kp[:, 0:128]
        mstrict = kp[:, 128:256]
        ind = kp[:, 256:258]
        blk64 = kp[:, 258:386]
        bdmask = kp[:, 386:642]

        rr = {"i": 0}

        def evac_eng():
            rr["i"] += 1
            return ("act", "dve")[rr["i"] % 2]

        for L in range(NL):
            last = (L == NL - 1)
            src = x_d if L == 0 else scr_d
            dst = y_d if last else scr_d
            dma(cp[:], cp_d[L], [], ["cp"], "lk2")
            dma(bp[:], bp_d[L], [], ["bp"], "lk3")
            dma(gwf[:], gw_d[L], [], ["gwf"], "lk4")
            dma(pwf[:], pw_d[L], [], ["pwf"], "lk5")
            cp_any("dve", gwb[:], gwf[:], ["gwf"], ["gwb"])
            cp_any("dve", pwb[:].rearrange("p a b -> p (a b)"), pwf[:], ["pwf"], ["pwb"])
            act(abc[:], bp[:, 1156:1160], AF.Exp, ["bp"], ["abc"])
            ts("dve", abc[:], abc[:], -1.0, None, ALU.mult, None, ["abc"], ["abc"])
            dma(xt[:, 0], (x_d if L == 0 else scr_d)[0:T, :].rearrange("(j p) d -> p j d", p=128),
                ["dst%d_%d" % (L - 1, 0)] if L > 0 else [], ["x0"], "ldx0")
            k = 0
            WR = [(0, 1024), (1024, 2048), (2048, 3072), (3072, DP)]

            def stage(k_):
                q = k_ % 4
                return xt[:, 1 + q // 2].rearrange("p j d -> p (j d)")[:, (q % 2) * 1024:(q % 2 + 1) * 1024], \
                    "x%d%s" % (1 + q // 2, "ab"[q % 2]), "lw%d" % q

            for r in (2, 3, 1, 0):
                c0, c1 = WR[r]
                for kc in range(8):
                    stg, sk, sem_ = stage(k)
                    dma(stg[:, 0:c1 - c0], win_d[L, kc * 128:(kc + 1) * 128, c0:c1], [], [sk], sem_)
                    cp_any(("act", "dve", "pool")[k % 3], wbf[:, kc, c0:c1], stg[:, 0:c1 - c0], [sk], ["w%d_%d" % (kc, r)])
                    k += 1
            for kc in range(8):
                stg, sk, sem_ = stage(k)
                dma(stg[:, 0:D], wout_d[L, kc * 128:(kc + 1) * 128, :], [], [sk], sem_)
                cp_any(("act", "dve", "pool")[k % 3], woutbf[:, kc, :], stg[:, 0:D], [sk], ["woutbf"])
                k += 1
            op("dve", lambda e: e.memset(S[:], 0.0), writes=["S"])
            op("dve", lambda e: e.memset(H[:], 0.0), writes=["H"])
            op("pool", lambda e: e.memset(ubuf[:, :, 0:2], 0.0), writes=["ubuf0", "ubuf1"])
            op("pool", lambda e: e.memset(pbuf[:, :, 0:16], 0.0), writes=["pbuf0", "pbuf1"])
            op("pool", lambda e: e.memset(xbuf[:, :, 0:3], 0.0), writes=["xbuf%d" % c for c in range(6)])

            normw = bp[:, 0:1024]
            gateb = bp[:, 1024:1152]
            dtb = bp[:, 1152:1156]

            def load_x(n_):
                s_ = n_ % 3
                rd = ["dst%d_%d" % (L - 1, n_)] if L > 0 else []
                dma(xt[:, s_], src[n_ * T:(n_ + 1) * T, :].rearrange("(j p) d -> p j d", p=128), rd, ["x%d" % s_], "ldx%d" % s_)

            bank = {"i": 0}
            zc = {"i": 0}
            ZE = os.environ.get("K_ZE", "act,act,act,dve").split(",")
            MV = os.environ.get("K_MV", "1") == "1"
            hpar = {"p": 0}

            NB = int(os.environ.get("K_NB", "3"))
            BANKS = [(PA, "PA"), (PB, "PB"), (PC, "PC")][:NB]

            def next_bank():
                b = BANKS[bank["i"] % len(BANKS)]
                bank["i"] += 1
                return b

            def fm_tile(col0, ncols, evac):
                bt, bk = next_bank()
                hp = hpar["p"]
                for kc in range(8):
                    mm(bt[0:ncols, 0:T], wbf[:, kc, col0:col0 + ncols], hT[:, hp, kc, :], kc == 0, kc == 7,
                       wkeys(col0, ncols) + ["hT%d" % hp], [bk])
                evac(bt[0:ncols, 0:T], bk)

            def pre(n):
                s = n % 3
                pr = n % 2
                hpar["p"] = pr
                xk = "x%d" % s
                X = xt[:, s]
                for j in range(2):
                    act(hbf[:, j, :], X[:, j, :], AF.Square, [xk], ["st_ss%d" % j, "hbf%d" % j], accum_out=st[:, j:j + 1])
                ts("dve", st[:, 2:4], st[:, 0:2], 1.0 / D, EPS, ALU.mult, ALU.add, ["st_ss0", "st_ss1"], ["st_ms"])
                tt("pool", st[:, 6:8], st[:, 2:4], mhalf[:], ALU.pow, ["st_ms", "mhalf"], ["st_rstd"])
                yield
                for j in range(2):
                    stt(hbf[:, j, :], X[:, j, :], st[:, 6 + j:7 + j], normw, ALU.mult, ALU.mult,
                        [xk, "st_rstd", "bp"], ["hbf%d" % j])
                    yield
                for g in range(4):
                    for kk in range(2):
                        for j in range(2):
                            kc = 2 * g + kk
                            tr(TRb[:, kk, j, :], hbf[:, j, kc * 128:(kc + 1) * 128], identb[:],
                               ["hbf%d" % j, "identb"], ["TRb"], signal=(kk == 1 and j == 1))
                    cp_any(evac_eng(), hT[:, pr, 2 * g:2 * g + 2, :].rearrange("p a b -> p (a b)"),
                           TRb[:].rearrange("p a j b -> p (a j b)"), ["TRb"], ["hT%d" % pr] + PSR("TRb"))
                    yield
                for c in range(6):
                    def ev_x(p, bk, c=c):
                        xk_ = "xbuf%d" % c
                        act(xbuf[:, c, 3:T + 3], p, AF.Copy, [bk], [xk_] + PSR(bk))
                        if c < 2:
                            o_, ok_ = xs[:, pr, c, :], "xs%dp%d" % (c, pr)
                        elif c < 4:
                            o_, ok_ = bfm[:, pr, c - 2, :], "bfm%dp%d" % (c - 2, pr)
                        else:
                            o_, ok_ = cvc[:, c - 4, :], "cvc%d" % (c - 4)
                        ts("dve", o_, xbuf[:, c, 0:T], cp[:, 6 + c * 4:7 + c * 4], None, ALU.mult, None, [xk_, "cp"], [ok_])
                        for k_ in range(1, 4):
                            stt(o_, xbuf[:, c, k_:T + k_], cp[:, 6 + c * 4 + k_:7 + c * 4 + k_], o_, ALU.mult, ALU.add,
                                [xk_, "cp", ok_], [ok_])
                        cp_any("pool", xbuf[:, c, 0:3], xbuf[:, c, T:T + 3], [xk_], [xk_])
                    fm_tile(C_XBC + c * 128, 128, ev_x)
                    yield
                for (col, buf, nm) in ((C_GZ, szg, "szg"), (C_SZ, szs, "szs"), (C_AZ, sza, "sza"), (C_PZ, szp, "szp")):
                    for i in range(2):
                        fm_tile(col + i * 128, 128,
                                lambda p, bk, buf=buf, nm=nm, i=i: cp_any(ZE[(zc.__setitem__("i", zc["i"] + 1) or zc["i"]) % len(ZE)], buf[:, pr, i, :], p, [bk],
                                                                          ["%s%dp%d" % (nm, i, pr)] + PSR(bk)))
                        yield
                for j in range(2):
                    for kc in range(8):
                        mm(KV[:, 0:388], hT[:, pr, kc, j * 128:(j + 1) * 128], wbf[:, kc, C_K:C_K + 388], kc == 0, kc == 7,
                           ["hT%d" % pr] + wkeys(C_K, 388), ["KV"])
                    act(ksb[:, pr, j, :], KV[:, 0:128], AF.Copy, ["KV"], ["ksb%dp%d" % (j, pr)] + PSR("KV"))
                    act(vbf[:, pr, j, :], KV[:, 128:384], AF.Copy, ["KV"], ["vbf%dp%d" % (j, pr)] + PSR("KV"))
                    tt("dve", dtx[:, pr, j, :], KV[:, 384:388], dtb, ALU.add, ["KV", "bp"], ["dtx%dp%d" % (j, pr)] + PSR("KV"))
                    yield
                fm_tile(C_Q, 128, lambda p, bk: act(qbf[:, pr, :], p, AF.Copy, [bk], ["qbfp%d" % pr] + PSR(bk), scale=32.0 ** -0.5))
                yield
                fm_tile(C_GLR, 16, lambda p, bk: act(glr[:, pr, :], p, AF.Copy, [bk], ["glrp%d" % pr] + PSR(bk)))
                yield

            def silu_batch(n):
                pr = n % 2
                for c in range(2):
                    k_ = "xs%dp%d" % (c, pr)
                    act(xs[:, pr, c, :], xs[:, pr, c, :], AF.Silu, [k_, "cp"], [k_], bias=cp[:, 30 + c:31 + c])
                for c in range(2):
                    k_ = "bfm%dp%d" % (c, pr)
                    act(bfm[:, pr, c, :], bfm[:, pr, c, :], AF.Silu, [k_, "cp"], [k_], bias=cp[:, 32 + c:33 + c])
                for c in range(2):
                    act(cbf[:, pr, c, :], cvc[:, c, :], AF.Silu, ["cvc%d" % c, "cp"], ["cbf%dp%d" % (c, pr)], bias=cp[:, 34 + c:35 + c])
                for (buf, nm) in ((szg, "szg"), (szs, "szs"), (sza, "sza"), (szp, "szp")):
                    for i in range(2):
                        k_ = "%s%dp%d" % (nm, i, pr)
                        act(buf[:, pr, i, :], buf[:, pr, i, :], AF.Silu, [k_], [k_])

            def big(n):
                pr = n % 2
                hpar["p"] = pr
                for i in range(2):
                    fm_tile(C_AH + i * 128, 128, lambda p, bk, i=i: act(ah[:, i, :], p, AF.Copy, [bk], ["ah%d" % i] + PSR(bk)))
                    yield
                for i in range(2):
                    def ev_ac(p, bk, i=i):
                        tt("dve", ubuf[:, i, 2:T + 2], p, ah[:, i, :], ALU.mult, [bk, "ah%d" % i], ["ubuf%d" % i] + PSR(bk))
                        uk = ["ubuf%d" % i, "cp"]
                        ts("dve", cva[:, i, :], ubuf[:, i, 0:T], cp[:, i * 3:i * 3 + 1], None, ALU.mult, None, uk, ["cva%d" % i])
                        stt(cva[:, i, :], ubuf[:, i, 1:T + 1], cp[:, i * 3 + 1:i * 3 + 2], cva[:, i, :], ALU.mult, ALU.add,
                            uk + ["cva%d" % i], ["cva%d" % i])
                        stt(cva[:, i, :], ubuf[:, i, 2:T + 2], cp[:, i * 3 + 2:i * 3 + 3], cva[:, i, :], ALU.mult, ALU.add,
                            uk + ["cva%d" % i], ["cva%d" % i])
                        cp_any("pool", ubuf[:, i, 0:2], ubuf[:, i, T:T + 2], ["ubuf%d" % i], ["ubuf%d" % i])
                    fm_tile(C_AC + i * 128, 128, ev_ac)
                    yield
                for i in range(2):
                    def ev_ab(p, bk, i=i):
                        tt("dve", yab[:, i, :], p, cva[:, i, :], ALU.mult, [bk, "cva%d" % i], ["yab%d" % i] + PSR(bk))
                        tt("pool", mixT[:, i, :], yab[:, i, :], sza[:, pr, i, :], ALU.mult,
                           ["yab%d" % i, "sza%dp%d" % (i, pr)], ["mix%d" % i])
                    fm_tile(C_AB + i * 128, 128, ev_ab)
                    yield
                for i in range(2):
                    fm_tile(C_PU + i * 128, 128, lambda p, bk, i=i: act(pbuf[:, i, 16:T + 16], p, AF.Copy, [bk],
                                                                       ["pbuf%d" % i] + PSR(bk)))
                    yield
                invc = kp[:, 642:1154] if n == 0 else kp[:, 1154:1666]
                tt("pool", bA[:, 0, 14:T + 16], pbuf[:, 0, 14:T + 16], pbuf[:, 0, 13:T + 15], ALU.add, ["pbuf0"], ["bA0"])
                tt("pool", bB[64:128, 0, 16:T + 16], bA[64:128, 0, 16:T + 16], bA[64:128, 0, 14:T + 14], ALU.add, ["bA0"], ["bB0"])
                tt("pool", pl[0:64, 0, :], bA[0:64, 0, 16:T + 16], invc[0:64, 0:T], ALU.mult, ["bA0", "kp"], ["pl0"])
                tt("pool", pl[64:128, 0, :], bB[64:128, 0, 16:T + 16], invc[64:128, 0:T], ALU.mult, ["bB0", "kp"], ["pl0"])
                tt("pool", pooled[:, 0, :], pl[:, 0, :], pbuf[:, 0, 16:T + 16], ALU.subtract, ["pl0", "pbuf0"], ["pooled0"])
                cp_any("pool", pbuf[:, 0, 0:16], pbuf[:, 0, T:T + 16], ["pbuf0"], ["pbuf0"])
                yield
                tt("pool", bA[:, 1, 2:T + 16], pbuf[:, 1, 2:T + 16], pbuf[:, 1, 1:T + 15], ALU.add, ["pbuf1"], ["bA1"])
                tt("pool", bB[:, 1, 4:T + 16], bA[:, 1, 4:T + 16], bA[:, 1, 2:T + 14], ALU.add, ["bA1"], ["bB1"])
                tt("pool", bA[:, 1, 8:T + 16], bB[:, 1, 8:T + 16], bB[:, 1, 4:T + 12], ALU.add, ["bB1", "bA1"], ["bA1"])
                tt("pool", bB[64:128, 1, 16:T + 16], bA[64:128, 1, 16:T + 16], bA[64:128, 1, 8:T + 8], ALU.add, ["bA1", "bB1"], ["bB1"])
                yield
                tt("pool", pl[0:64, 1, :], bA[0:64, 1, 16:T + 16], invc[0:64, T:2 * T], ALU.mult, ["bA1", "kp"], ["pl1"])
                tt("pool", pl[64:128, 1, :], bB[64:128, 1, 16:T + 16], invc[64:128, T:2 * T], ALU.mult, ["bB1", "kp"], ["pl1"])
                tt("pool", pooled[:, 1, :], pl[:, 1, :], pbuf[:, 1, 16:T + 16], ALU.subtract, ["pl1", "pbuf1"], ["pooled1"])
                cp_any("pool", pbuf[:, 1, 0:16], pbuf[:, 1, T:T + 16], ["pbuf1"], ["pbuf1"])
                yield
                for i in range(2):
                    bt, bk = next_bank()
                    mm(bt[:, 0:T], pwb[:, i, :], pooled[:, i, :], True, True, ["pwb", "pooled%d" % i], [bk])
                    stt(mixT[:, 4 + i, :], bt[:, 0:T], cp[:, 37 + i:38 + i], szp[:, pr, i, :], ALU.mult, ALU.mult,
                        [bk, "cp", "szp%dp%d" % (i, pr)], ["mix%d" % (4 + i)] + PSR(bk))
                    yield

            def pro(n):
                pr = n % 2
                for j in range(2):
                    D_ = dts[:, j]
                    dk_ = "dts%d" % j
                    act(D_[:, 4:8], dtx[:, pr, j, :], AF.Exp, ["dtx%dp%d" % (j, pr)], [dk_])
                    act(D_[:, 8:12], D_[:, 4:8], AF.Ln, [dk_], [dk_], bias=1.0)
                    tt("dve", D_[:, 12:16], D_[:, 8:12], abc[:], ALU.mult, [dk_, "abc"], [dk_])
                    ts("dve", D_[:, 24:28], D_[:, 12:16], ind[:, 0:1], None, ALU.mult, None, [dk_, "kp"], [dk_])
                    ts("dve", D_[:, 28:32], D_[:, 12:16], ind[:, 1:2], None, ALU.mult, None, [dk_, "kp"], [dk_])
                    mm(SM[:, 0:128], glr[0:16, pr, j * 128:(j + 1) * 128], gwb[0:16, :], True, True, ["glrp%d" % pr, "gwb"], ["SM"])
                    tt("dve", gx[:], SM[:, 0:128], gateb, ALU.add, ["SM", "bp"], ["gx"] + PSR("SM"))
                    yield
                    act(gx[:], gx[:], AF.Exp, ["gx"], ["gx"], scale=-1.0)
                    act(spl[:, j, :], gx[:], AF.Ln, ["gx"], ["spl%d" % j], bias=1.0)
                    mm(SM[:, 128:256], mstrict, spl[:, j, :], True, True, ["kp", "spl%d" % j], ["SM"])
                    mm(SM[:, 256:258], spl[:, j, :], ind, True, True, ["kp", "spl%d" % j], ["SM"])
                    mm(SM[:, 300:304], mstrict, D_[:, 12:16], True, True, ["kp", dk_], ["SM"])
                    mm(SM[:, 304:312], ones[:], D_[:, 24:32], True, True, ["ones", dk_], ["SM"])
                    act(Eg[:], SM[:, 128:256], AF.Exp, ["SM"], ["Eg"] + PSR("SM"), scale=-1.0 / 16)
                    act(gdec[:, pr, 2 * j:2 * j + 2], SM[:, 256:258], AF.Exp, ["SM"], ["gdec%dp%d" % (j, pr)] + PSR("SM"), scale=-1.0 / 16)
                    act(D_[:, 16:20], SM[:, 300:304], AF.Exp, ["SM"], [dk_] + PSR("SM"))
                    act(sdec[:, pr, j, :], SM[:, 304:312], AF.Exp, ["SM"], ["sdec%dp%d" % (j, pr)] + PSR("SM"))
                    tt("pool" if MV else "dve", kdec[:, pr, j, :], ksb[:, pr, j, :], Eg[:], ALU.mult, ["ksb%dp%d" % (j, pr), "Eg"], ["kdec%dp%d" % (j, pr)])
                    tt("dve", D_[:, 20:24], D_[:, 16:20], D_[:, 8:12], ALU.mult, [dk_], [dk_])
                    yield
                    for i in range(2):
                        tr(TRf[:, i * 128:(i + 1) * 128], xs[:, pr, i, j * 128:(j + 1) * 128], identf,
                           ["xs%dp%d" % (i, pr), "kp"], ["SM"], False)
                    for i in range(2):
                        tr(TRf[:, 256 + i * 128:256 + (i + 1) * 128], bfm[:, pr, i, j * 128:(j + 1) * 128], identf,
                           ["bfm%dp%d" % (i, pr), "kp"], ["SM"], i == 1)
                    tt("dve", wx[:, pr, j, :].rearrange("p (a b) -> p a b", a=4), TRf[:, 0:256].rearrange("p (a b) -> p a b", a=4),
                       D_[:, 20:24].unsqueeze(2).to_broadcast([128, 4, 64]), ALU.mult, ["SM", dk_], ["wx%dp%d" % (j, pr)] + PSR("SM"))
                    cp_any("act" if MV else "dve", btok[:, pr, j, :], TRf[:, 256:512], ["SM"], ["btok%dp%d" % (j, pr)] + PSR("SM"))
                    yield

            def rec(n):
                pr = n % 2
                for j in range(2):
                    for cc in range(2):
                        c = 2 * j + cc
                        r0 = cc * 64
                        mm(UB[:, 0:256], kdec[r0:r0 + 64, pr, j, :], vbf[r0:r0 + 64, pr, j, :], True, True,
                           ["kdec%dp%d" % (j, pr), "vbf%dp%d" % (j, pr)], ["UB"])
                        for g in range(2):
                            mm(UB[:, 256 + g * 128:256 + (g + 1) * 128], btok[r0:r0 + 64, pr, j, g * 128:(g + 1) * 128],
                               wx[r0:r0 + 64, pr, j, g * 128:(g + 1) * 128], True, True,
                               ["btok%dp%d" % (j, pr), "wx%dp%d" % (j, pr)], ["UB"])
                        tt("dve", tmpS[:], UB[:, 0:256], bdmask, ALU.mult, ["UB", "kp"], ["tmpS"] + PSR("UB"))
                        stt(S[:], S[:], gdec[:, pr, c:c + 1], tmpS[:], ALU.mult, ALU.add, ["S", "gdec%dp%d" % (j, pr), "tmpS"], ["S"])
                        cp_any("act", Sb[:], S[:], ["S"], ["Sb"])
                        tt("pool" if MV else "dve", tmpH[:].rearrange("p (a b) -> p a b", a=4), H[:].rearrange("p (a b) -> p a b", a=4),
                           sdec[:, pr, j, cc * 4:(cc + 1) * 4].unsqueeze(2).to_broadcast([128, 4, 64]), ALU.mult,
                           ["H", "sdec%dp%d" % (j, pr)], ["tmpH"])
                        tt("dve", H[:], tmpH[:], UB[:, 256:512], ALU.add, ["tmpH", "UB"], ["H"] + PSR("UB"))
                        cp_any("act", Hb[:], H[:], ["H"], ["Hb"])
                        yield
                        for i in range(2):
                            mm(OT[:, i, r0:r0 + 64], Sb[:, i * 128:(i + 1) * 128], qbf[:, pr, c * 64:(c + 1) * 64], True, True,
                               ["Sb", "qbfp%d" % pr], ["B6"])
                        for g in range(2):
                            mm(YT[:, g, r0:r0 + 64], Hb[:, g * 128:(g + 1) * 128], cbf[:, pr, g, c * 64:(c + 1) * 64], True, True,
                               ["Hb", "cbf%dp%d" % (g, pr)], ["B6"])
                        yield
                    js = slice(j * 128, (j + 1) * 128)
                    for i in range(2):
                        act(sq[:, i, js], OT[:, i, :], AF.Square, ["B6"], ["sq"] + PSR("B6"))
                        act(osb[:, i, js], OT[:, i, :], AF.Copy, ["B6"], ["osb"] + PSR("B6"))
                    for g in range(2):
                        stt(ysb[:, g, 0:128], xs[:, pr, g, js], cp[:, 39 + g:40 + g], YT[:, g, :], ALU.mult, ALU.add,
                            ["xs%dp%d" % (g, pr), "cp", "B6"], ["ysb%d" % g] + PSR("B6"))
                    yield
                    for i in range(2):
                        mm(OT[:, i, :], blk64, sq[:, i, js], True, True, ["kp", "sq"], ["B6"])
                    act(rs[:, 0:256], OT.rearrange("p a b -> p (a b)"), AF.Ln, ["B6"], ["rs"] + PSR("B6"), scale=1.0 / 64, bias=EPS)
                    act(rs[:, 0:256], rs[:, 0:256], AF.Exp, ["rs"], ["rs"], scale=-0.5)
                    for g in range(2):
                        tt("pool", yz[:, g, js], ysb[:, g, 0:128], szs[:, pr, g, js], ALU.mult,
                           ["ysb%d" % g, "szs%dp%d" % (g, pr)], ["yz"])
                        act(sq2[:, g, :], yz[:, g, js], AF.Square, ["yz"], ["sq2"])
                    yield
                    for i in range(2):
                        stt(t1[:, i, :], osb[:, i, js], cp[:, 36:37], rs[:, i * 128:(i + 1) * 128], ALU.mult, ALU.mult,
                            ["osb", "cp", "rs"], ["t1"])
                        tt("pool", mixT[:, 2 + i, js], t1[:, i, :], szg[:, pr, i, js], ALU.mult,
                           ["t1", "szg%dp%d" % (i, pr)], ["mix%d" % (2 + i)])
                    for g in range(2):
                        mm(YT[:, 0, :], ones[:], sq2[:, g, :], g == 0, g == 1, ["ones", "sq2"], ["B6"])
                    act(rs2[:], YT[:, 0, :], AF.Ln, ["B6"], ["rs2"] + PSR("B6"), scale=1.0 / 256, bias=EPS)
                    act(rs2[:], rs2[:], AF.Exp, ["rs2"], ["rs2"], scale=-0.5)
                    for g in range(2):
                        stt(mixT[:, 6 + g, js], yz[:, g, js], cp[:, 41 + g:42 + g], rs2[:], ALU.mult, ALU.mult,
                            ["yz", "cp", "rs2"], ["mix%d" % (6 + g)])
                    yield

            def post(n):
                s = n % 3
                xk = "x%d" % s
                X = xt[:, s]
                mixk = ["mix%d" % i for i in range(8)]
                for j in range(2):
                    for hf in range(2):
                        bt, bk = next_bank()
                        for kc in range(8):
                            mm(bt[:, :], mixT[:, kc, j * 128:(j + 1) * 128], woutbf[:, kc, hf * 512:(hf + 1) * 512],
                               kc == 0, kc == 7, mixk + ["woutbf"], [bk])
                        tt("dve", X[:, j, hf * 512:(hf + 1) * 512], bt[:, :], X[:, j, hf * 512:(hf + 1) * 512], ALU.add,
                           [bk, xk], [xk] + PSR(bk))
                if last:
                    for j in range(2):
                        act(mixT[:, 4 * j:4 * j + 4, :].rearrange("p a b -> p (a b)"), X[:, j, :], AF.Square, [xk],
                            ["st_fs%d" % j] + ["mix%d" % (4 * j + q) for q in range(4)], accum_out=st[:, 8 + j:9 + j])
                    ts("dve", st[:, 10:12], st[:, 8:10], 1.0 / D, EPS, ALU.mult, ALU.add, ["st_fs0", "st_fs1"], ["st_fm"])
                    tt("pool", st[:, 14:16], st[:, 10:12], mhalf[:], ALU.pow, ["st_fm", "mhalf"], ["st_fr"])
                    for j in range(2):
                        stt(X[:, j, :], X[:, j, :], st[:, 14 + j:15 + j], fnw[:], ALU.mult, ALU.mult,
                            [xk, "st_fr", "fnw"], [xk])
                dma(dst[n * T:(n + 1) * T, :].rearrange("(j p) d -> p j d", p=128), X, [xk], ["dst%d_%d" % (L, n)], "st%d" % s)
                if last:
                    P.final_wait("sp", ["dst%d_%d" % (L, n)])
                if n + 3 < NT:
                    load_x(n + 3)

            def drain(g):
                P.tag = g.__name__ if hasattr(g, "__name__") else ""
                for _ in g:
                    pass
                P.tag = ""


            def interleave(ga, na, gb, nb):
                mode = os.environ.get("ILV", "")
                if mode == "ab":
                    drain(ga); drain(gb); return
                if mode == "ba":
                    drain(gb); drain(ga); return
                a_done = b_done = 0
                a_live = b_live = True
                while a_live or b_live:
                    if a_live:
                        try:
                            next(ga)
                            a_done += 1
                        except StopIteration:
                            a_live = False
                    while b_live and (not a_live or b_done * na < a_done * nb):
                        try:
                            next(gb)
                            b_done += 1
                        except StopIteration:
                            b_live = False

            def chain2(*gens):
                for g in gens:
                    yield from g

            for n_ in range(1, min(3, NT)):
                load_x(n_)
            def silu_gen(n):
                silu_batch(n)
                yield

            drain(pre(0))
            silu_batch(0)
            drain(pro(0))
            for n in range(NT):
                drain(rec(n))
                drain(big(n))
                if n + 1 < NT:
                    drain(pre(n + 1))
                    P.tag = 'silu'
                    silu_batch(n + 1)
                    drain(pro(n + 1))
                P.tag = 'post'
                post(n)
        ms = P.schedule()
        print("[sched] ops=%d est_makespan_us=%.1f hist=%s" % (len(P.ops), ms, P.hist))
        P.emit(sems, block)
    return nc


def _kpack():
    kp = np.zeros((128, NK), np.float32)
    p = np.arange(128)
    kp[:, 0:128] = np.eye(128, dtype=np.float32)
    s = p[:, None]
    t = p[None, :]
    kp[:, 128:256] = ((s // 64 == t // 64) & (s > t)).astype(np.float32)
    kp[:, 256:258] = (s // 64 == np.arange(2)[None, :]).astype(np.float32)
    kp[:, 258:386] = (s // 64 == t // 64).astype(np.float32)
    col = np.arange(256)[None, :]
    kp[:, 386:642] = (s // 32 == col // 64).astype(np.float32)
    tt_ = np.arange(T)
    for i in range(2):
        win = np.array([POOL_WINDOWS[(i * 128 + pp) // 64] for pp in p], np.float32)
        kp[:, 642 + i * T:642 + (i + 1) * T] = 1.0 / np.minimum(tt_[None, :] + 1.0, win[:, None])
        kp[:, 1154 + i * T:1154 + (i + 1) * T] = (1.0 / win)[:, None]
    return kp


def _pack_layers(inp, NL):
    p = np.arange(128)
    cpk = np.zeros((NL, 128, 43), np.float32)
    bpk = np.zeros((NL, 128, 1160), np.float32)
    pwk = np.zeros((NL, 128, 256), np.float32)
    for L in range(NL):
        for i in range(2):
            for k in range(3):
                cpk[L, :, i * 3 + k] = inp["conv_a_w"][L, k, i * 128:(i + 1) * 128]
        for c in range(6):
            for k in range(4):
                cpk[L, :, 6 + c * 4 + k] = inp["ssd_conv_w"][L, k, c * 128:(c + 1) * 128]
            cpk[L, :, 30 + c] = inp["ssd_conv_b"][L, c * 128:(c + 1) * 128]
        cpk[L, :, 36] = inp["gla_norm_w"][L][p % 64]
        for i in range(2):
            cpk[L, :, 37 + i] = inp["pool_scale"][L, i * 128:(i + 1) * 128]
            cpk[L, :, 39 + i] = inp["ssd_d"][L][i * 2 + p // 64]
            cpk[L, :, 41 + i] = inp["ssd_norm_w"][L, i * 128:(i + 1) * 128]
        bpk[L, :, 0:1024] = inp["norm_w"][L][None, :]
        bpk[L, :, 1024:1152] = inp["gla_gate_b"][L][None, :]
        bpk[L, :, 1152:1156] = inp["ssd_dt_bias"][L][None, :]
        bpk[L, :, 1156:1160] = inp["ssd_a_log"][L][None, :]
        for i in range(2):
            for gl in range(2):
                pwk[L, gl * 64:(gl + 1) * 64, i * 128 + gl * 64:i * 128 + (gl + 1) * 64] = inp["pool_w"][L, 2 * i + gl]
    return cpk, bpk, pwk


def make_in_maps(inp, NL, ncores=8):
    inp = {k: np.asarray(v, np.float32) for k, v in inp.items()}
    B = inp["x"].shape[0]
    cpk, bpk, pwk = _pack_layers(inp, NL)
    kp = _kpack()
    fnw = np.ascontiguousarray(np.broadcast_to(inp["final_norm_w"][None, :], (128, D))).astype(np.float32)
    w_in = inp["w_in"][:NL]
    w_in = np.ascontiguousarray(np.concatenate([w_in[:, :, :1536], w_in[:, :, 3344:3348], w_in[:, :, 1536:3344]], axis=2))
    common = {"w_in": w_in, "w_out": np.ascontiguousarray(inp["w_out"][:NL]),
              "cpack": cpk, "bpack": bpk, "gatew": np.ascontiguousarray(inp["gla_gate_w"][:NL]),
              "poolw": pwk, "fnw": fnw, "kpack": kp}
    maps = []
    for c in range(ncores):
        m = dict(common)
        m["x"] = np.ascontiguousarray(inp["x"][c % B])
        maps.append(m)
    return maps


_NC_CACHE = {}


def kernel(**inputs):
    x = np.asarray(inputs["x"])
    B, SEQ, _ = x.shape
    NL = int(np.asarray(inputs["w_in"]).shape[0])
    key = (SEQ, NL)
    if key not in _NC_CACHE:
        _NC_CACHE[key] = build_nc(SEQ, NL)
    nc = _NC_CACHE[key]
    maps = make_in_maps(inputs, NL, 8)
    res = run_bass_kernel_spmd(nc, maps, core_ids=list(range(8)))
    out = np.stack([np.asarray(res.results[b]["y"], dtype=np.float32) for b in range(B)], axis=0)
    return out
```

```python
import contextlib
import os
import numpy as np
import concourse.bass as bass
import concourse.mybir as mybir
from concourse.bass_utils import run_bass_kernel_spmd

F32 = mybir.dt.float32
BF16 = mybir.dt.bfloat16
AF = mybir.ActivationFunctionType
ALU = mybir.AluOpType

D = 1024
DP = 3348
T = 256
EPS = 1e-6
POOL_WINDOWS = (2, 4, 8, 16)
C_AH, C_AB, C_AC, C_AZ, C_Q, C_K, C_V, C_DT, C_GLR, C_GZ, C_PU, C_PZ, C_SZ, C_XBC = (
    0, 256, 512, 768, 1024, 1152, 1280, 1536, 1540, 1556, 1812, 2068, 2324, 2580)
NK = 1666
ENGS = ("pe", "dve", "act", "pool", "sp")


class Prog:
    LAT = 0.2

    def __init__(self, nc):
        self.nc = nc
        self.ops = []

    tag = ""

    XSPLIT = {"x1": ["x1a", "x1b"], "x2": ["x2a", "x2b"]}

    def _ex(self, keys):
        out = []
        for k in keys:
            out.extend(self.XSPLIT.get(k, [k]))
        return out

    def op(self, eng, fn, reads=(), writes=(), dma_sem=None, signal=True, dur=0.3, aset=0):
        reads = self._ex(reads)
        writes = self._ex(writes)
        self.ops.append(dict(eng=eng, fns=[fn] if fn is not None else [], reads=list(reads), writes=list(writes),
                             dma_sem=dma_sem, signal=signal, dur=dur, aset=aset, tag=self.tag,
                             nm=",".join(writes[:2]) if writes else ""))

    def final_wait(self, eng, keys):
        self.op(eng, None, reads=keys, dur=0.0)

    def _merge_groups(self):
        out = []
        cur = None
        for o in self.ops:
            if cur is not None:
                assert o["eng"] == "pe", "unsignalled PE group interrupted"
                cur["fns"] += o["fns"]
                cur["reads"] += o["reads"]
                cur["writes"] += o["writes"]
                cur["dur"] += o["dur"]
                if o["signal"]:
                    cur["signal"] = True
                    out.append(cur)
                    cur = None
                continue
            if o["eng"] == "pe" and not o["signal"]:
                cur = dict(o)
                cur["fns"] = list(o["fns"])
                cur["reads"] = list(o["reads"])
                cur["writes"] = list(o["writes"])
                continue
            out.append(o)
        assert cur is None
        self.ops = out

    def schedule(self):
        self._merge_groups()
        ops = self.ops
        N = len(ops)
        lastw = {}
        readers = {}
        deps = [None] * N
        sdeps = []
        last_sp = None
        for i, o in enumerate(ops):
            d = set()
            for k in o["reads"]:
                if k in lastw:
                    d.add(lastw[k])
            for k in o["writes"]:
                if k in lastw:
                    d.add(lastw[k])
                d.update(readers.get(k, ()))
            d.discard(i)
            deps[i] = sorted(d)
            sd = set(d)
            if o["eng"] == "sp":
                if last_sp is not None:
                    sd.add(last_sp)
                last_sp = i
            sdeps.append(sorted(sd))
            for k in o["writes"]:
                lastw[k] = i
                readers[k] = []
            for k in o["reads"]:
                readers.setdefault(k, []).append(i)
        children = [[] for _ in range(N)]
        indeg = [0] * N
        for i in range(N):
            indeg[i] = len(sdeps[i])
            for d in sdeps[i]:
                children[d].append(i)
        prio = [0.0] * N
        for i in range(N - 1, -1, -1):
            m = 0.0
            for c in children[i]:
                if prio[c] > m:
                    m = prio[c]
            prio[i] = m + ops[i]["dur"] + 0.05
        def run(pr_):
            indeg_ = list(indeg)
            limdep = [None] * N
            blockers = []
            eng_free = {e: 0.0 for e in ENGS}
            act_set = [0]
            finish = [0.0] * N
            ready = [i for i in range(N) if indeg_[i] == 0]
            order = []
            LAT = self.LAT
            SAME = {'dve': 0.13, 'act': 0.10, 'pool': 0.17} if os.environ.get('K_SAME', '1') == '1' else {}
            WIN = float(os.environ.get('K_WIN', '0.15'))
            STAT = {} if os.environ.get('K_STAT') else None
            TL = [] if os.environ.get('K_TL') else None
            self.tl = TL
            self.stat = STAT
            HM = int(os.environ.get('K_H', '1'))
            ALPHA = float(os.environ.get('K_ALPHA', '0.01'))
            BETA = float(os.environ.get('K_BETA', '0.01'))
            while ready:
                best = None
                best_key = None
                cand = []
                for i in ready:
                    o = ops[i]
                    e = o["eng"]
                    t = eng_free[e]
                    for d in deps[i]:
                        f = finish[d] + (LAT if ops[d]["eng"] != e or ops[d]["dma_sem"] else SAME.get(e, 0.0))
                        if f > t:
                            t = f
                    if e == "act" and o["aset"] and o["aset"] != act_set[0]:
                        t += 1.3
                    cand.append((t, i))
                tmin = min(c[0] for c in cand)
                for t, i in cand:
                    if t <= tmin + WIN:
                        if HM == 1:
                            key = (-pr_[i], t)
                        elif HM == 2:
                            key = (i, t)
                        elif HM == 3:
                            key = (t - ALPHA * pr_[i], i)
                        else:
                            key = (-(pr_[i] - BETA * i), t)
                        if best_key is None or key < best_key:
                            best_key = key
                            best = (t, i)
                t, i = best
                o = ops[i]
                e = o["eng"]
                lim_ = None
                lt_ = -1.0
                for d in deps[i]:
                    if finish[d] > lt_:
                        lt_, lim_ = finish[d], d
                limdep[i] = lim_
                if e == "pe" and lim_ is not None and t - eng_free[e] > 0.1 and lt_ + 0.3 > t:
                    blockers.append((t - eng_free[e], i, lim_))
                if STAT is not None and e in ("pe", "dve", "act"):
                    idle = t - eng_free[e]
                    if idle > 0.05:
                        lim = None
                        lt = -1
                        for d in deps[i]:
                            f = finish[d]
                            if f > lt:
                                lt, lim = f, d
                        kk = (e, ops[lim]["eng"] if lim is not None else "-", o.get("tag", ""), ops[lim].get("tag", "") if lim is not None else "")
                        STAT[kk] = STAT.get(kk, 0.0) + idle
                if e == "act" and o["aset"]:
                    act_set[0] = o["aset"]
                if o["dma_sem"]:
                    eng_free[e] = t + 0.05
                    finish[i] = t + o["dur"]
                else:
                    eng_free[e] = t + o["dur"]
                    finish[i] = t + o["dur"]
                order.append(i)
                if TL is not None:
                    TL.append((t, e, o["dur"], o.get("tag", ""), o.get("nm", ""), i))
                ready.remove(i)
                for c in children[i]:
                    indeg_[c] -= 1
                    if indeg_[c] == 0:
                        ready.append(c)

            return (max(finish) if N else 0.0), order, blockers, limdep
        ITER = int(os.environ.get('K_ITER', '30'))
        GAIN = float(os.environ.get('K_GAIN', '1.0'))
        boost = [0.0] * N
        best = None
        hist = []
        for it in range(ITER):
            pr_ = [prio[i] + boost[i] for i in range(N)]
            ms_, ord_, blk_, lim_ = run(pr_)
            hist.append(round(ms_, 1))
            if best is None or ms_ < best[0]:
                best = (ms_, ord_)
            for (idle, i, d) in blk_:
                k = 0
                while d is not None and k < 12:
                    boost[d] += idle * GAIN
                    d = lim_[d]
                    k += 1
        self.hist = hist
        order = best[1]
        finish = [best[0]]
        assert len(order) == N
        self.order = order
        self.deps = deps
        self.makespan = max(finish) if N else 0.0
        return self.makespan

    def emit(self, sems, block):
        ops = self.ops
        cnt = {}
        ticket = {}
        queues = {e: [] for e in ENGS}
        seen = {e: {} for e in ENGS}
        for i in self.order:
            o = ops[i]
            e = o["eng"]
            waits = {}
            for d in self.deps[i]:
                od = ops[d]
                if od["eng"] == "pe" and e == "pe" and not od["dma_sem"]:
                    continue
                if not od["fns"]:
                    continue
                sk, v = ticket[d]
                if seen[e].get(sk, 0) >= v:
                    continue
                waits[sk] = max(waits.get(sk, 0), v)
            for sk, v in waits.items():
                seen[e][sk] = v
            inc = None
            if o["fns"]:
                if o["dma_sem"]:
                    sk = o["dma_sem"]
                    cnt[sk] = cnt.get(sk, 0) + 16
                    inc = (sk, 16)
                else:
                    sk = e
                    cnt[sk] = cnt.get(sk, 0) + 1
                    inc = (sk, 1)
                ticket[i] = (sk, cnt[sk])
            queues[e].append((list(waits.items()), o["fns"], inc))
        engmap = {"pe": "tensor", "dve": "vector", "act": "scalar", "pool": "gpsimd", "sp": "sync"}

        def make(ename):
            items = queues[ename]

            def body(eng):
                for waits, fns, inc in items:
                    for sk, v in waits:
                        eng.wait_ge(sems[sk], v)
                    ins = None
                    for fn in fns:
                        ins = fn(eng)
                    if ins is not None and inc is not None:
                        ins.then_inc(sems[inc[0]], inc[1])
            return body

        for ename in ENGS:
            if queues[ename]:
                getattr(block, engmap[ename])(make(ename))


def build_nc(SEQ, NL):
    NT = SEQ // T
    nc = bass.Bass("TRN2", target_bir_lowering=False)
    dt_ = nc.dram_tensor
    x_d = dt_("x", [SEQ, D], F32, kind="ExternalInput").ap()
    win_d = dt_("w_in", [NL, D, DP], F32, kind="ExternalInput").ap()
    wout_d = dt_("w_out", [NL, D, D], F32, kind="ExternalInput").ap()
    cp_d = dt_("cpack", [NL, 128, 43], F32, kind="ExternalInput").ap()
    bp_d = dt_("bpack", [NL, 128, 1160], F32, kind="ExternalInput").ap()
    gw_d = dt_("gatew", [NL, 16, 128], F32, kind="ExternalInput").ap()
    pw_d = dt_("poolw", [NL, 128, 256], F32, kind="ExternalInput").ap()
    fnw_d = dt_("fnw", [128, D], F32, kind="ExternalInput").ap()
    kp_d = dt_("kpack", [128, NK], F32, kind="ExternalInput").ap()
    y_d = dt_("y", [SEQ, D], F32, kind="ExternalOutput").ap()
    scr_d = dt_("scr", [SEQ, D], F32, kind="Internal").ap() if NL > 1 else None

    with contextlib.ExitStack() as es:
        def sb(name, shape, dt):
            return es.enter_context(nc.sbuf_tensor(name, shape, dt))

        def ps(name, shape, dt):
            return es.enter_context(nc.psum_tensor(name, shape, dt))

        wbf = sb("wbf", [128, 8, DP], BF16)
        woutbf = sb("woutbf", [128, 8, D], BF16)
        kp = sb("kp", [128, NK], F32)
        cp = sb("cp", [128, 43], F32)
        bp = sb("bp", [128, 1160], F32)
        fnw = sb("fnw_sb", [128, D], F32)
        gwf = sb("gwf", [16, 128], F32)
        gwb = sb("gwb", [16, 128], BF16)
        pwf = sb("pwf", [128, 256], F32)
        pwb = sb("pwb", [128, 2, 128], BF16)
        identb = sb("identb", [128, 128], BF16)
        ones = sb("ones", [128, 128], F32)
        abc = sb("abc", [128, 4], F32)
        xt = sb("xt", [128, 3, 2, D], F32)
        mhalf = sb("mhalf", [128, 2], F32)
        cvc = sb("cvc", [128, 2, T], F32)
        dtx = sb("dtx", [128, 2, 2, 4], F32)
        tmpS = sb("tmpS", [128, 256], F32)
        tmpH = sb("tmpH", [128, 256], F32)
        sq2 = sb("sq2", [128, 2, 128], F32)
        rs2 = sb("rs2", [128, 128], F32)
        hbf = sb("hbf", [128, 2, D], BF16)
        hT = sb("hT", [128, 2, 8, T], BF16)
        st = sb("st", [128, 16], F32)
        ah = sb("ah", [128, 2, T], F32)
        ubuf = sb("ubuf", [128, 2, T + 2], F32)
        cva = sb("cva", [128, 2, T], F32)
        yab = sb("yab", [128, 2, T], F32)
        sza = sb("sza", [128, 2, 2, T], F32)
        szg = sb("szg", [128, 2, 2, T], F32)
        szp = sb("szp", [128, 2, 2, T], F32)
        szs = sb("szs", [128, 2, 2, T], F32)
        qbf = sb("qbf", [128, 2, T], BF16)
        glr = sb("glr", [16, 2, T], BF16)
        pbuf = sb("pbuf", [128, 2, T + 16], F32)
        bA = sb("bA", [128, 2, T + 16], F32)
        bB = sb("bB", [128, 2, T + 16], F32)
        pl = sb("pl", [128, 2, T], F32)
        pooled = sb("pooled", [128, 2, T], BF16)
        xbuf = sb("xbuf", [128, 6, T + 3], F32)
        xs = sb("xs", [128, 2, 2, T], F32)
        bfm = sb("bfm", [128, 2, 2, T], F32)
        cbf = sb("cbf", [128, 2, 2, T], BF16)
        ksb = sb("ksb", [128, 2, 2, 128], F32)
        vbf = sb("vbf", [128, 2, 2, 256], BF16)
        gx = sb("gx", [128, 128], F32)
        spl = sb("spl", [128, 2, 128], F32)
        Eg = sb("Eg", [128, 128], F32)
        kdec = sb("kdec", [128, 2, 2, 128], BF16)
        gdec = sb("gdec", [128, 2, 4], F32)
        dts = sb("dts", [128, 2, 32], F32)
        sdec = sb("sdec", [128, 2, 2, 8], F32)
        wx = sb("wx", [128, 2, 2, 256], BF16)
        btok = sb("btok", [128, 2, 2, 256], BF16)
        S = sb("S", [128, 256], F32)
        Sb = sb("Sb", [128, 256], BF16)
        H = sb("H", [128, 256], F32)
        Hb = sb("Hb", [128, 256], BF16)
        osb = sb("osb", [128, 2, T], F32)
        sq = sb("sq", [128, 2, T], F32)
        rs = sb("rs", [128, T], F32)
        t1 = sb("t1", [128, 2, 128], F32)
        ysb = sb("ysb", [128, 2, 128], F32)
        yz = sb("yz", [128, 2, T], F32)
        mixT = sb("mixT", [128, 8, T], BF16)

        PA = ps("PA", [128, 512], F32)
        PB = ps("PB", [128, 512], F32)
        PC = ps("PC", [128, 512], F32)
        KV = ps("KV", [128, 512], F32)
        SM = ps("SM", [128, 512], F32)
        TRf = SM
        UB = ps("UB", [128, 512], F32)
        TRb = ps("TRb", [128, 2, 2, 128], BF16)
        OY = ps("OY", [128, 4, 128], F32)
        OT = OY[:, 0:2]
        YT = OY[:, 2:4]

        sem_names = list(ENGS) + ["ldx0", "ldx1", "ldx2", "st0", "st1", "st2", "lw0", "lw1", "lw2", "lw3", "lc"] + ["lk%d" % i for i in range(8)]
        sems = {k: es.enter_context(nc.semaphore(k)) for k in sem_names}
        block = es.enter_context(nc.Block())
        P = Prog(nc)
        op = P.op

        def PSR(bank):
            return [getattr(bank, "lock", bank) + ".lk"]

        PECOL = float(os.environ.get("K_PECOL", "0.00046"))
        ASET = {AF.Silu: 18, AF.Exp: 6, AF.Ln: 6, AF.Sqrt: 3}

        def act(out, in_, func, reads, writes, **kw):
            n = out.free_size()
            dur = 0.2 + n * 0.0009 + (0.1 if "accum_out" in kw else 0.0)
            return op("act", lambda e: e.activation(out=out, in_=in_, func=func, **kw), reads=reads, writes=writes,
                      dur=dur, aset=ASET.get(func, 0))

        def edur(eng, n):
            if eng == "pool":
                return 0.1 + n * 0.0022
            return 0.07 + n * 0.0014

        def cp_any(eng, out, in_, reads, writes):
            if eng == "act":
                return act(out, in_, AF.Copy, reads, writes)
            return op(eng, lambda e: e.tensor_copy(out=out, in_=in_), reads=reads, writes=writes, dur=edur(eng, out.free_size()))

        def tt(eng, out, in0, in1, o, reads, writes):
            return op(eng, lambda e: e.tensor_tensor(out=out, in0=in0, in1=in1, op=o), reads=reads, writes=writes,
                      dur=edur(eng, out.free_size()))

        def stt(out, in0, scalar, in1, o0, o1, reads, writes):
            return op("dve", lambda e: e.scalar_tensor_tensor(out=out, in0=in0, scalar=scalar, in1=in1, op0=o0, op1=o1),
                      reads=reads, writes=writes, dur=edur("dve", out.free_size()))

        def ts(eng, out, in0, s1, s2, o0, o1, reads, writes):
            d_ = edur(eng, out.free_size())
            if o1 is None:
                return op(eng, lambda e: e.tensor_scalar(out=out, in0=in0, scalar1=s1, scalar2=None, op0=o0),
                          reads=reads, writes=writes, dur=d_)
            return op(eng, lambda e: e.tensor_scalar(out=out, in0=in0, scalar1=s1, scalar2=s2, op0=o0, op1=o1),
                      reads=reads, writes=writes, dur=d_)

        def mm(out, lhsT, rhs, start, stop, reads, writes):
            n = max(64, rhs.free_size())
            d_ = n * PECOL * (2.5 if lhsT.dtype == F32 else 1) + 0.01
            return op("pe", lambda e: e.matmul(out, lhsT=lhsT, rhs=rhs, start=start, stop=stop),
                      reads=reads, writes=writes, signal=stop, dur=d_)

        def tr(out, in_, ident, reads, writes, signal):
            return op("pe", lambda e: e.transpose(out=out, in_=in_, identity=ident), reads=reads, writes=writes, signal=signal,
                      dur=0.12)

        def wkeys(col0, ncols):
            return ["w%d_%d" % (kc, hf) for hf in range(col0 // 1024, (col0 + ncols - 1) // 1024 + 1) for kc in range(8)]

        def dma(out, in_, reads, writes, sem):
            nbytes = out.free_size() * out.partition_size() * 4
            return op("sp", lambda e: e.dma_start(out=out, in_=in_), reads=reads, writes=writes, dma_sem=sem,
                      dur=1.5 + nbytes / 1.2e5)

        dma(kp[:], kp_d[:, :], [], ["kp"], "lk0")
        dma(fnw[:], fnw_d[:, :], [], ["fnw"], "lk1")
        cp_any("dve", identb[:], kp[:, 0:128], ["kp"], ["identb"])
        op("pool", lambda e: e.memset(ones[:], 1.0), writes=["ones"])
        op("pool", lambda e: e.memset(mhalf[:], -0.5), writes=["mhalf"])
        identf = kp[:, 0:128]
        mstrict = kp[:, 128:256]
        ind = kp[:, 256:258]
        blk64 = kp[:, 258:386]
        bdmask = kp[:, 386:642]

        rr = {"i": 0}

        def evac_eng():
            rr["i"] += 1
            return ("act", "dve")[rr["i"] % 2]

        for L in range(NL):
            last = (L == NL - 1)
            src = x_d if L == 0 else scr_d
            dst = y_d if last else scr_d
            dma(cp[:], cp_d[L], [], ["cp"], "lk2")
            dma(bp[:], bp_d[L], [], ["bp"], "lk3")
            dma(gwf[:], gw_d[L], [], ["gwf"], "lk4")
            dma(pwf[:], pw_d[L], [], ["pwf"], "lk5")
            cp_any("dve", gwb[:], gwf[:], ["gwf"], ["gwb"])
            cp_any("dve", pwb[:].rearrange("p a b -> p (a b)"), pwf[:], ["pwf"], ["pwb"])
            act(abc[:], bp[:, 1156:1160], AF.Exp, ["bp"], ["abc"])
            ts("dve", abc[:], abc[:], -1.0, None, ALU.mult, None, ["abc"], ["abc"])
            dma(xt[:, 0], (x_d if L == 0 else scr_d)[0:T, :].rearrange("(j p) d -> p j d", p=128),
                ["dst%d_%d" % (L - 1, 0)] if L > 0 else [], ["x0"], "ldx0")
            k = 0
            WR = [(0, 1024), (1024, 2048), (2048, 3072), (3072, DP)]

            def stage(k_):
                q = k_ % 4
                return xt[:, 1 + q // 2].rearrange("p j d -> p (j d)")[:, (q % 2) * 1024:(q % 2 + 1) * 1024], \
                    "x%d%s" % (1 + q // 2, "ab"[q % 2]), "lw%d" % q

            for r in (2, 3, 1, 0):
                c0, c1 = WR[r]
                for kc in range(8):
                    stg, sk, sem_ = stage(k)
                    dma(stg[:, 0:c1 - c0], win_d[L, kc * 128:(kc + 1) * 128, c0:c1], [], [sk], sem_)
                    cp_any(("act", "dve", "pool")[k % 3], wbf[:, kc, c0:c1], stg[:, 0:c1 - c0], [sk], ["w%d_%d" % (kc, r)])
                    k += 1
            for kc in range(8):
                stg, sk, sem_ = stage(k)
                dma(stg[:, 0:D], wout_d[L, kc * 128:(kc + 1) * 128, :], [], [sk], sem_)
                cp_any(("act", "dve", "pool")[k % 3], woutbf[:, kc, :], stg[:, 0:D], [sk], ["woutbf"])
                k += 1
            op("dve", lambda e: e.memset(S[:], 0.0), writes=["S"])
            op("dve", lambda e: e.memset(H[:], 0.0), writes=["H"])
            op("pool", lambda e: e.memset(ubuf[:, :, 0:2], 0.0), writes=["ubuf0", "ubuf1"])
            op("pool", lambda e: e.memset(pbuf[:, :, 0:16], 0.0), writes=["pbuf0", "pbuf1"])
            op("pool", lambda e: e.memset(xbuf[:, :, 0:3], 0.0), writes=["xbuf%d" % c for c in range(6)])

            normw = bp[:, 0:1024]
            gateb = bp[:, 1024:1152]
            dtb = bp[:, 1152:1156]

            def load_x(n_):
                s_ = n_ % 3
                rd = ["dst%d_%d" % (L - 1, n_)] if L > 0 else []
                dma(xt[:, s_], src[n_ * T:(n_ + 1) * T, :].rearrange("(j p) d -> p j d", p=128), rd, ["x%d" % s_], "ldx%d" % s_)

            bank = {"i": 0}
            zc = {"i": 0}
            ZE = os.environ.get("K_ZE", "act,act,act,dve").split(",")
            MV = os.environ.get("K_MV", "0") == "1"
            hpar = {"p": 0}

            NB = int(os.environ.get("K_NB", "3"))
            BANKS = [(PA, "PA"), (PB, "PB"), (PC, "PC")][:NB]

            def next_bank():
                b = BANKS[bank["i"] % len(BANKS)]
                bank["i"] += 1
                return b

            def fm_tile(col0, ncols, evac):
                bt, bk = next_bank()
                hp = hpar["p"]
                for kc in range(8):
                    mm(bt[0:ncols, 0:T], wbf[:, kc, col0:col0 + ncols], hT[:, hp, kc, :], kc == 0, kc == 7,
                       wkeys(col0, ncols) + ["hT%d" % hp], [bk])
                evac(bt[0:ncols, 0:T], bk)

            def pre(n):
                s = n % 3
                pr = n % 2
                hpar["p"] = pr
                xk = "x%d" % s
                X = xt[:, s]
                for j in range(2):
                    act(hbf[:, j, :], X[:, j, :], AF.Square, [xk], ["st_ss%d" % j, "hbf%d" % j], accum_out=st[:, j:j + 1])
                ts("dve", st[:, 2:4], st[:, 0:2], 1.0 / D, EPS, ALU.mult, ALU.add, ["st_ss0", "st_ss1"], ["st_ms"])
                tt("pool", st[:, 6:8], st[:, 2:4], mhalf[:], ALU.pow, ["st_ms", "mhalf"], ["st_rstd"])
                yield
                for j in range(2):
                    stt(hbf[:, j, :], X[:, j, :], st[:, 6 + j:7 + j], normw, ALU.mult, ALU.mult,
                        [xk, "st_rstd", "bp"], ["hbf%d" % j])
                    yield
                for g in range(4):
                    for kk in range(2):
                        for j in range(2):
                            kc = 2 * g + kk
                            tr(TRb[:, kk, j, :], hbf[:, j, kc * 128:(kc + 1) * 128], identb[:],
                               ["hbf%d" % j, "identb"], ["TRb"], signal=(kk == 1 and j == 1))
                    cp_any(evac_eng(), hT[:, pr, 2 * g:2 * g + 2, :].rearrange("p a b -> p (a b)"),
                           TRb[:].rearrange("p a j b -> p (a j b)"), ["TRb"], ["hT%d" % pr] + PSR("TRb"))
                    yield
                for c in range(6):
                    def ev_x(p, bk, c=c):
                        xk_ = "xbuf%d" % c
                        act(xbuf[:, c, 3:T + 3], p, AF.Copy, [bk], [xk_] + PSR(bk))
                        if c < 2:
                            o_, ok_ = xs[:, pr, c, :], "xs%dp%d" % (c, pr)
                        elif c < 4:
                            o_, ok_ = bfm[:, pr, c - 2, :], "bfm%dp%d" % (c - 2, pr)
                        else:
                            o_, ok_ = cvc[:, c - 4, :], "cvc%d" % (c - 4)
                        ts("dve", o_, xbuf[:, c, 0:T], cp[:, 6 + c * 4:7 + c * 4], None, ALU.mult, None, [xk_, "cp"], [ok_])
                        for k_ in range(1, 4):
                            stt(o_, xbuf[:, c, k_:T + k_], cp[:, 6 + c * 4 + k_:7 + c * 4 + k_], o_, ALU.mult, ALU.add,
                                [xk_, "cp", ok_], [ok_])
                        cp_any("pool", xbuf[:, c, 0:3], xbuf[:, c, T:T + 3], [xk_], [xk_])
                    fm_tile(C_XBC + c * 128, 128, ev_x)
                    yield
                for (col, buf, nm) in ((C_GZ, szg, "szg"), (C_SZ, szs, "szs"), (C_AZ, sza, "sza"), (C_PZ, szp, "szp")):
                    for i in range(2):
                        fm_tile(col + i * 128, 128,
                                lambda p, bk, buf=buf, nm=nm, i=i: cp_any(ZE[(zc.__setitem__("i", zc["i"] + 1) or zc["i"]) % len(ZE)], buf[:, pr, i, :], p, [bk],
                                                                          ["%s%dp%d" % (nm, i, pr)] + PSR(bk)))
                        yield
                for j in range(2):
                    for kc in range(8):
                        mm(KV[:, 0:388], hT[:, pr, kc, j * 128:(j + 1) * 128], wbf[:, kc, C_K:C_K + 388], kc == 0, kc == 7,
                           ["hT%d" % pr] + wkeys(C_K, 388), ["KV"])
                    act(ksb[:, pr, j, :], KV[:, 0:128], AF.Copy, ["KV"], ["ksb%dp%d" % (j, pr)] + PSR("KV"))
                    act(vbf[:, pr, j, :], KV[:, 128:384], AF.Copy, ["KV"], ["vbf%dp%d" % (j, pr)] + PSR("KV"))
                    tt("dve", dtx[:, pr, j, :], KV[:, 384:388], dtb, ALU.add, ["KV", "bp"], ["dtx%dp%d" % (j, pr)] + PSR("KV"))
                    yield
                fm_tile(C_Q, 128, lambda p, bk: act(qbf[:, pr, :], p, AF.Copy, [bk], ["qbfp%d" % pr] + PSR(bk), scale=32.0 ** -0.5))
                yield
                fm_tile(C_GLR, 16, lambda p, bk: act(glr[:, pr, :], p, AF.Copy, [bk], ["glrp%d" % pr] + PSR(bk)))
                yield

            def silu_batch(n):
                pr = n % 2
                for c in range(2):
                    k_ = "xs%dp%d" % (c, pr)
                    act(xs[:, pr, c, :], xs[:, pr, c, :], AF.Silu, [k_, "cp"], [k_], bias=cp[:, 30 + c:31 + c])
                for c in range(2):
                    k_ = "bfm%dp%d" % (c, pr)
                    act(bfm[:, pr, c, :], bfm[:, pr, c, :], AF.Silu, [k_, "cp"], [k_], bias=cp[:, 32 + c:33 + c])
                for c in range(2):
                    act(cbf[:, pr, c, :], cvc[:, c, :], AF.Silu, ["cvc%d" % c, "cp"], ["cbf%dp%d" % (c, pr)], bias=cp[:, 34 + c:35 + c])
                for (buf, nm) in ((szg, "szg"), (szs, "szs"), (sza, "sza"), (szp, "szp")):
                    for i in range(2):
                        k_ = "%s%dp%d" % (nm, i, pr)
                        act(buf[:, pr, i, :], buf[:, pr, i, :], AF.Silu, [k_], [k_])

            def big(n):
                pr = n % 2
                hpar["p"] = pr
                for i in range(2):
                    fm_tile(C_AH + i * 128, 128, lambda p, bk, i=i: act(ah[:, i, :], p, AF.Copy, [bk], ["ah%d" % i] + PSR(bk)))
                    yield
                for i in range(2):
                    def ev_ac(p, bk, i=i):
                        tt("dve", ubuf[:, i, 2:T + 2], p, ah[:, i, :], ALU.mult, [bk, "ah%d" % i], ["ubuf%d" % i] + PSR(bk))
                        uk = ["ubuf%d" % i, "cp"]
                        ts("dve", cva[:, i, :], ubuf[:, i, 0:T], cp[:, i * 3:i * 3 + 1], None, ALU.mult, None, uk, ["cva%d" % i])
                        stt(cva[:, i, :], ubuf[:, i, 1:T + 1], cp[:, i * 3 + 1:i * 3 + 2], cva[:, i, :], ALU.mult, ALU.add,
                            uk + ["cva%d" % i], ["cva%d" % i])
                        stt(cva[:, i, :], ubuf[:, i, 2:T + 2], cp[:, i * 3 + 2:i * 3 + 3], cva[:, i, :], ALU.mult, ALU.add,
                            uk + ["cva%d" % i], ["cva%d" % i])
                        cp_any("pool", ubuf[:, i, 0:2], ubuf[:, i, T:T + 2], ["ubuf%d" % i], ["ubuf%d" % i])
                    fm_tile(C_AC + i * 128, 128, ev_ac)
                    yield
                for i in range(2):
                    def ev_ab(p, bk, i=i):
                        tt("dve", yab[:, i, :], p, cva[:, i, :], ALU.mult, [bk, "cva%d" % i], ["yab%d" % i] + PSR(bk))
                        tt("pool", mixT[:, i, :], yab[:, i, :], sza[:, pr, i, :], ALU.mult,
                           ["yab%d" % i, "sza%dp%d" % (i, pr)], ["mix%d" % i])
                    fm_tile(C_AB + i * 128, 128, ev_ab)
                    yield
                for i in range(2):
                    fm_tile(C_PU + i * 128, 128, lambda p, bk, i=i: act(pbuf[:, i, 16:T + 16], p, AF.Copy, [bk],
                                                                       ["pbuf%d" % i] + PSR(bk)))
                    yield
                invc = kp[:, 642:1154] if n == 0 else kp[:, 1154:1666]
                tt("pool", bA[:, 0, 14:T + 16], pbuf[:, 0, 14:T + 16], pbuf[:, 0, 13:T + 15], ALU.add, ["pbuf0"], ["bA0"])
                tt("pool", bB[64:128, 0, 16:T + 16], bA[64:128, 0, 16:T + 16], bA[64:128, 0, 14:T + 14], ALU.add, ["bA0"], ["bB0"])
                tt("pool", pl[0:64, 0, :], bA[0:64, 0, 16:T + 16], invc[0:64, 0:T], ALU.mult, ["bA0", "kp"], ["pl0"])
                tt("pool", pl[64:128, 0, :], bB[64:128, 0, 16:T + 16], invc[64:128, 0:T], ALU.mult, ["bB0", "kp"], ["pl0"])
                tt("pool", pooled[:, 0, :], pl[:, 0, :], pbuf[:, 0, 16:T + 16], ALU.subtract, ["pl0", "pbuf0"], ["pooled0"])
                cp_any("pool", pbuf[:, 0, 0:16], pbuf[:, 0, T:T + 16], ["pbuf0"], ["pbuf0"])
                yield
                tt("pool", bA[:, 1, 2:T + 16], pbuf[:, 1, 2:T + 16], pbuf[:, 1, 1:T + 15], ALU.add, ["pbuf1"], ["bA1"])
                tt("pool", bB[:, 1, 4:T + 16], bA[:, 1, 4:T + 16], bA[:, 1, 2:T + 14], ALU.add, ["bA1"], ["bB1"])
                tt("pool", bA[:, 1, 8:T + 16], bB[:, 1, 8:T + 16], bB[:, 1, 4:T + 12], ALU.add, ["bB1", "bA1"], ["bA1"])
                tt("pool", bB[64:128, 1, 16:T + 16], bA[64:128, 1, 16:T + 16], bA[64:128, 1, 8:T + 8], ALU.add, ["bA1", "bB1"], ["bB1"])
                yield
                tt("pool", pl[0:64, 1, :], bA[0:64, 1, 16:T + 16], invc[0:64, T:2 * T], ALU.mult, ["bA1", "kp"], ["pl1"])
                tt("pool", pl[64:128, 1, :], bB[64:128, 1, 16:T + 16], invc[64:128, T:2 * T], ALU.mult, ["bB1", "kp"], ["pl1"])
                tt("pool", pooled[:, 1, :], pl[:, 1, :], pbuf[:, 1, 16:T + 16], ALU.subtract, ["pl1", "pbuf1"], ["pooled1"])
                cp_any("pool", pbuf[:, 1, 0:16], pbuf[:, 1, T:T + 16], ["pbuf1"], ["pbuf1"])
                yield
                for i in range(2):
                    bt, bk = next_bank()
                    mm(bt[:, 0:T], pwb[:, i, :], pooled[:, i, :], True, True, ["pwb", "pooled%d" % i], [bk])
                    stt(mixT[:, 4 + i, :], bt[:, 0:T], cp[:, 37 + i:38 + i], szp[:, pr, i, :], ALU.mult, ALU.mult,
                        [bk, "cp", "szp%dp%d" % (i, pr)], ["mix%d" % (4 + i)] + PSR(bk))
                    yield

            def pro(n):
                pr = n % 2
                for j in range(2):
                    D_ = dts[:, j]
                    dk_ = "dts%d" % j
                    act(D_[:, 4:8], dtx[:, pr, j, :], AF.Exp, ["dtx%dp%d" % (j, pr)], [dk_])
                    act(D_[:, 8:12], D_[:, 4:8], AF.Ln, [dk_], [dk_], bias=1.0)
                    tt("dve", D_[:, 12:16], D_[:, 8:12], abc[:], ALU.mult, [dk_, "abc"], [dk_])
                    ts("dve", D_[:, 24:28], D_[:, 12:16], ind[:, 0:1], None, ALU.mult, None, [dk_, "kp"], [dk_])
                    ts("dve", D_[:, 28:32], D_[:, 12:16], ind[:, 1:2], None, ALU.mult, None, [dk_, "kp"], [dk_])
                    mm(SM[:, 0:128], glr[0:16, pr, j * 128:(j + 1) * 128], gwb[0:16, :], True, True, ["glrp%d" % pr, "gwb"], ["SM"])
                    tt("dve", gx[:], SM[:, 0:128], gateb, ALU.add, ["SM", "bp"], ["gx"] + PSR("SM"))
                    yield
                    act(gx[:], gx[:], AF.Exp, ["gx"], ["gx"], scale=-1.0)
                    act(spl[:, j, :], gx[:], AF.Ln, ["gx"], ["spl%d" % j], bias=1.0)
                    mm(SM[:, 128:256], mstrict, spl[:, j, :], True, True, ["kp", "spl%d" % j], ["SM"])
                    mm(SM[:, 256:258], spl[:, j, :], ind, True, True, ["kp", "spl%d" % j], ["SM"])
                    mm(SM[:, 300:304], mstrict, D_[:, 12:16], True, True, ["kp", dk_], ["SM"])
                    mm(SM[:, 304:312], ones[:], D_[:, 24:32], True, True, ["ones", dk_], ["SM"])
                    act(Eg[:], SM[:, 128:256], AF.Exp, ["SM"], ["Eg"] + PSR("SM"), scale=-1.0 / 16)
                    act(gdec[:, pr, 2 * j:2 * j + 2], SM[:, 256:258], AF.Exp, ["SM"], ["gdec%dp%d" % (j, pr)] + PSR("SM"), scale=-1.0 / 16)
                    act(D_[:, 16:20], SM[:, 300:304], AF.Exp, ["SM"], [dk_] + PSR("SM"))
                    act(sdec[:, pr, j, :], SM[:, 304:312], AF.Exp, ["SM"], ["sdec%dp%d" % (j, pr)] + PSR("SM"))
                    tt("pool" if MV else "dve", kdec[:, pr, j, :], ksb[:, pr, j, :], Eg[:], ALU.mult, ["ksb%dp%d" % (j, pr), "Eg"], ["kdec%dp%d" % (j, pr)])
                    tt("dve", D_[:, 20:24], D_[:, 16:20], D_[:, 8:12], ALU.mult, [dk_], [dk_])
                    yield
                    for i in range(2):
                        tr(TRf[:, i * 128:(i + 1) * 128], xs[:, pr, i, j * 128:(j + 1) * 128], identf,
                           ["xs%dp%d" % (i, pr), "kp"], ["SM"], False)
                    for i in range(2):
                        tr(TRf[:, 256 + i * 128:256 + (i + 1) * 128], bfm[:, pr, i, j * 128:(j + 1) * 128], identf,
                           ["bfm%dp%d" % (i, pr), "kp"], ["SM"], i == 1)
                    tt("dve", wx[:, pr, j, :].rearrange("p (a b) -> p a b", a=4), TRf[:, 0:256].rearrange("p (a b) -> p a b", a=4),
                       D_[:, 20:24].unsqueeze(2).to_broadcast([128, 4, 64]), ALU.mult, ["SM", dk_], ["wx%dp%d" % (j, pr)] + PSR("SM"))
                    cp_any("act" if MV else "dve", btok[:, pr, j, :], TRf[:, 256:512], ["SM"], ["btok%dp%d" % (j, pr)] + PSR("SM"))
                    yield

            def rec(n):
                pr = n % 2
                for j in range(2):
                    for cc in range(2):
                        c = 2 * j + cc
                        r0 = cc * 64
                        mm(UB[:, 0:256], kdec[r0:r0 + 64, pr, j, :], vbf[r0:r0 + 64, pr, j, :], True, True,
                           ["kdec%dp%d" % (j, pr), "vbf%dp%d" % (j, pr)], ["UB"])
                        for g in range(2):
                            mm(UB[:, 256 + g * 128:256 + (g + 1) * 128], btok[r0:r0 + 64, pr, j, g * 128:(g + 1) * 128],
                               wx[r0:r0 + 64, pr, j, g * 128:(g + 1) * 128], True, True,
                               ["btok%dp%d" % (j, pr), "wx%dp%d" % (j, pr)], ["UB"])
                        tt("dve", tmpS[:], UB[:, 0:256], bdmask, ALU.mult, ["UB", "kp"], ["tmpS"] + PSR("UB"))
                        stt(S[:], S[:], gdec[:, pr, c:c + 1], tmpS[:], ALU.mult, ALU.add, ["S", "gdec%dp%d" % (j, pr), "tmpS"], ["S"])
                        cp_any("act", Sb[:], S[:], ["S"], ["Sb"])
                        tt("pool" if MV else "dve", tmpH[:].rearrange("p (a b) -> p a b", a=4), H[:].rearrange("p (a b) -> p a b", a=4),
                           sdec[:, pr, j, cc * 4:(cc + 1) * 4].unsqueeze(2).to_broadcast([128, 4, 64]), ALU.mult,
                           ["H", "sdec%dp%d" % (j, pr)], ["tmpH"])
                        tt("dve", H[:], tmpH[:], UB[:, 256:512], ALU.add, ["tmpH", "UB"], ["H"] + PSR("UB"))
                        cp_any("act", Hb[:], H[:], ["H"], ["Hb"])
                        yield
                        for i in range(2):
                            mm(OT[:, i, r0:r0 + 64], Sb[:, i * 128:(i + 1) * 128], qbf[:, pr, c * 64:(c + 1) * 64], True, True,
                               ["Sb", "qbfp%d" % pr], ["B6"])
                        for g in range(2):
                            mm(YT[:, g, r0:r0 + 64], Hb[:, g * 128:(g + 1) * 128], cbf[:, pr, g, c * 64:(c + 1) * 64], True, True,
                               ["Hb", "cbf%dp%d" % (g, pr)], ["B6"])
                        yield
                    js = slice(j * 128, (j + 1) * 128)
                    for i in range(2):
                        act(sq[:, i, js], OT[:, i, :], AF.Square, ["B6"], ["sq"] + PSR("B6"))
                        act(osb[:, i, js], OT[:, i, :], AF.Copy, ["B6"], ["osb"] + PSR("B6"))
                    for g in range(2):
                        stt(ysb[:, g, 0:128], xs[:, pr, g, js], cp[:, 39 + g:40 + g], YT[:, g, :], ALU.mult, ALU.add,
                            ["xs%dp%d" % (g, pr), "cp", "B6"], ["ysb%d" % g] + PSR("B6"))
                    yield
                    for i in range(2):
                        mm(OT[:, i, :], blk64, sq[:, i, js], True, True, ["kp", "sq"], ["B6"])
                    act(rs[:, 0:256], OT.rearrange("p a b -> p (a b)"), AF.Ln, ["B6"], ["rs"] + PSR("B6"), scale=1.0 / 64, bias=EPS)
                    act(rs[:, 0:256], rs[:, 0:256], AF.Exp, ["rs"], ["rs"], scale=-0.5)
                    for g in range(2):
                        tt("pool", yz[:, g, js], ysb[:, g, 0:128], szs[:, pr, g, js], ALU.mult,
                           ["ysb%d" % g, "szs%dp%d" % (g, pr)], ["yz"])
                        act(sq2[:, g, :], yz[:, g, js], AF.Square, ["yz"], ["sq2"])
                    yield
                    for i in range(2):
                        stt(t1[:, i, :], osb[:, i, js], cp[:, 36:37], rs[:, i * 128:(i + 1) * 128], ALU.mult, ALU.mult,
                            ["osb", "cp", "rs"], ["t1"])
                        tt("pool", mixT[:, 2 + i, js], t1[:, i, :], szg[:, pr, i, js], ALU.mult,
                           ["t1", "szg%dp%d" % (i, pr)], ["mix%d" % (2 + i)])
                    for g in range(2):
                        mm(YT[:, 0, :], ones[:], sq2[:, g, :], g == 0, g == 1, ["ones", "sq2"], ["B6"])
                    act(rs2[:], YT[:, 0, :], AF.Ln, ["B6"], ["rs2"] + PSR("B6"), scale=1.0 / 256, bias=EPS)
                    act(rs2[:], rs2[:], AF.Exp, ["rs2"], ["rs2"], scale=-0.5)
                    for g in range(2):
                        stt(mixT[:, 6 + g, js], yz[:, g, js], cp[:, 41 + g:42 + g], rs2[:], ALU.mult, ALU.mult,
                            ["yz", "cp", "rs2"], ["mix%d" % (6 + g)])
                    yield

            def post(n):
                s = n % 3
                xk = "x%d" % s
                X = xt[:, s]
                mixk = ["mix%d" % i for i in range(8)]
                for j in range(2):
                    for hf in range(2):
                        bt, bk = next_bank()
                        for kc in range(8):
                            mm(bt[:, :], mixT[:, kc, j * 128:(j + 1) * 128], woutbf[:, kc, hf * 512:(hf + 1) * 512],
                               kc == 0, kc == 7, mixk + ["woutbf"], [bk])
                        tt("dve", X[:, j, hf * 512:(hf + 1) * 512], bt[:, :], X[:, j, hf * 512:(hf + 1) * 512], ALU.add,
                           [bk, xk], [xk] + PSR(bk))
                if last:
                    for j in range(2):
                        act(mixT[:, 4 * j:4 * j + 4, :].rearrange("p a b -> p (a b)"), X[:, j, :], AF.Square, [xk],
                            ["st_fs%d" % j] + ["mix%d" % (4 * j + q) for q in range(4)], accum_out=st[:, 8 + j:9 + j])
                    ts("dve", st[:, 10:12], st[:, 8:10], 1.0 / D, EPS, ALU.mult, ALU.add, ["st_fs0", "st_fs1"], ["st_fm"])
                    tt("pool", st[:, 14:16], st[:, 10:12], mhalf[:], ALU.pow, ["st_fm", "mhalf"], ["st_fr"])
                    for j in range(2):
                        stt(X[:, j, :], X[:, j, :], st[:, 14 + j:15 + j], fnw[:], ALU.mult, ALU.mult,
                            [xk, "st_fr", "fnw"], [xk])
                dma(dst[n * T:(n + 1) * T, :].rearrange("(j p) d -> p j d", p=128), X, [xk], ["dst%d_%d" % (L, n)], "st%d" % s)
                if last:
                    P.final_wait("sp", ["dst%d_%d" % (L, n)])
                if n + 3 < NT:
                    load_x(n + 3)

            def drain(g):
                P.tag = g.__name__ if hasattr(g, "__name__") else ""
                for _ in g:
                    pass
                P.tag = ""


            def interleave(ga, na, gb, nb):
                mode = os.environ.get("ILV", "")
                if mode == "ab":
                    drain(ga); drain(gb); return
                if mode == "ba":
                    drain(gb); drain(ga); return
                a_done = b_done = 0
                a_live = b_live = True
                while a_live or b_live:
                    if a_live:
                        try:
                            next(ga)
                            a_done += 1
                        except StopIteration:
                            a_live = False
                    while b_live and (not a_live or b_done * na < a_done * nb):
                        try:
                            next(gb)
                            b_done += 1
                        except StopIteration:
                            b_live = False

            def chain2(*gens):
                for g in gens:
                    yield from g

            for n_ in range(1, min(3, NT)):
                load_x(n_)
            def silu_gen(n):
                silu_batch(n)
                yield

            drain(pre(0))
            silu_batch(0)
            drain(pro(0))
            for n in range(NT):
                drain(rec(n))
                drain(big(n))
                if n + 1 < NT:
                    drain(pre(n + 1))
                    P.tag = 'silu'
                    silu_batch(n + 1)
                    drain(pro(n + 1))
                P.tag = 'post'
                post(n)
        ms = P.schedule()
        print("[sched] ops=%d est_makespan_us=%.1f hist=%s" % (len(P.ops), ms, P.hist))
        P.emit(sems, block)
    return nc


def _kpack():
    kp = np.zeros((128, NK), np.float32)
    p = np.arange(128)
    kp[:, 0:128] = np.eye(128, dtype=np.float32)
    s = p[:, None]
    t = p[None, :]
    kp[:, 128:256] = ((s // 64 == t // 64) & (s > t)).astype(np.float32)
    kp[:, 256:258] = (s // 64 == np.arange(2)[None, :]).astype(np.float32)
    kp[:, 258:386] = (s // 64 == t // 64).astype(np.float32)
    col = np.arange(256)[None, :]
    kp[:, 386:642] = (s // 32 == col // 64).astype(np.float32)
    tt_ = np.arange(T)
    for i in range(2):
        win = np.array([POOL_WINDOWS[(i * 128 + pp) // 64] for pp in p], np.float32)
        kp[:, 642 + i * T:642 + (i + 1) * T] = 1.0 / np.minimum(tt_[None, :] + 1.0, win[:, None])
        kp[:, 1154 + i * T:1154 + (i + 1) * T] = (1.0 / win)[:, None]
    return kp


def _pack_layers(inp, NL):
    p = np.arange(128)
    cpk = np.zeros((NL, 128, 43), np.float32)
    bpk = np.zeros((NL, 128, 1160), np.float32)
    pwk = np.zeros((NL, 128, 256), np.float32)
    for L in range(NL):
        for i in range(2):
            for k in range(3):
                cpk[L, :, i * 3 + k] = inp["conv_a_w"][L, k, i * 128:(i + 1) * 128]
        for c in range(6):
            for k in range(4):
                cpk[L, :, 6 + c * 4 + k] = inp["ssd_conv_w"][L, k, c * 128:(c + 1) * 128]
            cpk[L, :, 30 + c] = inp["ssd_conv_b"][L, c * 128:(c + 1) * 128]
        cpk[L, :, 36] = inp["gla_norm_w"][L][p % 64]
        for i in range(2):
            cpk[L, :, 37 + i] = inp["pool_scale"][L, i * 128:(i + 1) * 128]
            cpk[L, :, 39 + i] = inp["ssd_d"][L][i * 2 + p // 64]
            cpk[L, :, 41 + i] = inp["ssd_norm_w"][L, i * 128:(i + 1) * 128]
        bpk[L, :, 0:1024] = inp["norm_w"][L][None, :]
        bpk[L, :, 1024:1152] = inp["gla_gate_b"][L][None, :]
        bpk[L, :, 1152:1156] = inp["ssd_dt_bias"][L][None, :]
        bpk[L, :, 1156:1160] = inp["ssd_a_log"][L][None, :]
        for i in range(2):
            for gl in range(2):
                pwk[L, gl * 64:(gl + 1) * 64, i * 128 + gl * 64:i * 128 + (gl + 1) * 64] = inp["pool_w"][L, 2 * i + gl]
    return cpk, bpk, pwk


def make_in_maps(inp, NL, ncores=8):
    inp = {k: np.asarray(v, np.float32) for k, v in inp.items()}
    B = inp["x"].shape[0]
    cpk, bpk, pwk = _pack_layers(inp, NL)
    kp = _kpack()
    fnw = np.ascontiguousarray(np.broadcast_to(inp["final_norm_w"][None, :], (128, D))).astype(np.float32)
    w_in = inp["w_in"][:NL]
    w_in = np.ascontiguousarray(np.concatenate([w_in[:, :, :1536], w_in[:, :, 3344:3348], w_in[:, :, 1536:3344]], axis=2))
    common = {"w_in": w_in, "w_out": np.ascontiguousarray(inp["w_out"][:NL]),
              "cpack": cpk, "bpack": bpk, "gatew": np.ascontiguousarray(inp["gla_gate_w"][:NL]),
              "poolw": pwk, "fnw": fnw, "kpack": kp}
    maps = []
    for c in range(ncores):
        m = dict(common)
        m["x"] = np.ascontiguousarray(inp["x"][c % B])
        maps.append(m)
    return maps


_NC_CACHE = {}


def kernel(**inputs):
    x = np.asarray(inputs["x"])
    B, SEQ, _ = x.shape
    NL = int(np.asarray(inputs["w_in"]).shape[0])
    key = (SEQ, NL)
    if key not in _NC_CACHE:
        _NC_CACHE[key] = build_nc(SEQ, NL)
    nc = _NC_CACHE[key]
    maps = make_in_maps(inputs, NL, 8)
    res = run_bass_kernel_spmd(nc, maps, core_ids=list(range(8)))
    out = np.stack([np.asarray(res.results[b]["y"], dtype=np.float32) for b in range(B)], axis=0)
    return out
```

```python
import contextlib
import os
import numpy as np
import concourse.bass as bass
import concourse.mybir as mybir
from concourse.bass_utils import run_bass_kernel_spmd

F32 = mybir.dt.float32
BF16 = mybir.dt.bfloat16
AF = mybir.ActivationFunctionType
ALU = mybir.AluOpType

D = 1024
DP = 3348
T = 256
EPS = 1e-6
POOL_WINDOWS = (2, 4, 8, 16)
C_AH, C_AB, C_AC, C_AZ, C_Q, C_K, C_V, C_DT, C_GLR, C_GZ, C_PU, C_PZ, C_SZ, C_XBC = (
    0, 256, 512, 768, 1024, 1152, 1280, 1536, 1540, 1556, 1812, 2068, 2324, 2580)
NK = 1666
ENGS = ("pe", "dve", "act", "pool", "sp")


class Prog:
    LAT = 0.2

    def __init__(self, nc):
        self.nc = nc
        self.ops = []

    tag = ""

    XSPLIT = {"x1": ["x1a", "x1b"], "x2": ["x2a", "x2b"]}

    def _ex(self, keys):
        out = []
        for k in keys:
            out.extend(self.XSPLIT.get(k, [k]))
        return out

    def op(self, eng, fn, reads=(), writes=(), dma_sem=None, signal=True, dur=0.3, aset=0):
        reads = self._ex(reads)
        writes = self._ex(writes)
        self.ops.append(dict(eng=eng, fns=[fn] if fn is not None else [], reads=list(reads), writes=list(writes),
                             dma_sem=dma_sem, signal=signal, dur=dur, aset=aset, tag=self.tag,
                             nm=",".join(writes[:2]) if writes else ""))

    def final_wait(self, eng, keys):
        self.op(eng, None, reads=keys, dur=0.0)

    def _merge_groups(self):
        out = []
        cur = None
        for o in self.ops:
            if cur is not None:
                assert o["eng"] == "pe", "unsignalled PE group interrupted"
                cur["fns"] += o["fns"]
                cur["reads"] += o["reads"]
                cur["writes"] += o["writes"]
                cur["dur"] += o["dur"]
                if o["signal"]:
                    cur["signal"] = True
                    out.append(cur)
                    cur = None
                continue
            if o["eng"] == "pe" and not o["signal"]:
                cur = dict(o)
                cur["fns"] = list(o["fns"])
                cur["reads"] = list(o["reads"])
                cur["writes"] = list(o["writes"])
                continue
            out.append(o)
        assert cur is None
        self.ops = out

    def schedule(self):
        self._merge_groups()
        ops = self.ops
        N = len(ops)
        lastw = {}
        readers = {}
        deps = [None] * N
        sdeps = []
        last_sp = None
        for i, o in enumerate(ops):
            d = set()
            for k in o["reads"]:
                if k in lastw:
                    d.add(lastw[k])
            for k in o["writes"]:
                if k in lastw:
                    d.add(lastw[k])
                d.update(readers.get(k, ()))
            d.discard(i)
            deps[i] = sorted(d)
            sd = set(d)
            if o["eng"] == "sp":
                if last_sp is not None:
                    sd.add(last_sp)
                last_sp = i
            sdeps.append(sorted(sd))
            for k in o["writes"]:
                lastw[k] = i
                readers[k] = []
            for k in o["reads"]:
                readers.setdefault(k, []).append(i)
        children = [[] for _ in range(N)]
        indeg = [0] * N
        for i in range(N):
            indeg[i] = len(sdeps[i])
            for d in sdeps[i]:
                children[d].append(i)
        prio = [0.0] * N
        for i in range(N - 1, -1, -1):
            m = 0.0
            for c in children[i]:
                if prio[c] > m:
                    m = prio[c]
            prio[i] = m + ops[i]["dur"] + 0.05
        def run(pr_):
            indeg_ = list(indeg)
            limdep = [None] * N
            blockers = []
            eng_free = {e: 0.0 for e in ENGS}
            act_set = [0]
            finish = [0.0] * N
            ready = [i for i in range(N) if indeg_[i] == 0]
            order = []
            LAT = self.LAT
            SAME = {'dve': 0.13, 'act': 0.10, 'pool': 0.17} if os.environ.get('K_SAME', '1') == '1' else {}
            WIN = float(os.environ.get('K_WIN', '0.15'))
            STAT = {} if os.environ.get('K_STAT') else None
            TL = [] if os.environ.get('K_TL') else None
            self.tl = TL
            self.stat = STAT
            HM = int(os.environ.get('K_H', '1'))
            ALPHA = float(os.environ.get('K_ALPHA', '0.01'))
            BETA = float(os.environ.get('K_BETA', '0.01'))
            while ready:
                best = None
                best_key = None
                cand = []
                for i in ready:
                    o = ops[i]
                    e = o["eng"]
                    t = eng_free[e]
                    for d in deps[i]:
                        f = finish[d] + (LAT if ops[d]["eng"] != e or ops[d]["dma_sem"] else SAME.get(e, 0.0))
                        if f > t:
                            t = f
                    if e == "act" and o["aset"] and o["aset"] != act_set[0]:
                        t += 1.3
                    cand.append((t, i))
                tmin = min(c[0] for c in cand)
                for t, i in cand:
                    if t <= tmin + WIN:
                        if HM == 1:
                            key = (-pr_[i], t)
                        elif HM == 2:
                            key = (i, t)
                        elif HM == 3:
                            key = (t - ALPHA * pr_[i], i)
                        else:
                            key = (-(pr_[i] - BETA * i), t)
                        if best_key is None or key < best_key:
                            best_key = key
                            best = (t, i)
                t, i = best
                o = ops[i]
                e = o["eng"]
                lim_ = None
                lt_ = -1.0
                for d in deps[i]:
                    if finish[d] > lt_:
                        lt_, lim_ = finish[d], d
                limdep[i] = lim_
                if e == "pe" and lim_ is not None and t - eng_free[e] > 0.1 and lt_ + 0.3 > t:
                    blockers.append((t - eng_free[e], i, lim_))
                if STAT is not None and e in ("pe", "dve", "act"):
                    idle = t - eng_free[e]
                    if idle > 0.05:
                        lim = None
                        lt = -1
                        for d in deps[i]:
                            f = finish[d]
                            if f > lt:
                                lt, lim = f, d
                        kk = (e, ops[lim]["eng"] if lim is not None else "-", o.get("tag", ""), ops[lim].get("tag", "") if lim is not None else "")
                        STAT[kk] = STAT.get(kk, 0.0) + idle
                if e == "act" and o["aset"]:
                    act_set[0] = o["aset"]
                if o["dma_sem"]:
                    eng_free[e] = t + 0.05
                    finish[i] = t + o["dur"]
                else:
                    eng_free[e] = t + o["dur"]
                    finish[i] = t + o["dur"]
                order.append(i)
                if TL is not None:
                    TL.append((t, e, o["dur"], o.get("tag", ""), o.get("nm", ""), i))
                ready.remove(i)
                for c in children[i]:
                    indeg_[c] -= 1
                    if indeg_[c] == 0:
                        ready.append(c)

            return (max(finish) if N else 0.0), order, blockers, limdep
        ITER = int(os.environ.get('K_ITER', '30'))
        GAIN = float(os.environ.get('K_GAIN', '1.0'))
        boost = [0.0] * N
        best = None
        hist = []
        for it in range(ITER):
            pr_ = [prio[i] + boost[i] for i in range(N)]
            ms_, ord_, blk_, lim_ = run(pr_)
            hist.append(round(ms_, 1))
            if best is None or ms_ < best[0]:
                best = (ms_, ord_)
            for (idle, i, d) in blk_:
                k = 0
                while d is not None and k < 12:
                    boost[d] += idle * GAIN
                    d = lim_[d]
                    k += 1
        self.hist = hist
        order = best[1]
        finish = [best[0]]
        assert len(order) == N
        self.order = order
        self.deps = deps
        self.makespan = max(finish) if N else 0.0
        return self.makespan

    def emit(self, sems, block):
        ops = self.ops
        cnt = {}
        ticket = {}
        queues = {e: [] for e in ENGS}
        seen = {e: {} for e in ENGS}
        for i in self.order:
            o = ops[i]
            e = o["eng"]
            waits = {}
            for d in self.deps[i]:
                od = ops[d]
                if od["eng"] == "pe" and e == "pe" and not od["dma_sem"]:
                    continue
                if not od["fns"]:
                    continue
                sk, v = ticket[d]
                if seen[e].get(sk, 0) >= v:
                    continue
                waits[sk] = max(waits.get(sk, 0), v)
            for sk, v in waits.items():
                seen[e][sk] = v
            inc = None
            if o["fns"]:
                if o["dma_sem"]:
                    sk = o["dma_sem"]
                    cnt[sk] = cnt.get(sk, 0) + 16
                    inc = (sk, 16)
                else:
                    sk = e
                    cnt[sk] = cnt.get(sk, 0) + 1
                    inc = (sk, 1)
                ticket[i] = (sk, cnt[sk])
            queues[e].append((list(waits.items()), o["fns"], inc))
        engmap = {"pe": "tensor", "dve": "vector", "act": "scalar", "pool": "gpsimd", "sp": "sync"}

        def make(ename):
            items = queues[ename]

            def body(eng):
                for waits, fns, inc in items:
                    for sk, v in waits:
                        eng.wait_ge(sems[sk], v)
                    ins = None
                    for fn in fns:
                        ins = fn(eng)
                    if ins is not None and inc is not None:
                        ins.then_inc(sems[inc[0]], inc[1])
            return body

        for ename in ENGS:
            if queues[ename]:
                getattr(block, engmap[ename])(make(ename))


def build_nc(SEQ, NL):
    NT = SEQ // T
    nc = bass.Bass("TRN2", target_bir_lowering=False)
    dt_ = nc.dram_tensor
    x_d = dt_("x", [SEQ, D], F32, kind="ExternalInput").ap()
    win_d = dt_("w_in", [NL, D, DP], F32, kind="ExternalInput").ap()
    wout_d = dt_("w_out", [NL, D, D], F32, kind="ExternalInput").ap()
    cp_d = dt_("cpack", [NL, 128, 43], F32, kind="ExternalInput").ap()
    bp_d = dt_("bpack", [NL, 128, 1160], F32, kind="ExternalInput").ap()
    gw_d = dt_("gatew", [NL, 16, 128], F32, kind="ExternalInput").ap()
    pw_d = dt_("poolw", [NL, 128, 256], F32, kind="ExternalInput").ap()
    fnw_d = dt_("fnw", [128, D], F32, kind="ExternalInput").ap()
    kp_d = dt_("kpack", [128, NK], F32, kind="ExternalInput").ap()
    y_d = dt_("y", [SEQ, D], F32, kind="ExternalOutput").ap()
    scr_d = dt_("scr", [SEQ, D], F32, kind="Internal").ap() if NL > 1 else None

    with contextlib.ExitStack() as es:
        def sb(name, shape, dt):
            return es.enter_context(nc.sbuf_tensor(name, shape, dt))

        def ps(name, shape, dt):
            return es.enter_context(nc.psum_tensor(name, shape, dt))

        wbf = sb("wbf", [128, 8, DP], BF16)
        woutbf = sb("woutbf", [128, 8, D], BF16)
        kp = sb("kp", [128, NK], F32)
        cp = sb("cp", [128, 43], F32)
        bp = sb("bp", [128, 1160], F32)
        fnw = sb("fnw_sb", [128, D], F32)
        gwf = sb("gwf", [16, 128], F32)
        gwb = sb("gwb", [16, 128], BF16)
        pwf = sb("pwf", [128, 256], F32)
        pwb = sb("pwb", [128, 2, 128], BF16)
        identb = sb("identb", [128, 128], BF16)
        ones = sb("ones", [128, 128], F32)
        abc = sb("abc", [128, 4], F32)
        xt = sb("xt", [128, 3, 2, D], F32)
        mhalf = sb("mhalf", [128, 2], F32)
        cvc = sb("cvc", [128, 2, T], F32)
        dtx = sb("dtx", [128, 2, 2, 4], F32)
        tmpS = sb("tmpS", [128, 256], F32)
        tmpH = sb("tmpH", [128, 256], F32)
        sq2 = sb("sq2", [128, 2, 128], F32)
        rs2 = sb("rs2", [128, 128], F32)
        hbf = sb("hbf", [128, 2, D], BF16)
        hT = sb("hT", [128, 2, 8, T], BF16)
        st = sb("st", [128, 16], F32)
        ah = sb("ah", [128, 2, T], F32)
        ubuf = sb("ubuf", [128, 2, T + 2], F32)
        cva = sb("cva", [128, 2, T], F32)
        yab = sb("yab", [128, 2, T], F32)
        sza = sb("sza", [128, 2, 2, T], F32)
        szg = sb("szg", [128, 2, 2, T], F32)
        szp = sb("szp", [128, 2, 2, T], F32)
        szs = sb("szs", [128, 2, 2, T], F32)
        qbf = sb("qbf", [128, 2, T], BF16)
        glr = sb("glr", [16, 2, T], BF16)
        pbuf = sb("pbuf", [128, 2, T + 16], F32)
        bA = sb("bA", [128, 2, T + 16], F32)
        bB = sb("bB", [128, 2, T + 16], F32)
        pl = sb("pl", [128, 2, T], F32)
        pooled = sb("pooled", [128, 2, T], BF16)
        xbuf = sb("xbuf", [128, 6, T + 3], F32)
        xs = sb("xs", [128, 2, 2, T], F32)
        bfm = sb("bfm", [128, 2, 2, T], F32)
        cbf = sb("cbf", [128, 2, 2, T], BF16)
        ksb = sb("ksb", [128, 2, 2, 128], F32)
        vbf = sb("vbf", [128, 2, 2, 256], BF16)
        gx = sb("gx", [128, 128], F32)
        spl = sb("spl", [128, 2, 128], F32)
        Eg = sb("Eg", [128, 128], F32)
        kdec = sb("kdec", [128, 2, 2, 128], BF16)
        gdec = sb("gdec", [128, 2, 4], F32)
        dts = sb("dts", [128, 2, 32], F32)
        sdec = sb("sdec", [128, 2, 2, 8], F32)
        wx = sb("wx", [128, 2, 2, 256], BF16)
        btok = sb("btok", [128, 2, 2, 256], BF16)
        S = sb("S", [128, 256], F32)
        Sb = sb("Sb", [128, 256], BF16)
        H = sb("H", [128, 256], F32)
        Hb = sb("Hb", [128, 256], BF16)
        osb = sb("osb", [128, 2, T], F32)
        sq = sb("sq", [128, 2, T], F32)
        rs = sb("rs", [128, T], F32)
        t1 = sb("t1", [128, 2, 128], F32)
        ysb = sb("ysb", [128, 2, 128], F32)
        yz = sb("yz", [128, 2, T], F32)
        mixT = sb("mixT", [128, 8, T], BF16)
        glt = sb("glt", [128, 16], F32)

        PA = ps("PA", [128, 512], F32)
        PB = ps("PB", [128, 512], F32)
        PC = ps("PC", [128, 512], F32)
        KV = ps("KV", [128, 512], F32)
        SM = ps("SM", [128, 512], F32)
        TRf = SM
        UB = ps("UB", [128, 512], F32)
        TRb = ps("TRb", [128, 2, 2, 128], BF16)
        OY = ps("OY", [128, 4, 128], F32)
        OT = OY[:, 0:2]
        YT = OY[:, 2:4]

        sem_names = list(ENGS) + ["ldx0", "ldx1", "ldx2", "st0", "st1", "st2", "lw0", "lw1", "lw2", "lw3", "lc"] + ["lk%d" % i for i in range(8)]
        sems = {k: es.enter_context(nc.semaphore(k)) for k in sem_names}
        block = es.enter_context(nc.Block())
        P = Prog(nc)
        op = P.op

        def PSR(bank):
            return [getattr(bank, "lock", bank) + ".lk"]

        PECOL = float(os.environ.get("K_PECOL", "0.00046"))
        ASET = {AF.Silu: 18, AF.Exp: 6, AF.Ln: 6, AF.Sqrt: 3}

        def act(out, in_, func, reads, writes, **kw):
            n = out.free_size()
            dur = 0.2 + n * 0.0009 + (0.1 if "accum_out" in kw else 0.0)
            return op("act", lambda e: e.activation(out=out, in_=in_, func=func, **kw), reads=reads, writes=writes,
                      dur=dur, aset=ASET.get(func, 0))

        def edur(eng, n):
            if eng == "pool":
                return 0.1 + n * 0.0022
            return 0.07 + n * 0.0014

        def cp_any(eng, out, in_, reads, writes):
            if eng == "act":
                return act(out, in_, AF.Copy, reads, writes)
            return op(eng, lambda e: e.tensor_copy(out=out, in_=in_), reads=reads, writes=writes, dur=edur(eng, out.free_size()))

        def tt(eng, out, in0, in1, o, reads, writes):
            return op(eng, lambda e: e.tensor_tensor(out=out, in0=in0, in1=in1, op=o), reads=reads, writes=writes,
                      dur=edur(eng, out.free_size()))

        def stt(out, in0, scalar, in1, o0, o1, reads, writes):
            return op("dve", lambda e: e.scalar_tensor_tensor(out=out, in0=in0, scalar=scalar, in1=in1, op0=o0, op1=o1),
                      reads=reads, writes=writes, dur=edur("dve", out.free_size()))

        def ts(eng, out, in0, s1, s2, o0, o1, reads, writes):
            d_ = edur(eng, out.free_size())
            if o1 is None:
                return op(eng, lambda e: e.tensor_scalar(out=out, in0=in0, scalar1=s1, scalar2=None, op0=o0),
                          reads=reads, writes=writes, dur=d_)
            return op(eng, lambda e: e.tensor_scalar(out=out, in0=in0, scalar1=s1, scalar2=s2, op0=o0, op1=o1),
                      reads=reads, writes=writes, dur=d_)

        def mm(out, lhsT, rhs, start, stop, reads, writes):
            n = max(64, rhs.free_size())
            d_ = n * PECOL * (2.5 if lhsT.dtype == F32 else 1) + 0.01
            return op("pe", lambda e: e.matmul(out, lhsT=lhsT, rhs=rhs, start=start, stop=stop),
                      reads=reads, writes=writes, signal=stop, dur=d_)

        def tr(out, in_, ident, reads, writes, signal):
            return op("pe", lambda e: e.transpose(out=out, in_=in_, identity=ident), reads=reads, writes=writes, signal=signal,
                      dur=0.12)

        def wkeys(col0, ncols):
            return ["w%d_%d" % (kc, hf) for hf in range(col0 // 1024, (col0 + ncols - 1) // 1024 + 1) for kc in range(8)]

        def dma(out, in_, reads, writes, sem):
            nbytes = out.free_size() * out.partition_size() * 4
            return op("sp", lambda e: e.dma_start(out=out, in_=in_), reads=reads, writes=writes, dma_sem=sem,
                      dur=1.5 + nbytes / 1.2e5)

        dma(kp[:], kp_d[:, :], [], ["kp"], "lk0")
        dma(fnw[:], fnw_d[:, :], [], ["fnw"], "lk1")
        cp_any("dve", identb[:], kp[:, 0:128], ["kp"], ["identb"])
        op("pool", lambda e: e.memset(ones[:], 1.0), writes=["ones"])
        op("pool", lambda e: e.memset(mhalf[:], -0.5), writes=["mhalf"])
        identf = kp[:, 0:128]
        mstrict = kp[:, 128:256]
        ind = kp[:, 256:258]
        blk64 = kp[:, 258:386]
        bdmask = kp[:, 386:642]

        rr = {"i": 0}

        def evac_eng():
            rr["i"] += 1
            return ("act", "dve")[rr["i"] % 2]

        for L in range(NL):
            last = (L == NL - 1)
            src = x_d if L == 0 else scr_d
            dst = y_d if last else scr_d
            dma(cp[:], cp_d[L], [], ["cp"], "lk2")
            dma(bp[:], bp_d[L], [], ["bp"], "lk3")
            dma(gwf[:], gw_d[L], [], ["gwf"], "lk4")
            dma(pwf[:], pw_d[L], [], ["pwf"], "lk5")
            cp_any("dve", gwb[:], gwf[:], ["gwf"], ["gwb"])
            cp_any("dve", pwb[:].rearrange("p a b -> p (a b)"), pwf[:], ["pwf"], ["pwb"])
            act(abc[:], bp[:, 1156:1160], AF.Exp, ["bp"], ["abc"])
            ts("dve", abc[:], abc[:], -1.0, None, ALU.mult, None, ["abc"], ["abc"])
            dma(xt[:, 0], (x_d if L == 0 else scr_d)[0:T, :].rearrange("(j p) d -> p j d", p=128),
                ["dst%d_%d" % (L - 1, 0)] if L > 0 else [], ["x0"], "ldx0")
            k = 0
            WR = [(0, 1024), (1024, 2048), (2048, 3072), (3072, DP)]

            def stage(k_):
                q = k_ % 4
                return xt[:, 1 + q // 2].rearrange("p j d -> p (j d)")[:, (q % 2) * 1024:(q % 2 + 1) * 1024], \
                    "x%d%s" % (1 + q // 2, "ab"[q % 2]), "lw%d" % q

            for r in (2, 3, 1, 0):
                c0, c1 = WR[r]
                for kc in range(8):
                    stg, sk, sem_ = stage(k)
                    dma(stg[:, 0:c1 - c0], win_d[L, kc * 128:(kc + 1) * 128, c0:c1], [], [sk], sem_)
                    cp_any(("act", "dve", "pool")[k % 3], wbf[:, kc, c0:c1], stg[:, 0:c1 - c0], [sk], ["w%d_%d" % (kc, r)])
                    k += 1
            for kc in range(8):
                stg, sk, sem_ = stage(k)
                dma(stg[:, 0:D], wout_d[L, kc * 128:(kc + 1) * 128, :], [], [sk], sem_)
                cp_any(("act", "dve", "pool")[k % 3], woutbf[:, kc, :], stg[:, 0:D], [sk], ["woutbf"])
                k += 1
            op("dve", lambda e: e.memset(S[:], 0.0), writes=["S"])
            op("dve", lambda e: e.memset(H[:], 0.0), writes=["H"])
            op("pool", lambda e: e.memset(ubuf[:, :, 0:2], 0.0), writes=["ubuf0", "ubuf1"])
            op("pool", lambda e: e.memset(pbuf[:, :, 0:16], 0.0), writes=["pbuf0", "pbuf1"])
            op("pool", lambda e: e.memset(xbuf[:, :, 0:3], 0.0), writes=["xbuf%d" % c for c in range(6)])

            normw = bp[:, 0:1024]
            gateb = bp[:, 1024:1152]
            dtb = bp[:, 1152:1156]

            def load_x(n_):
                s_ = n_ % 3
                rd = ["dst%d_%d" % (L - 1, n_)] if L > 0 else []
                dma(xt[:, s_], src[n_ * T:(n_ + 1) * T, :].rearrange("(j p) d -> p j d", p=128), rd, ["x%d" % s_], "ldx%d" % s_)

            bank = {"i": 0}
            hpar = {"p": 0}

            NB = int(os.environ.get("K_NB", "3"))
            BANKS = [(PA, "PA"), (PB, "PB"), (PC, "PC")][:NB]

            def next_bank():
                b = BANKS[bank["i"] % len(BANKS)]
                bank["i"] += 1
                return b

            def fm_tile(col0, ncols, evac):
                bt, bk = next_bank()
                hp = hpar["p"]
                for kc in range(8):
                    mm(bt[0:ncols, 0:T], wbf[:, kc, col0:col0 + ncols], hT[:, hp, kc, :], kc == 0, kc == 7,
                       wkeys(col0, ncols) + ["hT%d" % hp], [bk])
                evac(bt[0:ncols, 0:T], bk)

            def pre(n):
                s = n % 3
                pr = n % 2
                hpar["p"] = pr
                xk = "x%d" % s
                X = xt[:, s]
                for j in range(2):
                    act(hbf[:, j, :], X[:, j, :], AF.Square, [xk], ["st_ss%d" % j, "hbf%d" % j], accum_out=st[:, j:j + 1])
                ts("dve", st[:, 2:4], st[:, 0:2], 1.0 / D, EPS, ALU.mult, ALU.add, ["st_ss0", "st_ss1"], ["st_ms"])
                tt("pool", st[:, 6:8], st[:, 2:4], mhalf[:], ALU.pow, ["st_ms", "mhalf"], ["st_rstd"])
                yield
                for j in range(2):
                    stt(hbf[:, j, :], X[:, j, :], st[:, 6 + j:7 + j], normw, ALU.mult, ALU.mult,
                        [xk, "st_rstd", "bp"], ["hbf%d" % j])
                    yield
                for g in range(4):
                    for kk in range(2):
                        for j in range(2):
                            kc = 2 * g + kk
                            tr(TRb[:, kk, j, :], hbf[:, j, kc * 128:(kc + 1) * 128], identb[:],
                               ["hbf%d" % j, "identb"], ["TRb"], signal=(kk == 1 and j == 1))
                    cp_any(evac_eng(), hT[:, pr, 2 * g:2 * g + 2, :].rearrange("p a b -> p (a b)"),
                           TRb[:].rearrange("p a j b -> p (a j b)"), ["TRb"], ["hT%d" % pr] + PSR("TRb"))
                    yield
                for c in range(6):
                    def ev_x(p, bk, c=c):
                        xk_ = "xbuf%d" % c
                        act(xbuf[:, c, 3:T + 3], p, AF.Copy, [bk], [xk_] + PSR(bk))
                        if c < 2:
                            o_, ok_ = xs[:, pr, c, :], "xs%dp%d" % (c, pr)
                        elif c < 4:
                            o_, ok_ = bfm[:, pr, c - 2, :], "bfm%dp%d" % (c - 2, pr)
                        else:
                            o_, ok_ = cvc[:, c - 4, :], "cvc%d" % (c - 4)
                        ts("dve", o_, xbuf[:, c, 0:T], cp[:, 6 + c * 4:7 + c * 4], None, ALU.mult, None, [xk_, "cp"], [ok_])
                        for k_ in range(1, 4):
                            stt(o_, xbuf[:, c, k_:T + k_], cp[:, 6 + c * 4 + k_:7 + c * 4 + k_], o_, ALU.mult, ALU.add,
                                [xk_, "cp", ok_], [ok_])
                        cp_any("pool", xbuf[:, c, 0:3], xbuf[:, c, T:T + 3], [xk_], [xk_])
                    fm_tile(C_XBC + c * 128, 128, ev_x)
                    yield
                for (col, buf, nm) in ((C_GZ, szg, "szg"), (C_SZ, szs, "szs"), (C_AZ, sza, "sza"), (C_PZ, szp, "szp")):
                    for i in range(2):
                        fm_tile(col + i * 128, 128,
                                lambda p, bk, buf=buf, nm=nm, i=i: cp_any(evac_eng(), buf[:, pr, i, :], p, [bk],
                                                                          ["%s%dp%d" % (nm, i, pr)] + PSR(bk)))
                        yield
                for j in range(2):
                    for kc in range(8):
                        mm(KV[:, 0:404], hT[:, pr, kc, j * 128:(j + 1) * 128], wbf[:, kc, C_K:C_K + 404], kc == 0, kc == 7,
                           ["hT%d" % pr] + wkeys(C_K, 404), ["KV"])
                    act(ksb[:, pr, j, :], KV[:, 0:128], AF.Copy, ["KV"], ["ksb%dp%d" % (j, pr)] + PSR("KV"))
                    act(vbf[:, pr, j, :], KV[:, 128:384], AF.Copy, ["KV"], ["vbf%dp%d" % (j, pr)] + PSR("KV"))
                    tt("dve", dtx[:, pr, j, :], KV[:, 384:388], dtb, ALU.add, ["KV", "bp"], ["dtx%dp%d" % (j, pr)] + PSR("KV"))
                    act(glt[:], KV[:, 388:404], AF.Copy, ["KV"], ["glt"] + PSR("KV"))
                    tr(KV[0:16, 0:128], glt[:], identf, ["glt", "kp"], ["KV"], True)
                    act(glr[0:16, pr, j * 128:(j + 1) * 128], KV[0:16, 0:128], AF.Copy, ["KV"], ["glrp%d" % pr] + PSR("KV"))
                    yield
                fm_tile(C_Q, 128, lambda p, bk: act(qbf[:, pr, :], p, AF.Copy, [bk], ["qbfp%d" % pr] + PSR(bk), scale=32.0 ** -0.5))
                yield

            def silu_batch(n):
                pr = n % 2
                for c in range(2):
                    k_ = "xs%dp%d" % (c, pr)
                    act(xs[:, pr, c, :], xs[:, pr, c, :], AF.Silu, [k_, "cp"], [k_], bias=cp[:, 30 + c:31 + c])
                for c in range(2):
                    k_ = "bfm%dp%d" % (c, pr)
                    act(bfm[:, pr, c, :], bfm[:, pr, c, :], AF.Silu, [k_, "cp"], [k_], bias=cp[:, 32 + c:33 + c])
                for c in range(2):
                    act(cbf[:, pr, c, :], cvc[:, c, :], AF.Silu, ["cvc%d" % c, "cp"], ["cbf%dp%d" % (c, pr)], bias=cp[:, 34 + c:35 + c])
                for (buf, nm) in ((szg, "szg"), (szs, "szs"), (sza, "sza"), (szp, "szp")):
                    for i in range(2):
                        k_ = "%s%dp%d" % (nm, i, pr)
                        act(buf[:, pr, i, :], buf[:, pr, i, :], AF.Silu, [k_], [k_])

            def big(n):
                pr = n % 2
                hpar["p"] = pr
                for i in range(2):
                    fm_tile(C_AH + i * 128, 128, lambda p, bk, i=i: act(ah[:, i, :], p, AF.Copy, [bk], ["ah%d" % i] + PSR(bk)))
                    yield
                for i in range(2):
                    def ev_ac(p, bk, i=i):
                        tt("dve", ubuf[:, i, 2:T + 2], p, ah[:, i, :], ALU.mult, [bk, "ah%d" % i], ["ubuf%d" % i] + PSR(bk))
                        uk = ["ubuf%d" % i, "cp"]
                        ts("dve", cva[:, i, :], ubuf[:, i, 0:T], cp[:, i * 3:i * 3 + 1], None, ALU.mult, None, uk, ["cva%d" % i])
                        stt(cva[:, i, :], ubuf[:, i, 1:T + 1], cp[:, i * 3 + 1:i * 3 + 2], cva[:, i, :], ALU.mult, ALU.add,
                            uk + ["cva%d" % i], ["cva%d" % i])
                        stt(cva[:, i, :], ubuf[:, i, 2:T + 2], cp[:, i * 3 + 2:i * 3 + 3], cva[:, i, :], ALU.mult, ALU.add,
                            uk + ["cva%d" % i], ["cva%d" % i])
                        cp_any("pool", ubuf[:, i, 0:2], ubuf[:, i, T:T + 2], ["ubuf%d" % i], ["ubuf%d" % i])
                    fm_tile(C_AC + i * 128, 128, ev_ac)
                    yield
                for i in range(2):
                    def ev_ab(p, bk, i=i):
                        tt("dve", yab[:, i, :], p, cva[:, i, :], ALU.mult, [bk, "cva%d" % i], ["yab%d" % i] + PSR(bk))
                        tt("pool", mixT[:, i, :], yab[:, i, :], sza[:, pr, i, :], ALU.mult,
                           ["yab%d" % i, "sza%dp%d" % (i, pr)], ["mix%d" % i])
                    fm_tile(C_AB + i * 128, 128, ev_ab)
                    yield
                for i in range(2):
                    fm_tile(C_PU + i * 128, 128, lambda p, bk, i=i: act(pbuf[:, i, 16:T + 16], p, AF.Copy, [bk],
                                                                       ["pbuf%d" % i] + PSR(bk)))
                    yield
                invc = kp[:, 642:1154] if n == 0 else kp[:, 1154:1666]
                tt("pool", bA[:, 0, 14:T + 16], pbuf[:, 0, 14:T + 16], pbuf[:, 0, 13:T + 15], ALU.add, ["pbuf0"], ["bA0"])
                tt("pool", bB[64:128, 0, 16:T + 16], bA[64:128, 0, 16:T + 16], bA[64:128, 0, 14:T + 14], ALU.add, ["bA0"], ["bB0"])
                tt("pool", pl[0:64, 0, :], bA[0:64, 0, 16:T + 16], invc[0:64, 0:T], ALU.mult, ["bA0", "kp"], ["pl0"])
                tt("pool", pl[64:128, 0, :], bB[64:128, 0, 16:T + 16], invc[64:128, 0:T], ALU.mult, ["bB0", "kp"], ["pl0"])
                tt("pool", pooled[:, 0, :], pl[:, 0, :], pbuf[:, 0, 16:T + 16], ALU.subtract, ["pl0", "pbuf0"], ["pooled0"])
                cp_any("pool", pbuf[:, 0, 0:16], pbuf[:, 0, T:T + 16], ["pbuf0"], ["pbuf0"])
                yield
                tt("pool", bA[:, 1, 2:T + 16], pbuf[:, 1, 2:T + 16], pbuf[:, 1, 1:T + 15], ALU.add, ["pbuf1"], ["bA1"])
                tt("pool", bB[:, 1, 4:T + 16], bA[:, 1, 4:T + 16], bA[:, 1, 2:T + 14], ALU.add, ["bA1"], ["bB1"])
                tt("pool", bA[:, 1, 8:T + 16], bB[:, 1, 8:T + 16], bB[:, 1, 4:T + 12], ALU.add, ["bB1", "bA1"], ["bA1"])
                tt("pool", bB[64:128, 1, 16:T + 16], bA[64:128, 1, 16:T + 16], bA[64:128, 1, 8:T + 8], ALU.add, ["bA1", "bB1"], ["bB1"])
                yield
                tt("pool", pl[0:64, 1, :], bA[0:64, 1, 16:T + 16], invc[0:64, T:2 * T], ALU.mult, ["bA1", "kp"], ["pl1"])
                tt("pool", pl[64:128, 1, :], bB[64:128, 1, 16:T + 16], invc[64:128, T:2 * T], ALU.mult, ["bB1", "kp"], ["pl1"])
                tt("pool", pooled[:, 1, :], pl[:, 1, :], pbuf[:, 1, 16:T + 16], ALU.subtract, ["pl1", "pbuf1"], ["pooled1"])
                cp_any("pool", pbuf[:, 1, 0:16], pbuf[:, 1, T:T + 16], ["pbuf1"], ["pbuf1"])
                yield
                for i in range(2):
                    bt, bk = next_bank()
                    mm(bt[:, 0:T], pwb[:, i, :], pooled[:, i, :], True, True, ["pwb", "pooled%d" % i], [bk])
                    stt(mixT[:, 4 + i, :], bt[:, 0:T], cp[:, 37 + i:38 + i], szp[:, pr, i, :], ALU.mult, ALU.mult,
                        [bk, "cp", "szp%dp%d" % (i, pr)], ["mix%d" % (4 + i)] + PSR(bk))
                    yield

            def pro(n):
                pr = n % 2
                for j in range(2):
                    D_ = dts[:, j]
                    dk_ = "dts%d" % j
                    act(D_[:, 4:8], dtx[:, pr, j, :], AF.Exp, ["dtx%dp%d" % (j, pr)], [dk_])
                    act(D_[:, 8:12], D_[:, 4:8], AF.Ln, [dk_], [dk_], bias=1.0)
                    tt("dve", D_[:, 12:16], D_[:, 8:12], abc[:], ALU.mult, [dk_, "abc"], [dk_])
                    ts("dve", D_[:, 24:28], D_[:, 12:16], ind[:, 0:1], None, ALU.mult, None, [dk_, "kp"], [dk_])
                    ts("dve", D_[:, 28:32], D_[:, 12:16], ind[:, 1:2], None, ALU.mult, None, [dk_, "kp"], [dk_])
                    mm(SM[:, 0:128], glr[0:16, pr, j * 128:(j + 1) * 128], gwb[0:16, :], True, True, ["glrp%d" % pr, "gwb"], ["SM"])
                    tt("dve", gx[:], SM[:, 0:128], gateb, ALU.add, ["SM", "bp"], ["gx"] + PSR("SM"))
                    yield
                    act(gx[:], gx[:], AF.Exp, ["gx"], ["gx"], scale=-1.0)
                    act(spl[:, j, :], gx[:], AF.Ln, ["gx"], ["spl%d" % j], bias=1.0)
                    mm(SM[:, 128:256], mstrict, spl[:, j, :], True, True, ["kp", "spl%d" % j], ["SM"])
                    mm(SM[:, 256:258], spl[:, j, :], ind, True, True, ["kp", "spl%d" % j], ["SM"])
                    mm(SM[:, 300:304], mstrict, D_[:, 12:16], True, True, ["kp", dk_], ["SM"])
                    mm(SM[:, 304:312], ones[:], D_[:, 24:32], True, True, ["ones", dk_], ["SM"])
                    act(Eg[:], SM[:, 128:256], AF.Exp, ["SM"], ["Eg"] + PSR("SM"), scale=-1.0 / 16)
                    act(gdec[:, pr, 2 * j:2 * j + 2], SM[:, 256:258], AF.Exp, ["SM"], ["gdec%dp%d" % (j, pr)] + PSR("SM"), scale=-1.0 / 16)
                    act(D_[:, 16:20], SM[:, 300:304], AF.Exp, ["SM"], [dk_] + PSR("SM"))
                    act(sdec[:, pr, j, :], SM[:, 304:312], AF.Exp, ["SM"], ["sdec%dp%d" % (j, pr)] + PSR("SM"))
                    tt("dve", kdec[:, pr, j, :], ksb[:, pr, j, :], Eg[:], ALU.mult, ["ksb%dp%d" % (j, pr), "Eg"], ["kdec%dp%d" % (j, pr)])
                    tt("dve", D_[:, 20:24], D_[:, 16:20], D_[:, 8:12], ALU.mult, [dk_], [dk_])
                    yield
                    for i in range(2):
                        tr(TRf[:, i * 128:(i + 1) * 128], xs[:, pr, i, j * 128:(j + 1) * 128], identf,
                           ["xs%dp%d" % (i, pr), "kp"], ["SM"], False)
                    for i in range(2):
                        tr(TRf[:, 256 + i * 128:256 + (i + 1) * 128], bfm[:, pr, i, j * 128:(j + 1) * 128], identf,
                           ["bfm%dp%d" % (i, pr), "kp"], ["SM"], i == 1)
                    tt("dve", wx[:, pr, j, :].rearrange("p (a b) -> p a b", a=4), TRf[:, 0:256].rearrange("p (a b) -> p a b", a=4),
                       D_[:, 20:24].unsqueeze(2).to_broadcast([128, 4, 64]), ALU.mult, ["SM", dk_], ["wx%dp%d" % (j, pr)] + PSR("SM"))
                    cp_any("dve", btok[:, pr, j, :], TRf[:, 256:512], ["SM"], ["btok%dp%d" % (j, pr)] + PSR("SM"))
                    yield

            def rec(n):
                pr = n % 2
                for j in range(2):
                    for cc in range(2):
                        c = 2 * j + cc
                        r0 = cc * 64
                        mm(UB[:, 0:256], kdec[r0:r0 + 64, pr, j, :], vbf[r0:r0 + 64, pr, j, :], True, True,
                           ["kdec%dp%d" % (j, pr), "vbf%dp%d" % (j, pr)], ["UB"])
                        for g in range(2):
                            mm(UB[:, 256 + g * 128:256 + (g + 1) * 128], btok[r0:r0 + 64, pr, j, g * 128:(g + 1) * 128],
                               wx[r0:r0 + 64, pr, j, g * 128:(g + 1) * 128], True, True,
                               ["btok%dp%d" % (j, pr), "wx%dp%d" % (j, pr)], ["UB"])
                        tt("dve", tmpS[:], UB[:, 0:256], bdmask, ALU.mult, ["UB", "kp"], ["tmpS"] + PSR("UB"))
                        stt(S[:], S[:], gdec[:, pr, c:c + 1], tmpS[:], ALU.mult, ALU.add, ["S", "gdec%dp%d" % (j, pr), "tmpS"], ["S"])
                        cp_any("act", Sb[:], S[:], ["S"], ["Sb"])
                        tt("dve", tmpH[:].rearrange("p (a b) -> p a b", a=4), H[:].rearrange("p (a b) -> p a b", a=4),
                           sdec[:, pr, j, cc * 4:(cc + 1) * 4].unsqueeze(2).to_broadcast([128, 4, 64]), ALU.mult,
                           ["H", "sdec%dp%d" % (j, pr)], ["tmpH"])
                        tt("dve", H[:], tmpH[:], UB[:, 256:512], ALU.add, ["tmpH", "UB"], ["H"] + PSR("UB"))
                        cp_any("act", Hb[:], H[:], ["H"], ["Hb"])
                        yield
                        for i in range(2):
                            mm(OT[:, i, r0:r0 + 64], Sb[:, i * 128:(i + 1) * 128], qbf[:, pr, c * 64:(c + 1) * 64], True, True,
                               ["Sb", "qbfp%d" % pr], ["B6"])
                        for g in range(2):
                            mm(YT[:, g, r0:r0 + 64], Hb[:, g * 128:(g + 1) * 128], cbf[:, pr, g, c * 64:(c + 1) * 64], True, True,
                               ["Hb", "cbf%dp%d" % (g, pr)], ["B6"])
                        yield
                    js = slice(j * 128, (j + 1) * 128)
                    for i in range(2):
                        act(sq[:, i, js], OT[:, i, :], AF.Square, ["B6"], ["sq"] + PSR("B6"))
                        act(osb[:, i, js], OT[:, i, :], AF.Copy, ["B6"], ["osb"] + PSR("B6"))
                    for g in range(2):
                        stt(ysb[:, g, 0:128], xs[:, pr, g, js], cp[:, 39 + g:40 + g], YT[:, g, :], ALU.mult, ALU.add,
                            ["xs%dp%d" % (g, pr), "cp", "B6"], ["ysb%d" % g] + PSR("B6"))
                    yield
                    for i in range(2):
                        mm(OT[:, i, :], blk64, sq[:, i, js], True, True, ["kp", "sq"], ["B6"])
                    act(rs[:, 0:256], OT.rearrange("p a b -> p (a b)"), AF.Ln, ["B6"], ["rs"] + PSR("B6"), scale=1.0 / 64, bias=EPS)
                    act(rs[:, 0:256], rs[:, 0:256], AF.Exp, ["rs"], ["rs"], scale=-0.5)
                    for g in range(2):
                        tt("pool", yz[:, g, js], ysb[:, g, 0:128], szs[:, pr, g, js], ALU.mult,
                           ["ysb%d" % g, "szs%dp%d" % (g, pr)], ["yz"])
                        act(sq2[:, g, :], yz[:, g, js], AF.Square, ["yz"], ["sq2"])
                    yield
                    for i in range(2):
                        stt(t1[:, i, :], osb[:, i, js], cp[:, 36:37], rs[:, i * 128:(i + 1) * 128], ALU.mult, ALU.mult,
                            ["osb", "cp", "rs"], ["t1"])
                        tt("pool", mixT[:, 2 + i, js], t1[:, i, :], szg[:, pr, i, js], ALU.mult,
                           ["t1", "szg%dp%d" % (i, pr)], ["mix%d" % (2 + i)])
                    for g in range(2):
                        mm(YT[:, 0, :], ones[:], sq2[:, g, :], g == 0, g == 1, ["ones", "sq2"], ["B6"])
                    act(rs2[:], YT[:, 0, :], AF.Ln, ["B6"], ["rs2"] + PSR("B6"), scale=1.0 / 256, bias=EPS)
                    act(rs2[:], rs2[:], AF.Exp, ["rs2"], ["rs2"], scale=-0.5)
                    for g in range(2):
                        stt(mixT[:, 6 + g, js], yz[:, g, js], cp[:, 41 + g:42 + g], rs2[:], ALU.mult, ALU.mult,
                            ["yz", "cp", "rs2"], ["mix%d" % (6 + g)])
                    yield

            def post(n):
                s = n % 3
                xk = "x%d" % s
                X = xt[:, s]
                mixk = ["mix%d" % i for i in range(8)]
                for j in range(2):
                    for hf in range(2):
                        bt, bk = next_bank()
                        for kc in range(8):
                            mm(bt[:, :], mixT[:, kc, j * 128:(j + 1) * 128], woutbf[:, kc, hf * 512:(hf + 1) * 512],
                               kc == 0, kc == 7, mixk + ["woutbf"], [bk])
                        tt("dve", X[:, j, hf * 512:(hf + 1) * 512], bt[:, :], X[:, j, hf * 512:(hf + 1) * 512], ALU.add,
                           [bk, xk], [xk] + PSR(bk))
                if last:
                    for j in range(2):
                        act(mixT[:, 4 * j:4 * j + 4, :].rearrange("p a b -> p (a b)"), X[:, j, :], AF.Square, [xk],
                            ["st_fs%d" % j] + ["mix%d" % (4 * j + q) for q in range(4)], accum_out=st[:, 8 + j:9 + j])
                    ts("dve", st[:, 10:12], st[:, 8:10], 1.0 / D, EPS, ALU.mult, ALU.add, ["st_fs0", "st_fs1"], ["st_fm"])
                    tt("pool", st[:, 14:16], st[:, 10:12], mhalf[:], ALU.pow, ["st_fm", "mhalf"], ["st_fr"])
                    for j in range(2):
                        stt(X[:, j, :], X[:, j, :], st[:, 14 + j:15 + j], fnw[:], ALU.mult, ALU.mult,
                            [xk, "st_fr", "fnw"], [xk])
                dma(dst[n * T:(n + 1) * T, :].rearrange("(j p) d -> p j d", p=128), X, [xk], ["dst%d_%d" % (L, n)], "st%d" % s)
                if last:
                    P.final_wait("sp", ["dst%d_%d" % (L, n)])
                if n + 3 < NT:
                    load_x(n + 3)

            def drain(g):
                P.tag = g.__name__ if hasattr(g, "__name__") else ""
                for _ in g:
                    pass
                P.tag = ""


            def interleave(ga, na, gb, nb):
                mode = os.environ.get("ILV", "")
                if mode == "ab":
                    drain(ga); drain(gb); return
                if mode == "ba":
                    drain(gb); drain(ga); return
                a_done = b_done = 0
                a_live = b_live = True
                while a_live or b_live:
                    if a_live:
                        try:
                            next(ga)
                            a_done += 1
                        except StopIteration:
                            a_live = False
                    while b_live and (not a_live or b_done * na < a_done * nb):
                        try:
                            next(gb)
                            b_done += 1
                        except StopIteration:
                            b_live = False

            def chain2(*gens):
                for g in gens:
                    yield from g

            for n_ in range(1, min(3, NT)):
                load_x(n_)
            def silu_gen(n):
                silu_batch(n)
                yield

            drain(pre(0))
            silu_batch(0)
            drain(pro(0))
            for n in range(NT):
                drain(rec(n))
                drain(big(n))
                if n + 1 < NT:
                    drain(pre(n + 1))
                    P.tag = 'silu'
                    silu_batch(n + 1)
                    drain(pro(n + 1))
                P.tag = 'post'
                post(n)
        ms = P.schedule()
        print("[sched] ops=%d est_makespan_us=%.1f hist=%s" % (len(P.ops), ms, P.hist))
        P.emit(sems, block)
    return nc


def _kpack():
    kp = np.zeros((128, NK), np.float32)
    p = np.arange(128)
    kp[:, 0:128] = np.eye(128, dtype=np.float32)
    s = p[:, None]
    t = p[None, :]
    kp[:, 128:256] = ((s // 64 == t // 64) & (s > t)).astype(np.float32)
    kp[:, 256:258] = (s // 64 == np.arange(2)[None, :]).astype(np.float32)
    kp[:, 258:386] = (s // 64 == t // 64).astype(np.float32)
    col = np.arange(256)[None, :]
    kp[:, 386:642] = (s // 32 == col // 64).astype(np.float32)
    tt_ = np.arange(T)
    for i in range(2):
        win = np.array([POOL_WINDOWS[(i * 128 + pp) // 64] for pp in p], np.float32)
        kp[:, 642 + i * T:642 + (i + 1) * T] = 1.0 / np.minimum(tt_[None, :] + 1.0, win[:, None])
        kp[:, 1154 + i * T:1154 + (i + 1) * T] = (1.0 / win)[:, None]
    return kp


def _pack_layers(inp, NL):
    p = np.arange(128)
    cpk = np.zeros((NL, 128, 43), np.float32)
    bpk = np.zeros((NL, 128, 1160), np.float32)
    pwk = np.zeros((NL, 128, 256), np.float32)
    for L in range(NL):
        for i in range(2):
            for k in range(3):
                cpk[L, :, i * 3 + k] = inp["conv_a_w"][L, k, i * 128:(i + 1) * 128]
        for c in range(6):
            for k in range(4):
                cpk[L, :, 6 + c * 4 + k] = inp["ssd_conv_w"][L, k, c * 128:(c + 1) * 128]
            cpk[L, :, 30 + c] = inp["ssd_conv_b"][L, c * 128:(c + 1) * 128]
        cpk[L, :, 36] = inp["gla_norm_w"][L][p % 64]
        for i in range(2):
            cpk[L, :, 37 + i] = inp["pool_scale"][L, i * 128:(i + 1) * 128]
            cpk[L, :, 39 + i] = inp["ssd_d"][L][i * 2 + p // 64]
            cpk[L, :, 41 + i] = inp["ssd_norm_w"][L, i * 128:(i + 1) * 128]
        bpk[L, :, 0:1024] = inp["norm_w"][L][None, :]
        bpk[L, :, 1024:1152] = inp["gla_gate_b"][L][None, :]
        bpk[L, :, 1152:1156] = inp["ssd_dt_bias"][L][None, :]
        bpk[L, :, 1156:1160] = inp["ssd_a_log"][L][None, :]
        for i in range(2):
            for gl in range(2):
                pwk[L, gl * 64:(gl + 1) * 64, i * 128 + gl * 64:i * 128 + (gl + 1) * 64] = inp["pool_w"][L, 2 * i + gl]
    return cpk, bpk, pwk


def make_in_maps(inp, NL, ncores=8):
    inp = {k: np.asarray(v, np.float32) for k, v in inp.items()}
    B = inp["x"].shape[0]
    cpk, bpk, pwk = _pack_layers(inp, NL)
    kp = _kpack()
    fnw = np.ascontiguousarray(np.broadcast_to(inp["final_norm_w"][None, :], (128, D))).astype(np.float32)
    w_in = inp["w_in"][:NL]
    w_in = np.ascontiguousarray(np.concatenate([w_in[:, :, :1536], w_in[:, :, 3344:3348], w_in[:, :, 1536:3344]], axis=2))
    common = {"w_in": w_in, "w_out": np.ascontiguousarray(inp["w_out"][:NL]),
              "cpack": cpk, "bpack": bpk, "gatew": np.ascontiguousarray(inp["gla_gate_w"][:NL]),
              "poolw": pwk, "fnw": fnw, "kpack": kp}
    maps = []
    for c in range(ncores):
        m = dict(common)
        m["x"] = np.ascontiguousarray(inp["x"][c % B])
        maps.append(m)
    return maps


_NC_CACHE = {}


def kernel(**inputs):
    x = np.asarray(inputs["x"])
    B, SEQ, _ = x.shape
    NL = int(np.asarray(inputs["w_in"]).shape[0])
    key = (SEQ, NL)
    if key not in _NC_CACHE:
        _NC_CACHE[key] = build_nc(SEQ, NL)
    nc = _NC_CACHE[key]
    maps = make_in_maps(inputs, NL, 8)
    res = run_bass_kernel_spmd(nc, maps, core_ids=list(range(8)))
    out = np.stack([np.asarray(res.results[b]["y"], dtype=np.float32) for b in range(B)], axis=0)
    return out
```

```python
import contextlib
import os
import numpy as np
import concourse.bass as bass
import concourse.mybir as mybir
from concourse.bass_utils import run_bass_kernel_spmd

F32 = mybir.dt.float32
BF16 = mybir.dt.bfloat16
AF = mybir.ActivationFunctionType
ALU = mybir.AluOpType

D = 1024
DP = 3348
T = 256
EPS = 1e-6
POOL_WINDOWS = (2, 4, 8, 16)
C_AH, C_AB, C_AC, C_AZ, C_Q, C_K, C_V, C_DT, C_GLR, C_GZ, C_PU, C_PZ, C_SZ, C_XBC = (
    0, 256, 512, 768, 1024, 1152, 1280, 1536, 1540, 1556, 1812, 2068, 2324, 2580)
NK = 1666
ENGS = ("pe", "dve", "act", "pool", "sp")


class Prog:
    LAT = 0.2

    def __init__(self, nc):
        self.nc = nc
        self.ops = []

    tag = ""

    XSPLIT = {"x1": ["x1a", "x1b"], "x2": ["x2a", "x2b"]}

    def _ex(self, keys):
        out = []
        for k in keys:
            out.extend(self.XSPLIT.get(k, [k]))
        return out

    def op(self, eng, fn, reads=(), writes=(), dma_sem=None, signal=True, dur=0.3, aset=0):
        reads = self._ex(reads)
        writes = self._ex(writes)
        self.ops.append(dict(eng=eng, fns=[fn] if fn is not None else [], reads=list(reads), writes=list(writes),
                             dma_sem=dma_sem, signal=signal, dur=dur, aset=aset, tag=self.tag,
                             nm=",".join(writes[:2]) if writes else ""))

    def final_wait(self, eng, keys):
        self.op(eng, None, reads=keys, dur=0.0)

    def _merge_groups(self):
        out = []
        cur = None
        for o in self.ops:
            if cur is not None:
                assert o["eng"] == "pe", "unsignalled PE group interrupted"
                cur["fns"] += o["fns"]
                cur["reads"] += o["reads"]
                cur["writes"] += o["writes"]
                cur["dur"] += o["dur"]
                if o["signal"]:
                    cur["signal"] = True
                    out.append(cur)
                    cur = None
                continue
            if o["eng"] == "pe" and not o["signal"]:
                cur = dict(o)
                cur["fns"] = list(o["fns"])
                cur["reads"] = list(o["reads"])
                cur["writes"] = list(o["writes"])
                continue
            out.append(o)
        assert cur is None
        self.ops = out

    def schedule(self):
        self._merge_groups()
        ops = self.ops
        N = len(ops)
        lastw = {}
        readers = {}
        deps = [None] * N
        sdeps = []
        last_sp = None
        for i, o in enumerate(ops):
            d = set()
            for k in o["reads"]:
                if k in lastw:
                    d.add(lastw[k])
            for k in o["writes"]:
                if k in lastw:
                    d.add(lastw[k])
                d.update(readers.get(k, ()))
            d.discard(i)
            deps[i] = sorted(d)
            sd = set(d)
            if o["eng"] == "sp":
                if last_sp is not None:
                    sd.add(last_sp)
                last_sp = i
            sdeps.append(sorted(sd))
            for k in o["writes"]:
                lastw[k] = i
                readers[k] = []
            for k in o["reads"]:
                readers.setdefault(k, []).append(i)
        children = [[] for _ in range(N)]
        indeg = [0] * N
        for i in range(N):
            indeg[i] = len(sdeps[i])
            for d in sdeps[i]:
                children[d].append(i)
        prio = [0.0] * N
        for i in range(N - 1, -1, -1):
            m = 0.0
            for c in children[i]:
                if prio[c] > m:
                    m = prio[c]
            prio[i] = m + ops[i]["dur"] + 0.05
        def run(pr_):
            indeg_ = list(indeg)
            limdep = [None] * N
            blockers = []
            eng_free = {e: 0.0 for e in ENGS}
            act_set = [0]
            finish = [0.0] * N
            ready = [i for i in range(N) if indeg_[i] == 0]
            order = []
            LAT = self.LAT
            SAME = {'dve': 0.13, 'act': 0.10, 'pool': 0.17} if os.environ.get('K_SAME', '1') == '1' else {}
            WIN = float(os.environ.get('K_WIN', '0.15'))
            STAT = {} if os.environ.get('K_STAT') else None
            TL = [] if os.environ.get('K_TL') else None
            self.tl = TL
            self.stat = STAT
            HM = int(os.environ.get('K_H', '1'))
            ALPHA = float(os.environ.get('K_ALPHA', '0.01'))
            BETA = float(os.environ.get('K_BETA', '0.01'))
            while ready:
                best = None
                best_key = None
                cand = []
                for i in ready:
                    o = ops[i]
                    e = o["eng"]
                    t = eng_free[e]
                    for d in deps[i]:
                        f = finish[d] + (LAT if ops[d]["eng"] != e or ops[d]["dma_sem"] else SAME.get(e, 0.0))
                        if f > t:
                            t = f
                    if e == "act" and o["aset"] and o["aset"] != act_set[0]:
                        t += 1.3
                    cand.append((t, i))
                tmin = min(c[0] for c in cand)
                for t, i in cand:
                    if t <= tmin + WIN:
                        if HM == 1:
                            key = (-pr_[i], t)
                        elif HM == 2:
                            key = (i, t)
                        elif HM == 3:
                            key = (t - ALPHA * pr_[i], i)
                        else:
                            key = (-(pr_[i] - BETA * i), t)
                        if best_key is None or key < best_key:
                            best_key = key
                            best = (t, i)
                t, i = best
                o = ops[i]
                e = o["eng"]
                lim_ = None
                lt_ = -1.0
                for d in deps[i]:
                    if finish[d] > lt_:
                        lt_, lim_ = finish[d], d
                limdep[i] = lim_
                if e == "pe" and lim_ is not None and t - eng_free[e] > 0.1 and lt_ + 0.3 > t:
                    blockers.append((t - eng_free[e], i, lim_))
                if STAT is not None and e in ("pe", "dve", "act"):
                    idle = t - eng_free[e]
                    if idle > 0.05:
                        lim = None
                        lt = -1
                        for d in deps[i]:
                            f = finish[d]
                            if f > lt:
                                lt, lim = f, d
                        kk = (e, ops[lim]["eng"] if lim is not None else "-", o.get("tag", ""), ops[lim].get("tag", "") if lim is not None else "")
                        STAT[kk] = STAT.get(kk, 0.0) + idle
                if e == "act" and o["aset"]:
                    act_set[0] = o["aset"]
                if o["dma_sem"]:
                    eng_free[e] = t + 0.05
                    finish[i] = t + o["dur"]
                else:
                    eng_free[e] = t + o["dur"]
                    finish[i] = t + o["dur"]
                order.append(i)
                if TL is not None:
                    TL.append((t, e, o["dur"], o.get("tag", ""), o.get("nm", ""), i))
                ready.remove(i)
                for c in children[i]:
                    indeg_[c] -= 1
                    if indeg_[c] == 0:
                        ready.append(c)

            return (max(finish) if N else 0.0), order, blockers, limdep
        ITER = int(os.environ.get('K_ITER', '30'))
        GAIN = float(os.environ.get('K_GAIN', '1.0'))
        boost = [0.0] * N
        best = None
        hist = []
        for it in range(ITER):
            pr_ = [prio[i] + boost[i] for i in range(N)]
            ms_, ord_, blk_, lim_ = run(pr_)
            hist.append(round(ms_, 1))
            if best is None or ms_ < best[0]:
                best = (ms_, ord_)
            for (idle, i, d) in blk_:
                k = 0
                while d is not None and k < 12:
                    boost[d] += idle * GAIN
                    d = lim_[d]
                    k += 1
        self.hist = hist
        order = best[1]
        finish = [best[0]]
        assert len(order) == N
        self.order = order
        self.deps = deps
        self.makespan = max(finish) if N else 0.0
        return self.makespan

    def emit(self, sems, block):
        ops = self.ops
        cnt = {}
        ticket = {}
        queues = {e: [] for e in ENGS}
        seen = {e: {} for e in ENGS}
        for i in self.order:
            o = ops[i]
            e = o["eng"]
            waits = {}
            for d in self.deps[i]:
                od = ops[d]
                if od["eng"] == "pe" and e == "pe" and not od["dma_sem"]:
                    continue
                if not od["fns"]:
                    continue
                sk, v = ticket[d]
                if seen[e].get(sk, 0) >= v:
                    continue
                waits[sk] = max(waits.get(sk, 0), v)
            for sk, v in waits.items():
                seen[e][sk] = v
            inc = None
            if o["fns"]:
                if o["dma_sem"]:
                    sk = o["dma_sem"]
                    cnt[sk] = cnt.get(sk, 0) + 16
                    inc = (sk, 16)
                else:
                    sk = e
                    cnt[sk] = cnt.get(sk, 0) + 1
                    inc = (sk, 1)
                ticket[i] = (sk, cnt[sk])
            queues[e].append((list(waits.items()), o["fns"], inc))
        engmap = {"pe": "tensor", "dve": "vector", "act": "scalar", "pool": "gpsimd", "sp": "sync"}

        def make(ename):
            items = queues[ename]

            def body(eng):
                for waits, fns, inc in items:
                    for sk, v in waits:
                        eng.wait_ge(sems[sk], v)
                    ins = None
                    for fn in fns:
                        ins = fn(eng)
                    if ins is not None and inc is not None:
                        ins.then_inc(sems[inc[0]], inc[1])
            return body

        for ename in ENGS:
            if queues[ename]:
                getattr(block, engmap[ename])(make(ename))


def build_nc(SEQ, NL):
    NT = SEQ // T
    nc = bass.Bass("TRN2", target_bir_lowering=False)
    dt_ = nc.dram_tensor
    x_d = dt_("x", [SEQ, D], F32, kind="ExternalInput").ap()
    win_d = dt_("w_in", [NL, D, DP], F32, kind="ExternalInput").ap()
    wout_d = dt_("w_out", [NL, D, D], F32, kind="ExternalInput").ap()
    cp_d = dt_("cpack", [NL, 128, 43], F32, kind="ExternalInput").ap()
    bp_d = dt_("bpack", [NL, 128, 1160], F32, kind="ExternalInput").ap()
    gw_d = dt_("gatew", [NL, 16, 128], F32, kind="ExternalInput").ap()
    pw_d = dt_("poolw", [NL, 128, 256], F32, kind="ExternalInput").ap()
    fnw_d = dt_("fnw", [128, D], F32, kind="ExternalInput").ap()
    kp_d = dt_("kpack", [128, NK], F32, kind="ExternalInput").ap()
    y_d = dt_("y", [SEQ, D], F32, kind="ExternalOutput").ap()
    scr_d = dt_("scr", [SEQ, D], F32, kind="Internal").ap() if NL > 1 else None

    with contextlib.ExitStack() as es:
        def sb(name, shape, dt):
            return es.enter_context(nc.sbuf_tensor(name, shape, dt))

        def ps(name, shape, dt):
            return es.enter_context(nc.psum_tensor(name, shape, dt))

        wbf = sb("wbf", [128, 8, DP], BF16)
        woutbf = sb("woutbf", [128, 8, D], BF16)
        kp = sb("kp", [128, NK], F32)
        cp = sb("cp", [128, 43], F32)
        bp = sb("bp", [128, 1160], F32)
        fnw = sb("fnw_sb", [128, D], F32)
        gwf = sb("gwf", [16, 128], F32)
        gwb = sb("gwb", [16, 128], BF16)
        pwf = sb("pwf", [128, 256], F32)
        pwb = sb("pwb", [128, 2, 128], BF16)
        identb = sb("identb", [128, 128], BF16)
        ones = sb("ones", [128, 128], F32)
        abc = sb("abc", [128, 4], F32)
        xt = sb("xt", [128, 3, 2, D], F32)
        mhalf = sb("mhalf", [128, 2], F32)
        cvc = sb("cvc", [128, 2, T], F32)
        dtx = sb("dtx", [128, 2, 2, 4], F32)
        tmpS = sb("tmpS", [128, 256], F32)
        tmpH = sb("tmpH", [128, 256], F32)
        sq2 = sb("sq2", [128, 2, 128], F32)
        rs2 = sb("rs2", [128, 128], F32)
        hbf = sb("hbf", [128, 2, D], BF16)
        hT = sb("hT", [128, 2, 8, T], BF16)
        st = sb("st", [128, 16], F32)
        ah = sb("ah", [128, 2, T], F32)
        ubuf = sb("ubuf", [128, 2, T + 2], F32)
        cva = sb("cva", [128, 2, T], F32)
        yab = sb("yab", [128, 2, T], F32)
        sza = sb("sza", [128, 2, 2, T], F32)
        szg = sb("szg", [128, 2, 2, T], F32)
        szp = sb("szp", [128, 2, 2, T], F32)
        szs = sb("szs", [128, 2, 2, T], F32)
        qbf = sb("qbf", [128, 2, T], BF16)
        glr = sb("glr", [16, 2, T], BF16)
        pbuf = sb("pbuf", [128, 2, T + 16], F32)
        bA = sb("bA", [128, 2, T + 16], F32)
        bB = sb("bB", [128, 2, T + 16], F32)
        pl = sb("pl", [128, 2, T], F32)
        pooled = sb("pooled", [128, 2, T], BF16)
        xbuf = sb("xbuf", [128, 6, T + 3], F32)
        xs = sb("xs", [128, 2, 2, T], F32)
        bfm = sb("bfm", [128, 2, 2, T], F32)
        cbf = sb("cbf", [128, 2, 2, T], BF16)
        ksb = sb("ksb", [128, 2, 2, 128], F32)
        vbf = sb("vbf", [128, 2, 2, 256], BF16)
        gx = sb("gx", [128, 128], F32)
        spl = sb("spl", [128, 2, 128], F32)
        Eg = sb("Eg", [128, 128], F32)
        kdec = sb("kdec", [128, 2, 2, 128], BF16)
        gdec = sb("gdec", [128, 2, 4], F32)
        dts = sb("dts", [128, 2, 32], F32)
        sdec = sb("sdec", [128, 2, 2, 8], F32)
        wx = sb("wx", [128, 2, 2, 256], BF16)
        btok = sb("btok", [128, 2, 2, 256], BF16)
        S = sb("S", [128, 256], F32)
        Sb = sb("Sb", [128, 256], BF16)
        H = sb("H", [128, 256], F32)
        Hb = sb("Hb", [128, 256], BF16)
        osb = sb("osb", [128, 2, T], F32)
        sq = sb("sq", [128, 2, T], F32)
        rs = sb("rs", [128, T], F32)
        t1 = sb("t1", [128, 2, 128], F32)
        ysb = sb("ysb", [128, 2, 128], F32)
        yz = sb("yz", [128, 2, T], F32)
        mixT = sb("mixT", [128, 8, T], BF16)
        glt = sb("glt", [128, 16], F32)

        PA = ps("PA", [128, 512], F32)
        PB = ps("PB", [128, 512], F32)
        PC = ps("PC", [128, 512], F32)
        KV = ps("KV", [128, 512], F32)
        SM = ps("SM", [128, 512], F32)
        TRf = SM
        UB = ps("UB", [128, 512], F32)
        TRb = ps("TRb", [128, 2, 2, 128], BF16)
        OY = ps("OY", [128, 4, 128], F32)
        OT = OY[:, 0:2]
        YT = OY[:, 2:4]

        sem_names = list(ENGS) + ["ldx0", "ldx1", "ldx2", "st0", "st1", "st2", "lw0", "lw1", "lw2", "lw3", "lc"] + ["lk%d" % i for i in range(8)]
        sems = {k: es.enter_context(nc.semaphore(k)) for k in sem_names}
        block = es.enter_context(nc.Block())
        P = Prog(nc)
        op = P.op

        def PSR(bank):
            return [getattr(bank, "lock", bank) + ".lk"]

        PECOL = float(os.environ.get("K_PECOL", "0.00046"))
        ASET = {AF.Silu: 18, AF.Exp: 6, AF.Ln: 6, AF.Sqrt: 3}

        def act(out, in_, func, reads, writes, **kw):
            n = out.free_size()
            dur = 0.2 + n * 0.0009 + (0.1 if "accum_out" in kw else 0.0)
            return op("act", lambda e: e.activation(out=out, in_=in_, func=func, **kw), reads=reads, writes=writes,
                      dur=dur, aset=ASET.get(func, 0))

        def edur(eng, n):
            if eng == "pool":
                return 0.1 + n * 0.0022
            return 0.07 + n * 0.0014

        def cp_any(eng, out, in_, reads, writes):
            if eng == "act":
                return act(out, in_, AF.Copy, reads, writes)
            return op(eng, lambda e: e.tensor_copy(out=out, in_=in_), reads=reads, writes=writes, dur=edur(eng, out.free_size()))

        def tt(eng, out, in0, in1, o, reads, writes):
            return op(eng, lambda e: e.tensor_tensor(out=out, in0=in0, in1=in1, op=o), reads=reads, writes=writes,
                      dur=edur(eng, out.free_size()))

        def stt(out, in0, scalar, in1, o0, o1, reads, writes):
            return op("dve", lambda e: e.scalar_tensor_tensor(out=out, in0=in0, scalar=scalar, in1=in1, op0=o0, op1=o1),
                      reads=reads, writes=writes, dur=edur("dve", out.free_size()))

        def ts(eng, out, in0, s1, s2, o0, o1, reads, writes):
            d_ = edur(eng, out.free_size())
            if o1 is None:
                return op(eng, lambda e: e.tensor_scalar(out=out, in0=in0, scalar1=s1, scalar2=None, op0=o0),
                          reads=reads, writes=writes, dur=d_)
            return op(eng, lambda e: e.tensor_scalar(out=out, in0=in0, scalar1=s1, scalar2=s2, op0=o0, op1=o1),
                      reads=reads, writes=writes, dur=d_)

        def mm(out, lhsT, rhs, start, stop, reads, writes):
            n = max(64, rhs.free_size())
            d_ = n * PECOL * (2.5 if lhsT.dtype == F32 else 1) + 0.01
            return op("pe", lambda e: e.matmul(out, lhsT=lhsT, rhs=rhs, start=start, stop=stop),
                      reads=reads, writes=writes, signal=stop, dur=d_)

        def tr(out, in_, ident, reads, writes, signal):
            return op("pe", lambda e: e.transpose(out=out, in_=in_, identity=ident), reads=reads, writes=writes, signal=signal,
                      dur=0.12)

        WR = [(0, 1024), (1024, 2048), (2048, 2068), (2068, 3092), (3092, DP)]

        def wkeys(col0, ncols):
            return ["w%d_%d" % (kc, r) for r, (a, b) in enumerate(WR) if a < col0 + ncols and col0 < b for kc in range(8)]

        def dma(out, in_, reads, writes, sem):
            nbytes = out.free_size() * out.partition_size() * 4
            return op("sp", lambda e: e.dma_start(out=out, in_=in_), reads=reads, writes=writes, dma_sem=sem,
                      dur=1.5 + nbytes / 1.2e5)

        dma(kp[:], kp_d[:, :], [], ["kp"], "lk0")
        dma(fnw[:], fnw_d[:, :], [], ["fnw"], "lk1")
        cp_any("dve", identb[:], kp[:, 0:128], ["kp"], ["identb"])
        op("pool", lambda e: e.memset(ones[:], 1.0), writes=["ones"])
        op("pool", lambda e: e.memset(mhalf[:], -0.5), writes=["mhalf"])
        identf = kp[:, 0:128]
        mstrict = kp[:, 128:256]
        ind = kp[:, 256:258]
        blk64 = kp[:, 258:386]
        bdmask = kp[:, 386:642]

        rr = {"i": 0}

        def evac_eng():
            rr["i"] += 1
            return ("act", "dve")[rr["i"] % 2]

        for L in range(NL):
            last = (L == NL - 1)
            src = x_d if L == 0 else scr_d
            dst = y_d if last else scr_d
            dma(cp[:], cp_d[L], [], ["cp"], "lk2")
            dma(bp[:], bp_d[L], [], ["bp"], "lk3")
            dma(gwf[:], gw_d[L], [], ["gwf"], "lk4")
            dma(pwf[:], pw_d[L], [], ["pwf"], "lk5")
            cp_any("dve", gwb[:], gwf[:], ["gwf"], ["gwb"])
            cp_any("dve", pwb[:].rearrange("p a b -> p (a b)"), pwf[:], ["pwf"], ["pwb"])
            act(abc[:], bp[:, 1156:1160], AF.Exp, ["bp"], ["abc"])
            ts("dve", abc[:], abc[:], -1.0, None, ALU.mult, None, ["abc"], ["abc"])
            dma(xt[:, 0], (x_d if L == 0 else scr_d)[0:T, :].rearrange("(j p) d -> p j d", p=128),
                ["dst%d_%d" % (L - 1, 0)] if L > 0 else [], ["x0"], "ldx0")
            EARLY = NT >= 4 and (NT - 3) % 3 == 2

            def stage(q):
                return xt[:, 1 + q // 2].rearrange("p j d -> p (j d)")[:, (q % 2) * 1024:(q % 2 + 1) * 1024], \
                    "x%d%s" % (1 + q // 2, "ab"[q % 2]), "lw%d" % q

            def load_w(Lw, ranges, slots, with_out):
                k = 0
                for r in ranges:
                    c0, c1 = WR[r]
                    for kc in range(8):
                        stg, sk, sem_ = stage(slots[k % len(slots)])
                        dma(stg[:, 0:c1 - c0], win_d[Lw, kc * 128:(kc + 1) * 128, c0:c1], [], [sk], sem_)
                        cp_any(("act", "dve", "pool")[k % 3], wbf[:, kc, c0:c1], stg[:, 0:c1 - c0], [sk], ["w%d_%d" % (kc, r)])
                        k += 1
                if with_out:
                    for kc in range(8):
                        stg, sk, sem_ = stage(slots[k % len(slots)])
                        dma(stg[:, 0:D], wout_d[Lw, kc * 128:(kc + 1) * 128, :], [], [sk], sem_)
                        cp_any(("act", "dve", "pool")[k % 3], woutbf[:, kc, :], stg[:, 0:D], [sk], ["woutbf"])
                        k += 1

            if L > 0 and EARLY:
                load_w(L, (1, 2, 0), (0, 1, 2, 3), True)
            else:
                load_w(L, (3, 4, 1, 2, 0), (0, 1, 2, 3), True)
            op("dve", lambda e: e.memset(S[:], 0.0), writes=["S"])
            op("dve", lambda e: e.memset(H[:], 0.0), writes=["H"])
            op("pool", lambda e: e.memset(ubuf[:, :, 0:2], 0.0), writes=["ubuf0", "ubuf1"])
            op("pool", lambda e: e.memset(pbuf[:, :, 0:16], 0.0), writes=["pbuf0", "pbuf1"])
            op("pool", lambda e: e.memset(xbuf[:, :, 0:3], 0.0), writes=["xbuf%d" % c for c in range(6)])

            normw = bp[:, 0:1024]
            gateb = bp[:, 1024:1152]
            dtb = bp[:, 1152:1156]

            def load_x(n_):
                s_ = n_ % 3
                rd = ["dst%d_%d" % (L - 1, n_)] if L > 0 else []
                dma(xt[:, s_], src[n_ * T:(n_ + 1) * T, :].rearrange("(j p) d -> p j d", p=128), rd, ["x%d" % s_], "ldx%d" % s_)

            bank = {"i": 0}
            hpar = {"p": 0}

            NB = int(os.environ.get("K_NB", "3"))
            BANKS = [(PA, "PA"), (PB, "PB"), (PC, "PC")][:NB]

            def next_bank():
                b = BANKS[bank["i"] % len(BANKS)]
                bank["i"] += 1
                return b

            def fm_tile(col0, ncols, evac):
                bt, bk = next_bank()
                hp = hpar["p"]
                for kc in range(8):
                    mm(bt[0:ncols, 0:T], wbf[:, kc, col0:col0 + ncols], hT[:, hp, kc, :], kc == 0, kc == 7,
                       wkeys(col0, ncols) + ["hT%d" % hp], [bk])
                evac(bt[0:ncols, 0:T], bk)

            def pre(n):
                s = n % 3
                pr = n % 2
                hpar["p"] = pr
                xk = "x%d" % s
                X = xt[:, s]
                for j in range(2):
                    act(hbf[:, j, :], X[:, j, :], AF.Square, [xk], ["st_ss%d" % j, "hbf%d" % j], accum_out=st[:, j:j + 1])
                ts("dve", st[:, 2:4], st[:, 0:2], 1.0 / D, EPS, ALU.mult, ALU.add, ["st_ss0", "st_ss1"], ["st_ms"])
                tt("pool", st[:, 6:8], st[:, 2:4], mhalf[:], ALU.pow, ["st_ms", "mhalf"], ["st_rstd"])
                yield
                for j in range(2):
                    stt(hbf[:, j, :], X[:, j, :], st[:, 6 + j:7 + j], normw, ALU.mult, ALU.mult,
                        [xk, "st_rstd", "bp"], ["hbf%d" % j])
                    yield
                for g in range(4):
                    for kk in range(2):
                        for j in range(2):
                            kc = 2 * g + kk
                            tr(TRb[:, kk, j, :], hbf[:, j, kc * 128:(kc + 1) * 128], identb[:],
                               ["hbf%d" % j, "identb"], ["TRb"], signal=(kk == 1 and j == 1))
                    cp_any(evac_eng(), hT[:, pr, 2 * g:2 * g + 2, :].rearrange("p a b -> p (a b)"),
                           TRb[:].rearrange("p a j b -> p (a j b)"), ["TRb"], ["hT%d" % pr] + PSR("TRb"))
                    yield
                for c in range(6):
                    def ev_x(p, bk, c=c):
                        xk_ = "xbuf%d" % c
                        act(xbuf[:, c, 3:T + 3], p, AF.Copy, [bk], [xk_] + PSR(bk))
                        if c < 2:
                            o_, ok_ = xs[:, pr, c, :], "xs%dp%d" % (c, pr)
                        elif c < 4:
                            o_, ok_ = bfm[:, pr, c - 2, :], "bfm%dp%d" % (c - 2, pr)
                        else:
                            o_, ok_ = cvc[:, c - 4, :], "cvc%d" % (c - 4)
                        ts("dve", o_, xbuf[:, c, 0:T], cp[:, 6 + c * 4:7 + c * 4], None, ALU.mult, None, [xk_, "cp"], [ok_])
                        for k_ in range(1, 4):
                            stt(o_, xbuf[:, c, k_:T + k_], cp[:, 6 + c * 4 + k_:7 + c * 4 + k_], o_, ALU.mult, ALU.add,
                                [xk_, "cp", ok_], [ok_])
                        cp_any("pool", xbuf[:, c, 0:3], xbuf[:, c, T:T + 3], [xk_], [xk_])
                    fm_tile(C_XBC + c * 128, 128, ev_x)
                    yield
                for (col, buf, nm) in ((C_GZ, szg, "szg"), (C_SZ, szs, "szs"), (C_AZ, sza, "sza"), (C_PZ, szp, "szp")):
                    for i in range(2):
                        fm_tile(col + i * 128, 128,
                                lambda p, bk, buf=buf, nm=nm, i=i: cp_any(evac_eng(), buf[:, pr, i, :], p, [bk],
                                                                          ["%s%dp%d" % (nm, i, pr)] + PSR(bk)))
                        yield
                for j in range(2):
                    for kc in range(8):
                        mm(KV[:, 0:404], hT[:, pr, kc, j * 128:(j + 1) * 128], wbf[:, kc, C_K:C_K + 404], kc == 0, kc == 7,
                           ["hT%d" % pr] + wkeys(C_K, 404), ["KV"])
                    act(ksb[:, pr, j, :], KV[:, 0:128], AF.Copy, ["KV"], ["ksb%dp%d" % (j, pr)] + PSR("KV"))
                    act(vbf[:, pr, j, :], KV[:, 128:384], AF.Copy, ["KV"], ["vbf%dp%d" % (j, pr)] + PSR("KV"))
                    tt("dve", dtx[:, pr, j, :], KV[:, 384:388], dtb, ALU.add, ["KV", "bp"], ["dtx%dp%d" % (j, pr)] + PSR("KV"))
                    act(glt[:], KV[:, 388:404], AF.Copy, ["KV"], ["glt"] + PSR("KV"))
                    tr(KV[0:16, 0:128], glt[:], identf, ["glt", "kp"], ["KV"], True)
                    act(glr[0:16, pr, j * 128:(j + 1) * 128], KV[0:16, 0:128], AF.Copy, ["KV"], ["glrp%d" % pr] + PSR("KV"))
                    yield
                fm_tile(C_Q, 128, lambda p, bk: act(qbf[:, pr, :], p, AF.Copy, [bk], ["qbfp%d" % pr] + PSR(bk), scale=32.0 ** -0.5))
                yield

            def silu_batch(n):
                pr = n % 2
                for c in range(2):
                    k_ = "xs%dp%d" % (c, pr)
                    act(xs[:, pr, c, :], xs[:, pr, c, :], AF.Silu, [k_, "cp"], [k_], bias=cp[:, 30 + c:31 + c])
                for c in range(2):
                    k_ = "bfm%dp%d" % (c, pr)
                    act(bfm[:, pr, c, :], bfm[:, pr, c, :], AF.Silu, [k_, "cp"], [k_], bias=cp[:, 32 + c:33 + c])
                for c in range(2):
                    act(cbf[:, pr, c, :], cvc[:, c, :], AF.Silu, ["cvc%d" % c, "cp"], ["cbf%dp%d" % (c, pr)], bias=cp[:, 34 + c:35 + c])
                for (buf, nm) in ((szg, "szg"), (szs, "szs"), (sza, "sza"), (szp, "szp")):
                    for i in range(2):
                        k_ = "%s%dp%d" % (nm, i, pr)
                        act(buf[:, pr, i, :], buf[:, pr, i, :], AF.Silu, [k_], [k_])

            def big(n):
                pr = n % 2
                hpar["p"] = pr
                for i in range(2):
                    fm_tile(C_AH + i * 128, 128, lambda p, bk, i=i: act(ah[:, i, :], p, AF.Copy, [bk], ["ah%d" % i] + PSR(bk)))
                    yield
                for i in range(2):
                    def ev_ac(p, bk, i=i):
                        tt("dve", ubuf[:, i, 2:T + 2], p, ah[:, i, :], ALU.mult, [bk, "ah%d" % i], ["ubuf%d" % i] + PSR(bk))
                        uk = ["ubuf%d" % i, "cp"]
                        ts("dve", cva[:, i, :], ubuf[:, i, 0:T], cp[:, i * 3:i * 3 + 1], None, ALU.mult, None, uk, ["cva%d" % i])
                        stt(cva[:, i, :], ubuf[:, i, 1:T + 1], cp[:, i * 3 + 1:i * 3 + 2], cva[:, i, :], ALU.mult, ALU.add,
                            uk + ["cva%d" % i], ["cva%d" % i])
                        stt(cva[:, i, :], ubuf[:, i, 2:T + 2], cp[:, i * 3 + 2:i * 3 + 3], cva[:, i, :], ALU.mult, ALU.add,
                            uk + ["cva%d" % i], ["cva%d" % i])
                        cp_any("pool", ubuf[:, i, 0:2], ubuf[:, i, T:T + 2], ["ubuf%d" % i], ["ubuf%d" % i])
                    fm_tile(C_AC + i * 128, 128, ev_ac)
                    yield
                for i in range(2):
                    def ev_ab(p, bk, i=i):
                        tt("dve", yab[:, i, :], p, cva[:, i, :], ALU.mult, [bk, "cva%d" % i], ["yab%d" % i] + PSR(bk))
                        tt("pool", mixT[:, i, :], yab[:, i, :], sza[:, pr, i, :], ALU.mult,
                           ["yab%d" % i, "sza%dp%d" % (i, pr)], ["mix%d" % i])
                    fm_tile(C_AB + i * 128, 128, ev_ab)
                    yield
                for i in range(2):
                    fm_tile(C_PU + i * 128, 128, lambda p, bk, i=i: act(pbuf[:, i, 16:T + 16], p, AF.Copy, [bk],
                                                                       ["pbuf%d" % i] + PSR(bk)))
                    yield
                invc = kp[:, 642:1154] if n == 0 else kp[:, 1154:1666]
                tt("pool", bA[:, 0, 14:T + 16], pbuf[:, 0, 14:T + 16], pbuf[:, 0, 13:T + 15], ALU.add, ["pbuf0"], ["bA0"])
                tt("pool", bB[64:128, 0, 16:T + 16], bA[64:128, 0, 16:T + 16], bA[64:128, 0, 14:T + 14], ALU.add, ["bA0"], ["bB0"])
                tt("pool", pl[0:64, 0, :], bA[0:64, 0, 16:T + 16], invc[0:64, 0:T], ALU.mult, ["bA0", "kp"], ["pl0"])
                tt("pool", pl[64:128, 0, :], bB[64:128, 0, 16:T + 16], invc[64:128, 0:T], ALU.mult, ["bB0", "kp"], ["pl0"])
                tt("pool", pooled[:, 0, :], pl[:, 0, :], pbuf[:, 0, 16:T + 16], ALU.subtract, ["pl0", "pbuf0"], ["pooled0"])
                cp_any("pool", pbuf[:, 0, 0:16], pbuf[:, 0, T:T + 16], ["pbuf0"], ["pbuf0"])
                yield
                tt("pool", bA[:, 1, 2:T + 16], pbuf[:, 1, 2:T + 16], pbuf[:, 1, 1:T + 15], ALU.add, ["pbuf1"], ["bA1"])
                tt("pool", bB[:, 1, 4:T + 16], bA[:, 1, 4:T + 16], bA[:, 1, 2:T + 14], ALU.add, ["bA1"], ["bB1"])
                tt("pool", bA[:, 1, 8:T + 16], bB[:, 1, 8:T + 16], bB[:, 1, 4:T + 12], ALU.add, ["bB1", "bA1"], ["bA1"])
                tt("pool", bB[64:128, 1, 16:T + 16], bA[64:128, 1, 16:T + 16], bA[64:128, 1, 8:T + 8], ALU.add, ["bA1", "bB1"], ["bB1"])
                yield
                tt("pool", pl[0:64, 1, :], bA[0:64, 1, 16:T + 16], invc[0:64, T:2 * T], ALU.mult, ["bA1", "kp"], ["pl1"])
                tt("pool", pl[64:128, 1, :], bB[64:128, 1, 16:T + 16], invc[64:128, T:2 * T], ALU.mult, ["bB1", "kp"], ["pl1"])
                tt("pool", pooled[:, 1, :], pl[:, 1, :], pbuf[:, 1, 16:T + 16], ALU.subtract, ["pl1", "pbuf1"], ["pooled1"])
                cp_any("pool", pbuf[:, 1, 0:16], pbuf[:, 1, T:T + 16], ["pbuf1"], ["pbuf1"])
                yield
                for i in range(2):
                    bt, bk = next_bank()
                    mm(bt[:, 0:T], pwb[:, i, :], pooled[:, i, :], True, True, ["pwb", "pooled%d" % i], [bk])
                    stt(mixT[:, 4 + i, :], bt[:, 0:T], cp[:, 37 + i:38 + i], szp[:, pr, i, :], ALU.mult, ALU.mult,
                        [bk, "cp", "szp%dp%d" % (i, pr)], ["mix%d" % (4 + i)] + PSR(bk))
                    yield

            def pro(n):
                pr = n % 2
                for j in range(2):
                    D_ = dts[:, j]
                    dk_ = "dts%d" % j
                    act(D_[:, 4:8], dtx[:, pr, j, :], AF.Exp, ["dtx%dp%d" % (j, pr)], [dk_])
                    act(D_[:, 8:12], D_[:, 4:8], AF.Ln, [dk_], [dk_], bias=1.0)
                    tt("dve", D_[:, 12:16], D_[:, 8:12], abc[:], ALU.mult, [dk_, "abc"], [dk_])
                    ts("dve", D_[:, 24:28], D_[:, 12:16], ind[:, 0:1], None, ALU.mult, None, [dk_, "kp"], [dk_])
                    ts("dve", D_[:, 28:32], D_[:, 12:16], ind[:, 1:2], None, ALU.mult, None, [dk_, "kp"], [dk_])
                    mm(SM[:, 0:128], glr[0:16, pr, j * 128:(j + 1) * 128], gwb[0:16, :], True, True, ["glrp%d" % pr, "gwb"], ["SM"])
                    tt("dve", gx[:], SM[:, 0:128], gateb, ALU.add, ["SM", "bp"], ["gx"] + PSR("SM"))
                    yield
                    act(gx[:], gx[:], AF.Exp, ["gx"], ["gx"], scale=-1.0)
                    act(spl[:, j, :], gx[:], AF.Ln, ["gx"], ["spl%d" % j], bias=1.0)
                    mm(SM[:, 128:256], mstrict, spl[:, j, :], True, True, ["kp", "spl%d" % j], ["SM"])
                    mm(SM[:, 256:258], spl[:, j, :], ind, True, True, ["kp", "spl%d" % j], ["SM"])
                    mm(SM[:, 300:304], mstrict, D_[:, 12:16], True, True, ["kp", dk_], ["SM"])
                    mm(SM[:, 304:312], ones[:], D_[:, 24:32], True, True, ["ones", dk_], ["SM"])
                    act(Eg[:], SM[:, 128:256], AF.Exp, ["SM"], ["Eg"] + PSR("SM"), scale=-1.0 / 16)
                    act(gdec[:, pr, 2 * j:2 * j + 2], SM[:, 256:258], AF.Exp, ["SM"], ["gdec%dp%d" % (j, pr)] + PSR("SM"), scale=-1.0 / 16)
                    act(D_[:, 16:20], SM[:, 300:304], AF.Exp, ["SM"], [dk_] + PSR("SM"))
                    act(sdec[:, pr, j, :], SM[:, 304:312], AF.Exp, ["SM"], ["sdec%dp%d" % (j, pr)] + PSR("SM"))
                    tt("dve", kdec[:, pr, j, :], ksb[:, pr, j, :], Eg[:], ALU.mult, ["ksb%dp%d" % (j, pr), "Eg"], ["kdec%dp%d" % (j, pr)])
                    tt("dve", D_[:, 20:24], D_[:, 16:20], D_[:, 8:12], ALU.mult, [dk_], [dk_])
                    yield
                    for i in range(2):
                        tr(TRf[:, i * 128:(i + 1) * 128], xs[:, pr, i, j * 128:(j + 1) * 128], identf,
                           ["xs%dp%d" % (i, pr), "kp"], ["SM"], False)
                    for i in range(2):
                        tr(TRf[:, 256 + i * 128:256 + (i + 1) * 128], bfm[:, pr, i, j * 128:(j + 1) * 128], identf,
                           ["bfm%dp%d" % (i, pr), "kp"], ["SM"], i == 1)
                    tt("dve", wx[:, pr, j, :].rearrange("p (a b) -> p a b", a=4), TRf[:, 0:256].rearrange("p (a b) -> p a b", a=4),
                       D_[:, 20:24].unsqueeze(2).to_broadcast([128, 4, 64]), ALU.mult, ["SM", dk_], ["wx%dp%d" % (j, pr)] + PSR("SM"))
                    cp_any("dve", btok[:, pr, j, :], TRf[:, 256:512], ["SM"], ["btok%dp%d" % (j, pr)] + PSR("SM"))
                    yield

            def rec(n):
                pr = n % 2
                for j in range(2):
                    for cc in range(2):
                        c = 2 * j + cc
                        r0 = cc * 64
                        mm(UB[:, 0:256], kdec[r0:r0 + 64, pr, j, :], vbf[r0:r0 + 64, pr, j, :], True, True,
                           ["kdec%dp%d" % (j, pr), "vbf%dp%d" % (j, pr)], ["UB"])
                        for g in range(2):
                            mm(UB[:, 256 + g * 128:256 + (g + 1) * 128], btok[r0:r0 + 64, pr, j, g * 128:(g + 1) * 128],
                               wx[r0:r0 + 64, pr, j, g * 128:(g + 1) * 128], True, True,
                               ["btok%dp%d" % (j, pr), "wx%dp%d" % (j, pr)], ["UB"])
                        tt("dve", tmpS[:], UB[:, 0:256], bdmask, ALU.mult, ["UB", "kp"], ["tmpS"] + PSR("UB"))
                        stt(S[:], S[:], gdec[:, pr, c:c + 1], tmpS[:], ALU.mult, ALU.add, ["S", "gdec%dp%d" % (j, pr), "tmpS"], ["S"])
                        cp_any("act", Sb[:], S[:], ["S"], ["Sb"])
                        tt("dve", tmpH[:].rearrange("p (a b) -> p a b", a=4), H[:].rearrange("p (a b) -> p a b", a=4),
                           sdec[:, pr, j, cc * 4:(cc + 1) * 4].unsqueeze(2).to_broadcast([128, 4, 64]), ALU.mult,
                           ["H", "sdec%dp%d" % (j, pr)], ["tmpH"])
                        tt("dve", H[:], tmpH[:], UB[:, 256:512], ALU.add, ["tmpH", "UB"], ["H"] + PSR("UB"))
                        cp_any("act", Hb[:], H[:], ["H"], ["Hb"])
                        yield
                        for i in range(2):
                            mm(OT[:, i, r0:r0 + 64], Sb[:, i * 128:(i + 1) * 128], qbf[:, pr, c * 64:(c + 1) * 64], True, True,
                               ["Sb", "qbfp%d" % pr], ["B6"])
                        for g in range(2):
                            mm(YT[:, g, r0:r0 + 64], Hb[:, g * 128:(g + 1) * 128], cbf[:, pr, g, c * 64:(c + 1) * 64], True, True,
                               ["Hb", "cbf%dp%d" % (g, pr)], ["B6"])
                        yield
                    js = slice(j * 128, (j + 1) * 128)
                    for i in range(2):
                        act(sq[:, i, js], OT[:, i, :], AF.Square, ["B6"], ["sq"] + PSR("B6"))
                        act(osb[:, i, js], OT[:, i, :], AF.Copy, ["B6"], ["osb"] + PSR("B6"))
                    for g in range(2):
                        stt(ysb[:, g, 0:128], xs[:, pr, g, js], cp[:, 39 + g:40 + g], YT[:, g, :], ALU.mult, ALU.add,
                            ["xs%dp%d" % (g, pr), "cp", "B6"], ["ysb%d" % g] + PSR("B6"))
                    yield
                    for i in range(2):
                        mm(OT[:, i, :], blk64, sq[:, i, js], True, True, ["kp", "sq"], ["B6"])
                    act(rs[:, 0:256], OT.rearrange("p a b -> p (a b)"), AF.Ln, ["B6"], ["rs"] + PSR("B6"), scale=1.0 / 64, bias=EPS)
                    act(rs[:, 0:256], rs[:, 0:256], AF.Exp, ["rs"], ["rs"], scale=-0.5)
                    for g in range(2):
                        tt("pool", yz[:, g, js], ysb[:, g, 0:128], szs[:, pr, g, js], ALU.mult,
                           ["ysb%d" % g, "szs%dp%d" % (g, pr)], ["yz"])
                        act(sq2[:, g, :], yz[:, g, js], AF.Square, ["yz"], ["sq2"])
                    yield
                    for i in range(2):
                        stt(t1[:, i, :], osb[:, i, js], cp[:, 36:37], rs[:, i * 128:(i + 1) * 128], ALU.mult, ALU.mult,
                            ["osb", "cp", "rs"], ["t1"])
                        tt("pool", mixT[:, 2 + i, js], t1[:, i, :], szg[:, pr, i, js], ALU.mult,
                           ["t1", "szg%dp%d" % (i, pr)], ["mix%d" % (2 + i)])
                    for g in range(2):
                        mm(YT[:, 0, :], ones[:], sq2[:, g, :], g == 0, g == 1, ["ones", "sq2"], ["B6"])
                    act(rs2[:], YT[:, 0, :], AF.Ln, ["B6"], ["rs2"] + PSR("B6"), scale=1.0 / 256, bias=EPS)
                    act(rs2[:], rs2[:], AF.Exp, ["rs2"], ["rs2"], scale=-0.5)
                    for g in range(2):
                        stt(mixT[:, 6 + g, js], yz[:, g, js], cp[:, 41 + g:42 + g], rs2[:], ALU.mult, ALU.mult,
                            ["yz", "cp", "rs2"], ["mix%d" % (6 + g)])
                    yield

            def post(n):
                s = n % 3
                xk = "x%d" % s
                X = xt[:, s]
                mixk = ["mix%d" % i for i in range(8)]
                for j in range(2):
                    for hf in range(2):
                        bt, bk = next_bank()
                        for kc in range(8):
                            mm(bt[:, :], mixT[:, kc, j * 128:(j + 1) * 128], woutbf[:, kc, hf * 512:(hf + 1) * 512],
                               kc == 0, kc == 7, mixk + ["woutbf"], [bk])
                        tt("dve", X[:, j, hf * 512:(hf + 1) * 512], bt[:, :], X[:, j, hf * 512:(hf + 1) * 512], ALU.add,
                           [bk, xk], [xk] + PSR(bk))
                if last:
                    for j in range(2):
                        act(mixT[:, 4 * j:4 * j + 4, :].rearrange("p a b -> p (a b)"), X[:, j, :], AF.Square, [xk],
                            ["st_fs%d" % j] + ["mix%d" % (4 * j + q) for q in range(4)], accum_out=st[:, 8 + j:9 + j])
                    ts("dve", st[:, 10:12], st[:, 8:10], 1.0 / D, EPS, ALU.mult, ALU.add, ["st_fs0", "st_fs1"], ["st_fm"])
                    tt("pool", st[:, 14:16], st[:, 10:12], mhalf[:], ALU.pow, ["st_fm", "mhalf"], ["st_fr"])
                    for j in range(2):
                        stt(X[:, j, :], X[:, j, :], st[:, 14 + j:15 + j], fnw[:], ALU.mult, ALU.mult,
                            [xk, "st_fr", "fnw"], [xk])
                dma(dst[n * T:(n + 1) * T, :].rearrange("(j p) d -> p j d", p=128), X, [xk], ["dst%d_%d" % (L, n)], "st%d" % s)
                if last:
                    P.final_wait("sp", ["dst%d_%d" % (L, n)])
                if n + 3 < NT:
                    load_x(n + 3)

            def drain(g):
                P.tag = g.__name__ if hasattr(g, "__name__") else ""
                for _ in g:
                    pass
                P.tag = ""


            def interleave(ga, na, gb, nb):
                mode = os.environ.get("ILV", "")
                if mode == "ab":
                    drain(ga); drain(gb); return
                if mode == "ba":
                    drain(gb); drain(ga); return
                a_done = b_done = 0
                a_live = b_live = True
                while a_live or b_live:
                    if a_live:
                        try:
                            next(ga)
                            a_done += 1
                        except StopIteration:
                            a_live = False
                    while b_live and (not a_live or b_done * na < a_done * nb):
                        try:
                            next(gb)
                            b_done += 1
                        except StopIteration:
                            b_live = False

            def chain2(*gens):
                for g in gens:
                    yield from g

            for n_ in range(1, min(3, NT)):
                load_x(n_)
            def silu_gen(n):
                silu_batch(n)
                yield

            drain(pre(0))
            silu_batch(0)
            drain(pro(0))
            for n in range(NT):
                drain(rec(n))
                drain(big(n))
                if n + 1 < NT:
                    drain(pre(n + 1))
                    P.tag = 'silu'
                    silu_batch(n + 1)
                    drain(pro(n + 1))
                P.tag = 'post'
                post(n)
                if EARLY and n == NT - 2 and L + 1 < NL:
                    P.tag = 'wearly'
                    load_w(L + 1, (3, 4), (2, 3), False)
        ms = P.schedule()
        print("[sched] ops=%d est_makespan_us=%.1f hist=%s" % (len(P.ops), ms, P.hist))
        P.emit(sems, block)
    return nc


def _kpack():
    kp = np.zeros((128, NK), np.float32)
    p = np.arange(128)
    kp[:, 0:128] = np.eye(128, dtype=np.float32)
    s = p[:, None]
    t = p[None, :]
    kp[:, 128:256] = ((s // 64 == t // 64) & (s > t)).astype(np.float32)
    kp[:, 256:258] = (s // 64 == np.arange(2)[None, :]).astype(np.float32)
    kp[:, 258:386] = (s // 64 == t // 64).astype(np.float32)
    col = np.arange(256)[None, :]
    kp[:, 386:642] = (s // 32 == col // 64).astype(np.float32)
    tt_ = np.arange(T)
    for i in range(2):
        win = np.array([POOL_WINDOWS[(i * 128 + pp) // 64] for pp in p], np.float32)
        kp[:, 642 + i * T:642 + (i + 1) * T] = 1.0 / np.minimum(tt_[None, :] + 1.0, win[:, None])
        kp[:, 1154 + i * T:1154 + (i + 1) * T] = (1.0 / win)[:, None]
    return kp


def _pack_layers(inp, NL):
    p = np.arange(128)
    cpk = np.zeros((NL, 128, 43), np.float32)
    bpk = np.zeros((NL, 128, 1160), np.float32)
    pwk = np.zeros((NL, 128, 256), np.float32)
    for L in range(NL):
        for i in range(2):
            for k in range(3):
                cpk[L, :, i * 3 + k] = inp["conv_a_w"][L, k, i * 128:(i + 1) * 128]
        for c in range(6):
            for k in range(4):
                cpk[L, :, 6 + c * 4 + k] = inp["ssd_conv_w"][L, k, c * 128:(c + 1) * 128]
            cpk[L, :, 30 + c] = inp["ssd_conv_b"][L, c * 128:(c + 1) * 128]
        cpk[L, :, 36] = inp["gla_norm_w"][L][p % 64]
        for i in range(2):
            cpk[L, :, 37 + i] = inp["pool_scale"][L, i * 128:(i + 1) * 128]
            cpk[L, :, 39 + i] = inp["ssd_d"][L][i * 2 + p // 64]
            cpk[L, :, 41 + i] = inp["ssd_norm_w"][L, i * 128:(i + 1) * 128]
        bpk[L, :, 0:1024] = inp["norm_w"][L][None, :]
        bpk[L, :, 1024:1152] = inp["gla_gate_b"][L][None, :]
        bpk[L, :, 1152:1156] = inp["ssd_dt_bias"][L][None, :]
        bpk[L, :, 1156:1160] = inp["ssd_a_log"][L][None, :]
        for i in range(2):
            for gl in range(2):
                pwk[L, gl * 64:(gl + 1) * 64, i * 128 + gl * 64:i * 128 + (gl + 1) * 64] = inp["pool_w"][L, 2 * i + gl]
    return cpk, bpk, pwk


def make_in_maps(inp, NL, ncores=8):
    inp = {k: np.asarray(v, np.float32) for k, v in inp.items()}
    B = inp["x"].shape[0]
    cpk, bpk, pwk = _pack_layers(inp, NL)
    kp = _kpack()
    fnw = np.ascontiguousarray(np.broadcast_to(inp["final_norm_w"][None, :], (128, D))).astype(np.float32)
    w_in = inp["w_in"][:NL]
    w_in = np.ascontiguousarray(np.concatenate([w_in[:, :, :1536], w_in[:, :, 3344:3348], w_in[:, :, 1536:3344]], axis=2))
    common = {"w_in": w_in, "w_out": np.ascontiguousarray(inp["w_out"][:NL]),
              "cpack": cpk, "bpack": bpk, "gatew": np.ascontiguousarray(inp["gla_gate_w"][:NL]),
              "poolw": pwk, "fnw": fnw, "kpack": kp}
    maps = []
    for c in range(ncores):
        m = dict(common)
        m["x"] = np.ascontiguousarray(inp["x"][c % B])
        maps.append(m)
    return maps


_NC_CACHE = {}


def kernel(**inputs):
    x = np.asarray(inputs["x"])
    B, SEQ, _ = x.shape
    NL = int(np.asarray(inputs["w_in"]).shape[0])
    key = (SEQ, NL)
    if key not in _NC_CACHE:
        _NC_CACHE[key] = build_nc(SEQ, NL)
    nc = _NC_CACHE[key]
    maps = make_in_maps(inputs, NL, 8)
    res = run_bass_kernel_spmd(nc, maps, core_ids=list(range(8)))
    out = np.stack([np.asarray(res.results[b]["y"], dtype=np.float32) for b in range(B)], axis=0)
    return out
```
